# Optimizing a Trainium2 kernel written in Bass

```python
import jax, jax.numpy as jnp
from jax import lax
import numpy as np

D_MODEL = 1024
BATCH = 8
SEQ = 4096
DEPTH = 1
DEC_BATCH = 128
DEC_SEQ = 8
PAST_LEN = 16384
PAGE_SIZE = 128

MIX_WIDTH = D_MODEL
HG_WIDTH = MIX_WIDTH // 2
HG_EXPAND = 128
HG_HEADS = HG_WIDTH // HG_EXPAND
HG_DK = HG_EXPAND
HG_DV = HG_WIDTH // HG_HEADS
HG_CHUNK = 32
SWA_WIDTH = MIX_WIDTH - HG_WIDTH
SWA_HEAD_DIM = 64
SWA_Q_HEADS = SWA_WIDTH // SWA_HEAD_DIM
SWA_KV_HEADS = 2
SWA_GROUP = SWA_Q_HEADS // SWA_KV_HEADS
WINDOW = 128
SWA_SCALE = SWA_HEAD_DIM ** -0.5
D_FF = 4 * D_MODEL
EPS = 1e-6
IN_SPLITS = (HG_WIDTH, HG_WIDTH, HG_WIDTH, HG_WIDTH,
             SWA_Q_HEADS * SWA_HEAD_DIM, SWA_KV_HEADS * SWA_HEAD_DIM, SWA_KV_HEADS * SWA_HEAD_DIM)
N_IN = sum(IN_SPLITS)

kernel_name = 'hymba_hgrn2_swa_sink_decoder_step'


def _rmsnorm(x, g):
    xf = x.astype(jnp.float32)
    y = xf * lax.rsqrt(jnp.mean(xf * xf, axis=-1, keepdims=True) + EPS)
    return (y * g.astype(jnp.float32)).astype(x.dtype)


def _split(z):
    outs, off = [], 0
    for w in IN_SPLITS:
        outs.append(z[..., off:off + w])
        off += w
    return outs


def _gla_chunked(q, k, v, log_f, S0):
    B, T, H, DK = q.shape
    DV = v.shape[-1]
    L = min(HG_CHUNK, T)
    pad = (-T) % L
    if pad:
        pw = ((0, 0), (0, pad), (0, 0), (0, 0))
        q, k, v, log_f = [jnp.pad(a, pw) for a in (q, k, v, log_f)]
    NC = (T + pad) // L

    def blocks(a):
        return a.reshape(B, NC, L, H, a.shape[-1]).transpose(1, 0, 3, 2, 4)

    q, k, v, g = blocks(q), blocks(k), blocks(v), blocks(log_f)
    b = jnp.cumsum(g, axis=3)
    b_last = b[:, :, :, -1:, :]
    qt = q * jnp.exp(b)
    kt = k * jnp.exp(-b)
    ke = k * jnp.exp(b_last - b)
    decay = jnp.exp(b_last[:, :, :, 0, :])
    causal = jnp.tril(jnp.ones((L, L), dtype=bool))
    A = jnp.where(causal, jnp.einsum('nbhtd,nbhsd->nbhts', qt, kt), 0.0)
    o_intra = jnp.einsum('nbhts,nbhsv->nbhtv', A, v)

    def step(S, xs):
        qt_n, ke_n, v_n, dec_n = xs
        o_n = jnp.einsum('bhtd,bhdv->bhtv', qt_n, S)
        S = dec_n[..., None] * S + jnp.einsum('bhsd,bhsv->bhdv', ke_n, v_n)
        return S, o_n

    S_fin, o_inter = lax.scan(step, S0, (qt, ke, v, decay))
    o = (o_intra + o_inter).transpose(1, 0, 3, 2, 4).reshape(B, NC * L, H, DV)[:, :T]
    return o, S_fin


def _hgrn2(zq, zf, zi, zg, lb, g_norm, S0):
    B, T, _ = zq.shape
    f32 = jnp.float32
    q = jax.nn.silu(zq.astype(f32)).reshape(B, T, HG_HEADS, HG_DK)
    fl = zf.astype(f32).reshape(B, T, HG_HEADS, HG_DK)
    lb = lb.astype(f32)
    f = lb + (1.0 - lb) * jax.nn.sigmoid(fl)
    k = (1.0 - lb) * jax.nn.sigmoid(-fl)
    v = zi.astype(f32).reshape(B, T, HG_HEADS, HG_DV)
    o, S = _gla_chunked(q, k, v, jnp.log(f), S0.astype(f32))
    o = _rmsnorm(o, g_norm)
    o = o.reshape(B, T, HG_WIDTH) * jax.nn.silu(zg.astype(f32))
    return o.astype(zq.dtype), S.astype(S0.dtype)


def _sink_softmax(s, mask, sink):
    s = jnp.where(mask, s, -jnp.inf)
    m = jnp.maximum(jnp.max(s, axis=-1, keepdims=True), sink)
    p = jnp.exp(s - m)
    return p / (jnp.sum(p, axis=-1, keepdims=True) + jnp.exp(sink - m))


def _swa_prompt(q, k, v, sinks):
    B, T = q.shape[:2]
    W = WINDOW
    NB = T // W
    f32 = jnp.float32
    qb = q.astype(f32).reshape(B, NB, W, SWA_KV_HEADS, SWA_GROUP, SWA_HEAD_DIM)
    kb = k.astype(f32).reshape(B, NB, W, SWA_KV_HEADS, SWA_HEAD_DIM)
    vb = v.astype(f32).reshape(B, NB, W, SWA_KV_HEADS, SWA_HEAD_DIM)
    prev = lambda a: jnp.pad(a[:, :-1], ((0, 0), (1, 0), (0, 0), (0, 0), (0, 0)))
    kk = jnp.concatenate([prev(kb), kb], axis=2)
    vv = jnp.concatenate([prev(vb), vb], axis=2)
    s = jnp.einsum('bnqkgd,bnskd->bnkgqs', qb, kk) * SWA_SCALE
    n = jnp.arange(NB)[:, None, None]
    i = jnp.arange(W)[None, :, None]
    j = jnp.arange(2 * W)[None, None, :]
    kpos = (n - 1) * W + j
    d = n * W + i - kpos
    mask = ((d >= 0) & (d < WINDOW) & (kpos >= 0))[None, :, None, None]
    sink = sinks.astype(f32).reshape(SWA_KV_HEADS, SWA_GROUP)[None, None, :, :, None, None]
    p = _sink_softmax(s, mask, sink)
    o = jnp.einsum('bnkgqs,bnskd->bnqkgd', p, vv)
    return o.reshape(B, T, SWA_WIDTH).astype(q.dtype)


def _swa_sample(q, k_new, v_new, ck, cv, sinks):
    DB, S = q.shape[:2]
    WB = ck.shape[1]
    f32 = jnp.float32
    kk = jnp.concatenate([ck, k_new.astype(ck.dtype)], axis=1)
    vv = jnp.concatenate([cv, v_new.astype(cv.dtype)], axis=1)
    qpos = PAST_LEN + jnp.arange(S)
    kpos = jnp.concatenate([PAST_LEN - WB + jnp.arange(WB), PAST_LEN + jnp.arange(S)])
    d = qpos[:, None] - kpos[None, :]
    mask = ((d >= 0) & (d < WINDOW))[None, None, None]
    qg = q.astype(f32).reshape(DB, S, SWA_KV_HEADS, SWA_GROUP, SWA_HEAD_DIM)
    s = jnp.einsum('bqkgd,bskd->bkgqs', qg, kk.astype(f32)) * SWA_SCALE
    sink = sinks.astype(f32).reshape(SWA_KV_HEADS, SWA_GROUP)[None, :, :, None, None]
    p = _sink_softmax(s, mask, sink)
    o = jnp.einsum('bkgqs,bskd->bqkgd', p, vv.astype(f32))
    return o.reshape(DB, S, SWA_WIDTH).astype(q.dtype), kk[:, -WB:], vv[:, -WB:]


def _layer(h, ln_mix, w_in, lb, hg_norm, sinks, w_out, ln_mlp, w_up, w_down, S0, ck=None, cv=None):
    B, T, _ = h.shape
    xn = _rmsnorm(h, ln_mix)
    zq, zf, zi, zg, sq, sk, sv = _split(xn @ w_in)
    o_hg, S_new = _hgrn2(zq, zf, zi, zg, lb, hg_norm, S0)
    q = sq.reshape(B, T, SWA_Q_HEADS, SWA_HEAD_DIM)
    k = sk.reshape(B, T, SWA_KV_HEADS, SWA_HEAD_DIM)
    v = sv.reshape(B, T, SWA_KV_HEADS, SWA_HEAD_DIM)
    if ck is None:
        o_swa = _swa_prompt(q, k, v, sinks)
        wb = min(WINDOW, T)
        k_buf, v_buf = k[:, -wb:], v[:, -wb:]
    else:
        o_swa, k_buf, v_buf = _swa_sample(q, k, v, ck, cv, sinks)
    h = h + jnp.concatenate([o_hg, o_swa], axis=-1) @ w_out
    u = _rmsnorm(h, ln_mlp) @ w_up
    h = h + jnp.square(jax.nn.relu(u)) @ w_down
    return h, S_new, k_buf, v_buf


def setup_inputs(seed: int = 0) -> dict:
    key = jax.random.key(seed)
    ks = jax.random.split(key, 16)
    WB = min(WINDOW, PAST_LEN)
    nrm = jax.random.normal
    f32 = jnp.float32
    return {
        'x_prompt': nrm(ks[0], (BATCH, SEQ, D_MODEL), f32),
        'x_sample': nrm(ks[1], (DEC_BATCH, DEC_SEQ, D_MODEL), f32),
        'state_hgrn': 0.5 * nrm(ks[2], (DEPTH, DEC_BATCH, HG_HEADS, HG_DK, HG_DV), f32),
        'cache_swa_k': nrm(ks[3], (DEPTH, DEC_BATCH, WB, SWA_KV_HEADS, SWA_HEAD_DIM), f32),
        'cache_swa_v': nrm(ks[4], (DEPTH, DEC_BATCH, WB, SWA_KV_HEADS, SWA_HEAD_DIM), f32),
        'ln_mix': 1.0 + 0.02 * nrm(ks[5], (DEPTH, D_MODEL), f32),
        'w_in': nrm(ks[6], (DEPTH, D_MODEL, N_IN), f32) * D_MODEL ** -0.5,
        'lb_logits': 0.1 * nrm(ks[7], (DEPTH + 1, HG_HEADS * HG_DK), f32),
        'hg_norm': 1.0 + 0.02 * nrm(ks[8], (DEPTH, HG_DV), f32),
        'sinks': nrm(ks[9], (DEPTH, SWA_Q_HEADS), f32),
        'w_out': nrm(ks[10], (DEPTH, MIX_WIDTH, D_MODEL), f32) * MIX_WIDTH ** -0.5,
        'ln_mlp': 1.0 + 0.02 * nrm(ks[11], (DEPTH, D_MODEL), f32),
        'w_up': nrm(ks[12], (DEPTH, D_MODEL, D_FF), f32) * D_MODEL ** -0.5,
        'w_down': nrm(ks[13], (DEPTH, D_FF, D_MODEL), f32) * D_FF ** -0.5,
        'ln_final': 1.0 + 0.02 * nrm(ks[14], (D_MODEL,), f32),
    }


def reference(x_prompt, x_sample, state_hgrn, cache_swa_k, cache_swa_v, ln_mix, w_in, lb_logits,
              hg_norm, sinks, w_out, ln_mlp, w_up, w_down, ln_final):
    lbs = jnp.cumsum(jax.nn.softmax(lb_logits.astype(jnp.float32), axis=0), axis=0)
    hp, hs = x_prompt, x_sample
    sp_l, kp_l, vp_l, ss_l, ks_l, vs_l = [], [], [], [], [], []
    for l in range(DEPTH):
        lb = lbs[l].reshape(HG_HEADS, HG_DK)
        w = (ln_mix[l], w_in[l], lb, hg_norm[l], sinks[l], w_out[l], ln_mlp[l], w_up[l], w_down[l])
        S0p = jnp.zeros((hp.shape[0], HG_HEADS, HG_DK, HG_DV), state_hgrn.dtype)
        hp, sp, kp, vp = _layer(hp, *w, S0p)
        hs, ss, kss, vss = _layer(hs, *w, state_hgrn[l], cache_swa_k[l], cache_swa_v[l])
        sp_l.append(sp); kp_l.append(kp.astype(cache_swa_k.dtype)); vp_l.append(vp.astype(cache_swa_v.dtype))
        ss_l.append(ss); ks_l.append(kss); vs_l.append(vss)
    y_prompt = _rmsnorm(hp, ln_final)
    y_sample = _rmsnorm(hs, ln_final)
    return (y_prompt, y_sample, jnp.stack(sp_l), jnp.stack(kp_l), jnp.stack(vp_l),
            jnp.stack(ss_l), jnp.stack(ks_l), jnp.stack(vs_l))
```

```python
import os
import numpy as np
import concourse.bass as bass
import concourse.mybir as mybir
from concourse.bass_utils import run_bass_kernel_spmd

F32 = mybir.dt.float32
BF16 = mybir.dt.bfloat16
AF = mybir.ActivationFunctionType
ALU = mybir.AluOpType

NCORES = 8
D = 1024
NT = 32
G = 4
NG = NT // G
NSLOT = 5
NCH = 24
EPS = 1e-6


class Buf:
    __slots__ = ("ap", "w", "r")

    def __init__(self, ap):
        self.ap = ap
        self.w = None
        self.r = {}

    def __getitem__(self, k):
        return self.ap[k]


class Eng:
    def __init__(self, h, semname):
        self.h = h
        self.semname = semname
        self.n = 0
        self.waited = {}


class Prog:
    def __init__(self, nc):
        self.nc = nc
        self.sems = {}
        self.dcnt = {}
        self.E = {}
        for name, h in (("pe", nc.tensor), ("act", nc.scalar), ("dve", nc.vector),
                        ("pool", nc.gpsimd), ("sp", nc.sync)):
            self.sems["e_" + name] = nc.alloc_semaphore("e_" + name)
            self.E[name] = Eng(h, "e_" + name)

    def dsem(self, name):
        if name not in self.sems:
            self.sems[name] = self.nc.alloc_semaphore(name)
            self.dcnt[name] = 0
        return name

    def _wait(self, e, tok):
        if tok is None:
            return
        sn, v = tok
        if e is self.E["pe"] and sn == "e_pe":
            return
        if e.waited.get(sn, 0) >= v:
            return
        e.h.wait_ge(self.sems[sn], v)
        e.waited[sn] = v

    def _deps(self, e, reads, writes):
        for b in reads:
            self._wait(e, b.w)
        for b in writes:
            self._wait(e, b.w)
            for sn, v in b.r.items():
                self._wait(e, (sn, v))

    @staticmethod
    def _mark(tok, reads, writes):
        sn, v = tok
        for b in reads:
            if b.r.get(sn, 0) < v:
                b.r[sn] = v
        for b in writes:
            b.w = tok
            b.r = {}

    def op(self, en, fns, reads=(), writes=()):
        e = self.E[en]
        if callable(fns):
            fns = [fns]
        self._deps(e, reads, writes)
        ins = None
        for f in fns:
            ins = f(e.h)
        ins.then_inc(self.sems[e.semname], 1)
        e.n += 1
        tok = (e.semname, e.n)
        self._mark(tok, reads, writes)
        return tok

    def dma(self, out, in_, reads=(), writes=(), sem=None):
        e = self.E["sp"]
        self.dsem(sem)
        self._deps(e, reads, writes)
        ins = e.h.dma_start(out=out, in_=in_)
        ins.then_inc(self.sems[sem], 16)
        self.dcnt[sem] += 16
        tok = (sem, self.dcnt[sem])
        self._mark(tok, reads, writes)
        return tok

    def wait_engines(self, waiters, targets):
        for wn in waiters:
            e = self.E[wn]
            for tn in targets:
                t = self.E[tn]
                if t.n > 0:
                    self._wait(e, (t.semname, t.n))


class Carver:
    def __init__(self, pool_ap, nbytes):
        self.pool = pool_ap
        self.nbytes = nbytes
        self.off = 0

    def take(self, free_shape, dt):
        n = int(np.prod(free_shape))
        nb = n * (4 if dt == F32 else 2)
        s = self.off // 2
        assert self.off + nb <= self.nbytes, ("carve overflow", self.off, nb, self.nbytes)
        ap = self.pool[:, s:s + nb // 2]
        if dt == F32:
            ap = ap.bitcast(F32)
        if len(free_shape) > 1:
            names = "abcd"[:len(free_shape)]
            ap = ap.rearrange("p (%s) -> p %s" % (" ".join(names), " ".join(names)),
                              **{k: int(v) for k, v in zip(names, free_shape)})
        self.off += (nb + 31) // 32 * 32
        return Buf(ap)


class NS:
    pass


class _Stop(Exception):
    pass


def build_program(worder=None, stop=None, dumps=()):
    record = worder is None
    WORDER = [] if record else list(worder)
    nc = bass.Bass("TRN2", target_bir_lowering=False)
    P = Prog(nc)
    REG = {}

    def chk(stage):
        if stop == stage:
            raise _Stop()

    def din(name, shape, dt=F32):
        return nc.dram_tensor(name, list(shape), dt, kind="ExternalInput").ap()

    def dout(name, shape, dt=F32):
        return nc.dram_tensor(name, list(shape), dt, kind="ExternalOutput").ap()

    x_d = din("x", [NT * 128, D])
    xs_d = din("xs", [128, D])
    st_d = din("st", [16, 4, 128, 128])
    ck_d = din("ck", [16, 128, 128])
    cv_d = din("cv", [16, 128, 128])
    wall_d = din("wall", [NCH, 128, 4096])
    lnmix_d = din("lnmix", [128, 8])
    lnmlp_d = din("lnmlp", [128, 8])
    lbl_d = din("lbl", [128, 8])
    hgn_d = din("hgn", [128, 1])
    sinkT_d = din("sinkT", [128, 4])
    gfin_d = din("gfin", [128, D])
    y_d = dout("y", [NT * 128, D])
    ys_d = dout("ys", [128, D])
    sp_d = dout("sp", [4, 128, 128])
    kp_d = dout("kp", [128, 128])
    vp_d = dout("vp", [128, 128])
    ss_d = dout("ss", [16, 4, 128, 128])
    ks_d = dout("ks", [16, 128, 128])
    vs_d = dout("vs", [16, 128, 128])
    wbf_d = nc.dram_tensor("wbf", [NCH, 128, 4096], BF16).ap()

    final_toks = []

    def wait_final():
        mx = {}
        for sn_, v_ in final_toks:
            mx[sn_] = max(mx.get(sn_, 0), v_)
        for sn_, v_ in mx.items():
            P._wait(P.E["sp"], (sn_, v_))

    def sb(name, free_shape, dt):
        return Buf(nc.alloc_sbuf_tensor("s_" + name, [128] + list(free_shape), dt)[:])

    C = NS()
    C.ident = sb("ident", [128], BF16)
    C.onesm = sb("onesm", [128], BF16)
    C.ones64 = sb("ones64", [64], BF16)
    C.mhalf = sb("mhalf", [1], F32)
    C.epsc = sb("epsc", [1], F32)
    C.cmask = sb("cmask", [512], F32)
    C.smask = sb("smask", [512], F32)
    C.swm_cur = sb("swm_cur", [512], BF16)
    C.swm_prev = sb("swm_prev", [512], BF16)
    C.seqmask = sb("seqmask", [16], F32)
    C.gfin = sb("gfin", [D], F32)
    C.lnmix = sb("lnmix", [8], F32)
    C.lnmlp = sb("lnmlp", [8], F32)
    C.wos = sb("wos", [8], F32)
    C.lbl = sb("lbl", [8], F32)
    C.hgn = sb("hgn", [1], F32)
    C.sinkT = sb("sinkT", [4], F32)
    C.esinkT = sb("esinkT", [4], F32)
    C.dl = sb("dl", [4], F32)
    C.lb = sb("lb", [4], F32)
    C.oml = sb("oml", [4], F32)
    C.noml = sb("noml", [4], F32)

    NXS = 8
    XR = nc.alloc_sbuf_tensor("XR", [128, NXS * 2048], BF16)
    xres = [Buf(XR[:, i * 2048:(i + 1) * 2048].bitcast(F32)) for i in range(NXS)]
    xnb = [sb("xnb%d" % i, [D], BF16) for i in range(3)]
    xnb_free = [0, 1, 2]
    junk = sb("junk", [D], BF16)
    NST = 6
    stat = [(sb("ssq%d" % i, [1], F32), sb("tv%d" % i, [1], F32), sb("rs%d" % i, [1], F32)) for i in range(NST)]
    xnT = sb("xnT", [8, 512], BF16)
    hnT = sb("hnT", [8, 512], BF16)
    NKV = 8
    kT_all = sb("kT_all", [NKV * 128], BF16)
    vS_all = sb("vS_all", [NKV, 128], BF16)
    Sall = nc.alloc_sbuf_tensor("Sst", [128, 512], F32)
    S = [Buf(Sall[:, h * 128:(h + 1) * 128]) for h in range(4)]
    S_bf = sb("S_bf", [512], BF16)
    omT = sb("omT", [8, 512], BF16)
    aT = sb("aT", [8, 512], BF16)
    wring = [sb("wr%d" % i, [8, 512], BF16) for i in range(NSLOT)]
    ybuf = [sb("yb%d" % i, [D], F32) for i in range(2)]
    kvout_p = sb("kvout", [256], F32)
    rl_p = [sb("rl%d" % i, [512], BF16) for i in range(2)]
    stg32 = [sb("stg%d" % i, [2, 512], F32) for i in range(4)]
    UBYTES = 50 * 1024
    U = nc.alloc_sbuf_tensor("U", [128, UBYTES // 2], BF16)

    banks = [Buf(nc.alloc_psum_tensor("pb%d" % i, [128, 512], F32)[:]) for i in range(8)]
    pfreeq = list(range(8))

    def psum():
        assert pfreeq, "out of PSUM banks"
        return banks[pfreeq.pop(0)]

    def pfree(b):
        for i, bb in enumerate(banks):
            if bb is b:
                assert i not in pfreeq
                pfreeq.append(i)
                return
        raise AssertionError("not a bank")

    def pool1(fn, writes, reads=()):
        return P.op("pool", fn, reads=reads, writes=writes)

    cl = [(C.lnmix, lnmix_d), (C.lnmlp, lnmlp_d), (C.lbl, lbl_d), (C.hgn, hgn_d), (C.sinkT, sinkT_d),
          (C.gfin, gfin_d)]
    for b, d_ in cl:
        P.dma(b.ap, d_, writes=[b], sem="cst")
    ctok = ("cst", P.dcnt["cst"])
    for b, _ in cl:
        b.w = ctok

    pool1(lambda e: e.memset(C.ident[:, :], 0.0), [C.ident])
    pool1(lambda e: e.affine_select(out=C.ident[:, :], in_=C.ident[:, :], pattern=[[-1, 128]], base=0,
                                    channel_multiplier=1, compare_op=ALU.not_equal, fill=1.0), [C.ident])
    pool1(lambda e: e.memset(C.onesm[:, :], 1.0 / 128.0), [C.onesm])
    pool1(lambda e: e.memset(C.ones64[:, :], 1.0), [C.ones64])
    pool1(lambda e: e.memset(C.mhalf[:, :], -0.5), [C.mhalf])
    pool1(lambda e: e.memset(C.epsc[:, :], EPS), [C.epsc])
    pool1(lambda e: e.memset(C.cmask[:, :], 1.0), [C.cmask])
    pool1(lambda e: e.affine_select(out=C.cmask[:, :], in_=C.cmask[:, :], pattern=[[0, 4], [1, 128]], base=0,
                                    channel_multiplier=-1, compare_op=ALU.is_ge, fill=0.0), [C.cmask])
    pool1(lambda e: e.memset(C.smask[:, :], 1.0), [C.smask])
    pool1(lambda e: e.memset(C.smask[:, :].rearrange("p (c t) -> p c t", t=128)[:, :, 0:1], 0.0), [C.smask])
    pool1(lambda e: e.memset(C.swm_cur[:, :], 1.0), [C.swm_cur])
    pool1(lambda e: e.affine_select(out=C.swm_cur[:, :], in_=C.swm_cur[:, :], pattern=[[0, 4], [1, 128]], base=0,
                                    channel_multiplier=-1, compare_op=ALU.is_ge, fill=0.0), [C.swm_cur])
    pool1(lambda e: e.memset(C.swm_prev[:, :], 1.0), [C.swm_prev])
    pool1(lambda e: e.affine_select(out=C.swm_prev[:, :], in_=C.swm_prev[:, :], pattern=[[0, 4], [-1, 128]], base=-1,
                                    channel_multiplier=1, compare_op=ALU.is_ge, fill=0.0), [C.swm_prev])
    pool1(lambda e: e.memset(C.seqmask[:, :], 1.0), [C.seqmask])
    pool1(lambda e: e.affine_select(out=C.seqmask[:, :], in_=C.seqmask[:, :], pattern=[[-8, 16]], base=0,
                                    channel_multiplier=1, compare_op=ALU.is_ge, fill=0.0), [C.seqmask])
    pool1(lambda e: e.affine_select(out=C.seqmask[:, :], in_=C.seqmask[:, :], pattern=[[8, 16]], base=7,
                                    channel_multiplier=-1, compare_op=ALU.is_ge, fill=0.0), [C.seqmask])
    pool1(lambda e: e.memset(C.wos[:, :], 1.0), [C.wos])
    for c in range(4):
        P.op("dve", lambda e, c=c: e.tensor_copy(out=C.wos[:, c:c + 1], in_=C.hgn[:, 0:1]), reads=[C.hgn], writes=[C.wos])
    P.op("dve", lambda e: e.tensor_tensor(out=C.dl[:, :], in0=C.lbl[:, 0:4], in1=C.lbl[:, 4:8], op=ALU.subtract),
         reads=[C.lbl], writes=[C.dl])
    P.op("act", lambda e: e.activation(out=C.lb[:, :], in_=C.dl[:, :], func=AF.Sigmoid), reads=[C.dl], writes=[C.lb])
    P.op("dve", lambda e: e.tensor_scalar(out=C.oml[:, :], in0=C.lb[:, :], scalar1=-1.0, scalar2=1.0, op0=ALU.mult,
                                          op1=ALU.add), reads=[C.lb], writes=[C.oml])
    P.op("dve", lambda e: e.tensor_scalar(out=C.noml[:, :], in0=C.lb[:, :], scalar1=-1.0, scalar2=None, op0=ALU.add),
         reads=[C.lb], writes=[C.noml])
    P.op("act", lambda e: e.activation(out=C.esinkT[:, :], in_=C.sinkT[:, :], func=AF.Exp), reads=[C.sinkT],
         writes=[C.esinkT])
    for h in range(4):
        pool1(lambda e, h=h: e.memset(S[h][:, :], 0.0), [S[h]])
    pool1(lambda e: e.memset(S_bf[:, :], 0.0), [S_bf])

    wslot_free = list(range(NSLOT))
    loaded = {}
    wr = {"next": 0, "use": 0}

    converted = set()
    wbfb = [Buf(None) for _ in range(NCH)]
    pending = []
    qc = {"i": 0}

    def flush_casts():
        while pending:
            cid, s_, qs = pending.pop(0)
            slot = wring[s_]
            sc = C.lnmix if cid < 6 else (C.wos if cid < 8 else (C.lnmlp if cid < 16 else None))
            for qi, st in enumerate(qs):
                en = ("act", "dve")[qi % 2]
                if sc is not None:
                    if en == "act":
                        fns = [lambda e, k=k, st=st: e.activation(out=slot[:, 2 * qi + k, :], in_=st[:, k, :],
                                                                  func=AF.Identity, scale=sc[:, 2 * qi + k:2 * qi + k + 1])
                               for k in range(2)]
                    else:
                        fns = [lambda e, k=k, st=st: e.tensor_scalar(out=slot[:, 2 * qi + k, :], in0=st[:, k, :],
                                                                     scalar1=sc[:, 2 * qi + k:2 * qi + k + 1], scalar2=None,
                                                                     op0=ALU.mult) for k in range(2)]
                    P.op(en, fns, reads=[st, sc], writes=[slot])
                else:
                    if en == "act":
                        P.op(en, lambda e, st=st: e.copy(out=slot[:, 2 * qi:2 * qi + 2, :], in_=st[:, :, :]), reads=[st],
                             writes=[slot])
                    else:
                        P.op(en, lambda e, st=st: e.tensor_copy(out=slot[:, 2 * qi:2 * qi + 2, :], in_=st[:, :, :]),
                             reads=[st], writes=[slot])
            P.dma(wbf_d[cid], slot.ap.rearrange("p a b -> p (a b)"), reads=[slot], writes=[wbfb[cid]], sem="ws%d" % s_)

    def w_load(l):
        cid = WORDER[l]
        s_ = wslot_free.pop(0)
        if cid in converted:
            P.dma(wring[s_].ap.rearrange("p a b -> p (a b)"), wbf_d[cid], reads=[wbfb[cid]], writes=[wring[s_]],
                  sem="wr%d" % s_)
        else:
            flush_casts()
            converted.add(cid)
            qs = []
            for qi in range(4):
                st = stg32[qc["i"] % 4]
                qc["i"] += 1
                P.dma(st.ap.rearrange("p a b -> p (a b)"), wall_d[cid][:, qi * 1024:(qi + 1) * 1024], writes=[st],
                      sem="pl%d" % (qc["i"] % 4))
                qs.append(st)
            pending.append((cid, s_, qs))
        loaded[l] = s_

    def w_prefetch():
        if record:
            return
        while wslot_free and wr["next"] < len(WORDER):
            w_load(wr["next"])
            wr["next"] += 1

    def w_use(cid):
        flush_casts()
        u = wr["use"]
        wr["use"] += 1
        if record:
            WORDER.append(cid)
        else:
            assert WORDER[u] == cid, (u, cid, WORDER[u])
        if u not in loaded:
            assert wr["next"] == u
            w_load(u)
            wr["next"] += 1
        flush_casts()
        w_prefetch()
        return wring[loaded[u]], u

    def w_release(u):
        wslot_free.append(loaded.pop(u))
        w_prefetch()

    def carve(TB, sample):
        cvr = Carver(U[:, :], UBYTES)
        B = NS()
        B.TB = TB
        nt = TB // 128
        B.sg4 = [cvr.take([TB], F32) for _ in range(2)]
        B.gt = cvr.take([TB], F32)
        B.kf = cvr.take([TB], F32)
        B.bt = cvr.take([TB], F32)
        B.Eb = cvr.take([TB], F32)
        B.Enb = cvr.take([TB], F32)
        B.qtT = cvr.take([4, TB], BF16)
        B.ktT = cvr.take([4, TB], BF16)
        B.sgT = cvr.take([4, TB], BF16)
        B.qsT = cvr.take([4, TB], BF16)
        B.v_tok = cvr.take([nt, 512], BF16)
        B.dec = cvr.take([64], F32)
        B.ATm = cvr.take([512], BF16)
        B.kt_tok = cvr.take([512], BF16)
        pd = cvr.take([512], F32)
        B.Pd = [Buf(pd[:, h * 128:(h + 1) * 128]) for h in range(4)]
        B.sqn = cvr.take([512], BF16)
        B.lnv = cvr.take([512], F32)
        B.rstdn = B.lnv
        B.t1 = cvr.take([512], BF16)
        B.PT = [cvr.take([512], BF16) for _ in range(4)]
        B.dsum = cvr.take([512], F32)
        B.rden = B.dsum
        B.rl = rl_p
        B.kvout = kvout_p
        if sample:
            B.cmask_s = cvr.take([512], F32)
            B.smask_s = cvr.take([128], F32)
            B.swm_scur = cvr.take([512], BF16)
            B.swm_c = cvr.take([512], BF16)
            B.ckb = cvr.take([16, 128], BF16)
            B.cvb = cvr.take([16, 128], BF16)
            B.kcT = cvr.take([16, 128], BF16)
            B.ktm = [cvr.take([512], BF16) for _ in range(2)]
            B.S0b = [cvr.take([512], BF16) for _ in range(2)]
            xc = Carver(XR[:, 4 * 2048:7 * 2048], 3 * 4096)
            B.ck32 = [xc.take([4, 128], F32)] * 2
            B.cv32 = [cvr.take([4, 128], F32)] * 2
            B.S0 = [xc.take([512], F32) for _ in range(3)]
            B.Sn = []
            B.SnT = []
            for _ in range(2):
                t_ = xc.take([512], F32)
                B.Sn.append([Buf(t_[:, h * 128:(h + 1) * 128]) for h in range(4)])
                B.SnT.append(t_.ap.rearrange("p (h v) -> p h v", h=4))
        return B

    stc = {"i": 0, "x": 0, "y": 0}

    def norm_stats(src):
        ssq, tv, rs = stat[stc["i"] % NST]
        stc["i"] += 1
        P.op("act", lambda e: e.activation(out=junk[:, :], in_=src[:, :], func=AF.Square, accum_out=ssq[:, 0:1]),
             reads=[src], writes=[ssq, junk])
        P.op("act", lambda e: e.activation(out=tv[:, :], in_=ssq[:, :], func=AF.Ln, scale=1.0 / D, bias=C.epsc[:, 0:1]),
             reads=[ssq, C.epsc], writes=[tv])
        P.op("act", lambda e: e.activation(out=rs[:, :], in_=tv[:, :], func=AF.Exp, scale=-0.5), reads=[tv], writes=[rs])
        return rs

    def norm_pre(src):
        rs = norm_stats(src)
        assert xnb_free, "xnb ring exhausted"
        xb = xnb[xnb_free.pop(0)]
        P.op("dve", lambda e: e.tensor_scalar(out=xb[:, :], in0=src[:, :], scalar1=rs[:, 0:1], scalar2=None,
                                              op0=ALU.mult), reads=[src, rs], writes=[xb])
        return xb

    def norm_post(xb, dstT, j):
        bk = psum()
        bkb = bk.ap.bitcast(BF16)
        P.op("pe", [lambda e, k=k: e.transpose(bkb[:, k * 128:(k + 1) * 128], xb[:, k * 128:(k + 1) * 128], C.ident[:, :])
                    for k in range(8)], reads=[xb, C.ident], writes=[bk])
        P.op("dve", lambda e: e.tensor_copy(out=dstT[:, 0:8, j * 128:(j + 1) * 128],
                                            in_=bkb.rearrange("p (a b) -> p a b", a=8)), reads=[bk], writes=[dstT])
        pfree(bk)
        for i_, b_ in enumerate(xnb):
            if b_ is xb:
                xnb_free.append(i_)

    def fm_mm(W, col0, srcT, TB, ncols=128):
        bk = psum()
        P.op("pe", [lambda e, k=k: e.matmul(bk[0:ncols, 0:TB], lhsT=W[:, k, col0:col0 + ncols], rhs=srcT[:, k, 0:TB],
                                            start=(k == 0), stop=(k == 7)) for k in range(8)],
             reads=[W, srcT], writes=[bk])
        return bk

    def tm_mm(W, ncols, srcT, j):
        bk = psum()
        P.op("pe", [lambda e, k=k: e.matmul(bk[:, 0:ncols], lhsT=srcT[:, k, j * 128:(j + 1) * 128], rhs=W[:, k, 0:ncols],
                                            start=(k == 0), stop=(k == 7)) for k in range(8)],
             reads=[W, srcT], writes=[bk])
        return bk

    def head12_steps(B, tiles, slots, sample):
        TB = B.TB
        nt = TB // 128
        xb_prev = None
        for j in range(nt):
            xb = norm_pre(slots[j])
            if xb_prev is not None:
                norm_post(xb_prev, xnT, j - 1)
            xb_prev = xb
            yield
        norm_post(xb_prev, xnT, nt - 1)
        yield
        W, u = w_use(0)
        for h in range(4):
            bk = fm_mm(W, h * 128, xnT, TB)
            P.op("act", lambda e, h=h, bk=bk: e.activation(out=B.qtT[:, h, :], in_=bk[:, 0:TB], func=AF.Silu),
                 reads=[bk], writes=[B.qtT])
            pfree(bk)
            yield
        w_release(u)
        W, u = w_use(1)
        for h in range(4):
            bk = fm_mm(W, h * 128, xnT, TB)
            P.op("act", lambda e, h=h, bk=bk: e.activation(out=B.sgT[:, h, :], in_=bk[:, 0:TB], func=AF.Silu),
                 reads=[bk], writes=[B.sgT])
            pfree(bk)
            yield
        w_release(u)
        W, u = w_use(2)
        sm = B.smask_s if sample else C.smask
        for hp in range(2):
            for hh in range(2):
                h = 2 * hp + hh
                bk = fm_mm(W, h * 128, xnT, TB)
                P.op("act", lambda e, hh=hh, bk=bk: e.activation(out=B.sg4[hh][:, :], in_=bk[:, 0:TB], func=AF.Sigmoid),
                     reads=[bk], writes=[B.sg4[hh]])
                pfree(bk)
                yield
            if hp == 1:
                w_release(u)
            for hh in range(2):
                h = 2 * hp + hh
                sg = B.sg4[hh]
                P.op("act", lambda e, h=h, sg=sg: e.activation(out=B.gt[:, :], in_=sg[:, :], func=AF.Ln,
                                                               scale=C.oml[:, h:h + 1], bias=C.lb[:, h:h + 1]),
                     reads=[sg, C.oml, C.lb], writes=[B.gt])
                P.op("dve", lambda e, h=h, sg=sg: e.tensor_scalar(out=B.kf[:, :], in0=sg[:, :], scalar1=C.noml[:, h:h + 1],
                                                                  scalar2=C.oml[:, h:h + 1], op0=ALU.mult, op1=ALU.add),
                     reads=[sg, C.noml, C.oml], writes=[B.kf])
                P.op("dve", lambda e: e.tensor_tensor_scan(out=B.bt[:, :], data0=sm[:, 0:TB], data1=B.gt[:, :],
                                                           initial=0.0, op0=ALU.mult, op1=ALU.add), reads=[sm, B.gt],
                     writes=[B.bt])
                P.op("act", lambda e: e.activation(out=B.Eb[:, :], in_=B.bt[:, :], func=AF.Exp), reads=[B.bt],
                     writes=[B.Eb])
                P.op("act", lambda e: e.activation(out=B.Enb[:, :], in_=B.bt[:, :], func=AF.Exp, scale=-1.0),
                     reads=[B.bt], writes=[B.Enb])
                P.op("dve", lambda e, h=h: e.tensor_tensor(out=B.ktT[:, h, :], in0=B.kf[:, :], in1=B.Enb[:, :],
                                                           op=ALU.mult), reads=[B.kf, B.Enb], writes=[B.ktT])
                P.op("dve", lambda e, h=h: e.tensor_tensor(out=B.qtT[:, h, :], in0=B.qtT[:, h, :], in1=B.Eb[:, :],
                                                           op=ALU.mult), reads=[B.Eb], writes=[B.qtT])
                if sample:
                    P.op("act", lambda e, h=h: e.copy(out=B.dec[:, h * 16:(h + 1) * 16],
                                                              in_=B.Eb[:, :].rearrange("p (s i) -> p s i", i=8)[:, :, 7]),
                         reads=[B.Eb], writes=[B.dec])
                else:
                    P.op("act", lambda e, h=h: e.copy(
                        out=B.dec[:, h * nt:(h + 1) * nt],
                        in_=B.Eb[:, :].rearrange("p (j t) -> p j t", t=128)[:, :, 127]), reads=[B.Eb], writes=[B.dec])
                yield
        W, u = w_use(3)
        for g in range(4):
            bk = fm_mm(W, g * 128, xnT, TB)
            P.op("dve", lambda e, g=g, bk=bk: e.tensor_copy(out=B.qsT[:, g, :], in_=bk[:, 0:TB]), reads=[bk],
                 writes=[B.qsT])
            pfree(bk)
            yield
        w_release(u)
        W4, u = w_use(4)
        for j, n in enumerate(tiles):
            bk = tm_mm(W4, 512, xnT, j)
            P.op("act", lambda e, j=j, bk=bk: e.copy(out=B.v_tok[:, j, :], in_=bk[:, :]), reads=[bk], writes=[B.v_tok])
            pfree(bk)
            yield
        w_release(u)
        W5, u = w_use(5)
        bk = fm_mm(W5, 0, xnT, TB)
        s0_ = tiles[0] % NKV
        P.op("act", lambda e, bk=bk: e.copy(out=kT_all[:, s0_ * 128:s0_ * 128 + TB], in_=bk[:, 0:TB]), reads=[bk],
             writes=[kT_all])
        pfree(bk)
        yield
        for j, n in enumerate(tiles):
            bk = tm_mm(W5, 256, xnT, j)
            ns = n % NKV
            if sample or n == NT - 1:
                P.op("act", lambda e, bk=bk: e.copy(out=B.kvout[:, :], in_=bk[:, 0:256]), reads=[bk], writes=[B.kvout])
                P.op("dve", lambda e, ns=ns: e.tensor_copy(out=vS_all[:, ns, :], in_=B.kvout[:, 128:256]),
                     reads=[B.kvout], writes=[vS_all])
                if sample:
                    final_toks.append(P.dma(ks_d[:, 120:128, :], B.kvout[:, 0:128], reads=[B.kvout], sem="kvo"))
                    final_toks.append(P.dma(vs_d[:, 120:128, :], B.kvout[:, 128:256], reads=[B.kvout], sem="kvo"))
                else:
                    final_toks.append(P.dma(kp_d, B.kvout[:, 0:128], reads=[B.kvout], sem="kvo"))
                    final_toks.append(P.dma(vp_d, B.kvout[:, 128:256], reads=[B.kvout], sem="kvo"))
            else:
                P.op("dve", lambda e, ns=ns, bk=bk: e.tensor_copy(out=vS_all[:, ns, :], in_=bk[:, 128:256]), reads=[bk],
                     writes=[vS_all])
            pfree(bk)
            yield
        w_release(u)

    def hg_front(B, j, cmask):
        cs = slice(j * 128, (j + 1) * 128)
        bkA = psum()
        P.op("pe", [lambda e, h=h: e.matmul(bkA[:, h * 128:(h + 1) * 128], lhsT=B.ktT[:, h, cs], rhs=B.qtT[:, h, cs],
                                            start=True, stop=True) for h in range(4)],
             reads=[B.ktT, B.qtT], writes=[bkA])
        P.op("dve", lambda e: e.tensor_tensor(out=B.ATm[:, :], in0=bkA[:, :], in1=cmask[:, :], op=ALU.mult),
             reads=[bkA, cmask], writes=[B.ATm])
        pfree(bkA)
        bkT = psum()
        bkTb = bkT.ap.bitcast(BF16)
        P.op("pe", [lambda e, h=h: e.transpose(bkTb[:, h * 128:(h + 1) * 128], B.ktT[:, h, cs], C.ident[:, :])
                    for h in range(4)], reads=[B.ktT, C.ident], writes=[bkT])
        P.op("act", lambda e: e.copy(out=B.kt_tok[:, :], in_=bkTb[:, 0:512]), reads=[bkT], writes=[B.kt_tok])
        pfree(bkT)

    def hg_square(B, oaps, obufs):
        ub = list({id(b): b for b in obufs}.values())
        if len(ub) == 1:
            P.op("act", lambda e: e.activation(out=B.sqn[:, :], in_=ub[0][:, :], func=AF.Square), reads=ub,
                 writes=[B.sqn])
        else:
            P.op("act", [lambda e, h=h: e.activation(out=B.sqn[:, h * 128:(h + 1) * 128], in_=oaps[h], func=AF.Square)
                         for h in range(4)], reads=ub, writes=[B.sqn])

    def hg_norm_out(B, j, oaps, obufs):
        cs = slice(j * 128, (j + 1) * 128)
        ub = list({id(b): b for b in obufs}.values())
        bkM = psum()
        P.op("pe", lambda e: e.matmul(bkM[:, :], lhsT=C.onesm[:, :], rhs=B.sqn[:, :], start=True, stop=True),
             reads=[C.onesm, B.sqn], writes=[bkM])
        P.op("act", lambda e: e.activation(out=B.lnv[:, :], in_=bkM[:, :], func=AF.Ln, bias=C.epsc[:, 0:1]),
             reads=[bkM, C.epsc], writes=[B.lnv])
        pfree(bkM)
        P.op("act", lambda e: e.activation(out=B.rstdn[:, :], in_=B.lnv[:, :], func=AF.Exp, scale=-0.5),
             reads=[B.lnv], writes=[B.rstdn])
        if len(ub) == 1:
            P.op("dve", lambda e: e.tensor_tensor(out=B.t1[:, :], in0=ub[0][:, :], in1=B.rstdn[:, :], op=ALU.mult),
                 reads=ub + [B.rstdn], writes=[B.t1])
        else:
            P.op("dve", [lambda e, h=h: e.tensor_tensor(out=B.t1[:, h * 128:(h + 1) * 128], in0=oaps[h],
                                                        in1=B.rstdn[:, h * 128:(h + 1) * 128], op=ALU.mult)
                         for h in range(4)], reads=ub + [B.rstdn], writes=[B.t1])
        P.op("dve", lambda e: e.tensor_tensor(out=omT[:, 0:4, cs], in0=B.t1[:, :].rearrange("p (a b) -> p a b", a=4),
                                              in1=B.sgT[:, 0:4, cs], op=ALU.mult), reads=[B.t1, B.sgT], writes=[omT])
        for b in ub:
            pfree(b)

    def hg_state(B, j):
        nt = B.TB // 128
        cs = slice(j * 128, (j + 1) * 128)
        bkO = psum()
        fns = []
        for h in range(4):
            hs = slice(h * 128, (h + 1) * 128)
            fns.append(lambda e, h=h, hs=hs: e.matmul(bkO[:, hs], lhsT=B.v_tok[:, j, hs], rhs=B.ATm[:, hs], start=True,
                                                      stop=False))
            fns.append(lambda e, h=h, hs=hs: e.matmul(bkO[:, hs], lhsT=S_bf[:, hs], rhs=B.qtT[:, h, cs], start=False,
                                                      stop=True))
        P.op("pe", fns, reads=[B.v_tok, B.ATm, S_bf, B.qtT], writes=[bkO])
        bkP = psum()
        P.op("pe", [lambda e, hs=slice(h * 128, (h + 1) * 128): e.matmul(bkP[:, hs], lhsT=B.kt_tok[:, hs],
                                                                         rhs=B.v_tok[:, j, hs], start=True, stop=True)
                    for h in range(4)], reads=[B.kt_tok, B.v_tok], writes=[bkP])
        for h in range(4):
            hs = slice(h * 128, (h + 1) * 128)
            dc = B.dec[:, h * nt + j:h * nt + j + 1]
            P.op("act", lambda e, h=h, hs=hs, dc=dc: e.activation(out=B.Pd[h][:, :], in_=bkP[:, hs], func=AF.Identity,
                                                                  scale=dc), reads=[bkP, B.dec], writes=[B.Pd[h]])
            P.op("dve", lambda e, h=h, dc=dc: e.scalar_tensor_tensor(out=S[h][:, :], in0=S[h][:, :], scalar=dc,
                                                                     in1=B.Pd[h][:, :], op0=ALU.mult, op1=ALU.add),
                 reads=[B.Pd[h], B.dec], writes=[S[h]])
        pfree(bkP)
        P.op("act", lambda e: e.copy(out=S_bf[:, :], in_=Sall[:, :]), reads=S, writes=[S_bf])
        hg_square(B, None, [bkO])
        return bkO

    def swa_finish(B, j, bkO, bkD, sample=False):
        cs = slice(j * 128, (j + 1) * 128)
        if sample:
            dv = B.dsum[:, :].rearrange("p (s g i) -> p s g i", s=16, g=4)
            bv = bkD[:, :].rearrange("p (s g i) -> p s g i", s=16, g=4)
            P.op("dve", [lambda e, g=g: e.tensor_scalar(out=dv[:, :, g, :], in0=bv[:, :, g, :],
                                                        scalar1=C.esinkT[:, g:g + 1], scalar2=None, op0=ALU.add)
                         for g in range(4)], reads=[bkD, C.esinkT], writes=[B.dsum])
        else:
            P.op("dve", [lambda e, g=g: e.tensor_scalar(out=B.dsum[:, g * 128:(g + 1) * 128],
                                                        in0=bkD[:, g * 128:(g + 1) * 128], scalar1=C.esinkT[:, g:g + 1],
                                                        scalar2=None, op0=ALU.add) for g in range(4)],
                 reads=[bkD, C.esinkT], writes=[B.dsum])
        pfree(bkD)
        P.op("dve", lambda e: e.reciprocal(out=B.rden[:, :], in_=B.dsum[:, :]), reads=[B.dsum], writes=[B.rden])
        if sample:
            P.op("dve", lambda e: e.tensor_tensor(
                out=omT[:, 4:8, cs].rearrange("p g (s i) -> p g s i", i=8),
                in0=bkO[:, :].rearrange("p (s g i) -> p g s i", s=16, g=4),
                in1=B.rden[:, :].rearrange("p (s g i) -> p g s i", s=16, g=4), op=ALU.mult),
                 reads=[bkO, B.rden], writes=[omT])
        else:
            P.op("dve", lambda e: e.tensor_tensor(out=omT[:, 4:8, cs], in0=bkO[:, :].rearrange("p (a b) -> p a b", a=4),
                                                  in1=B.rden[:, :].rearrange("p (a b) -> p a b", a=4), op=ALU.mult),
                 reads=[bkO, B.rden], writes=[omT])
        pfree(bkO)

    ptc = {"i": 0}

    def swa_st(B, rows, kcols, qrhs, mask):
        bkS = psum()
        P.op("pe", lambda e: e.matmul(bkS[:, :], lhsT=kT_all[rows, kcols], rhs=qrhs, start=True, stop=True),
             reads=[kT_all, B.qsT], writes=[bkS])
        pt = B.PT[ptc["i"] % 4]
        ptc["i"] += 1
        P.op("act", lambda e: e.activation(out=pt[:, :], in_=bkS[:, :], func=AF.Exp, scale=0.125), reads=[bkS],
             writes=[pt])
        pfree(bkS)
        P.op("dve", lambda e: e.tensor_tensor(out=pt[:, :], in0=pt[:, :], in1=mask[:, :], op=ALU.mult), reads=[mask],
             writes=[pt])
        return pt

    def p3_tile_steps(B, j, n):
        cs = slice(j * 128, (j + 1) * 128)
        blocks = []
        for kv in range(2):
            rows = slice(64 * kv, 64 * kv + 64)
            bl = ([(n - 1, C.swm_prev)] if n > 0 else []) + [(n, C.swm_cur)]
            for bi, (kt, mask) in enumerate(bl):
                blocks.append((rows, kt % NKV, mask, bi == 0, bi == len(bl) - 1))

        def st(bi):
            rows, ks, mask, _, _ = blocks[bi]
            return swa_st(B, rows, slice(ks * 128, (ks + 1) * 128), B.qsT[rows, 0:4, cs], mask)

        def pv(bi, pt):
            rows, ks, mask, first, last = blocks[bi]
            P.op("pe", [lambda e: e.matmul(swO[rows, :], lhsT=vS_all[:, ks, rows], rhs=pt[:, :], start=first, stop=last),
                        lambda e: e.matmul(swD[rows, :], lhsT=C.ones64[:, :], rhs=pt[:, :], start=first, stop=last)],
                 reads=[vS_all, pt, C.ones64], writes=[swO, swD])

        hg_front(B, j, C.cmask)
        yield
        swO = psum()
        swD = psum()
        pts = {0: st(0)}
        yield
        hgO = hg_state(B, j)
        yield
        pv(0, pts[0])
        pts[1] = st(1)
        yield
        hg_norm_out(B, j, [hgO[:, h * 128:(h + 1) * 128] for h in range(4)], [hgO] * 4)
        yield
        for bi in range(1, len(blocks)):
            pv(bi, pts[bi])
            if bi + 1 < len(blocks):
                pts[bi + 1] = st(bi + 1)
            yield
        swa_finish(B, j, swO, swD)
        yield

    def head3_steps(B, tiles):
        for j, n in enumerate(tiles):
            for _ in p3_tile_steps(B, j, n):
                yield
        if tiles[-1] == NT - 1:
            for h in range(4):
                final_toks.append(P.dma(sp_d[h], S[h].ap, reads=[S[h]], sem="spo"))

    def p3_sample(B):
        TS = NT % NKV
        pool1(lambda e: e.memset(B.smask_s[:, :], 1.0), [B.smask_s])
        pool1(lambda e: e.memset(B.smask_s[:, :].rearrange("p (c t) -> p c t", t=8)[:, :, 0:1], 0.0), [B.smask_s])
        for mk, pa, pb in ((B.cmask_s, [[0, 4], [8, 16], [1, 8]], [[0, 4], [-8, 16], [0, 8]]),
                           (B.swm_scur, [[8, 16], [0, 4], [1, 8]], [[-8, 16], [0, 4], [0, 8]])):
            pool1(lambda e, mk=mk: e.memset(mk[:, :], 1.0), [mk])
            pool1(lambda e, mk=mk, pa=pa: e.affine_select(out=mk[:, :], in_=mk[:, :], pattern=pa, base=0,
                                                          channel_multiplier=-1, compare_op=ALU.is_ge, fill=0.0), [mk])
            pool1(lambda e, mk=mk, pb=pb: e.affine_select(out=mk[:, :], in_=mk[:, :], pattern=pb, base=0,
                                                          channel_multiplier=1, compare_op=ALU.is_ge, fill=0.0), [mk])
        pool1(lambda e: e.memset(B.swm_c[:, :], 1.0), [B.swm_c])
        pool1(lambda e: e.affine_select(out=B.swm_c[:, :], in_=B.swm_c[:, :], pattern=[[0, 64], [-1, 8]], base=-1,
                                        channel_multiplier=1, compare_op=ALU.is_ge, fill=0.0), [B.swm_c])

    def p3_sample_body(B):
        TS = NT % NKV
        for c4 in range(4):
            k32 = B.ck32[c4 % 2]
            v32 = B.cv32[c4 % 2]
            P.dma(k32.ap, ck_d[c4 * 4:(c4 + 1) * 4].rearrange("s k c -> k s c"), writes=[k32], sem="ck%d" % (c4 % 2))
            P.dma(v32.ap, cv_d[c4 * 4:(c4 + 1) * 4].rearrange("s k c -> k s c"), writes=[v32], sem="cv%d" % (c4 % 2))
            P.op("dve", lambda e, c4=c4, k32=k32: e.tensor_copy(out=B.ckb[:, c4 * 4:(c4 + 1) * 4, :], in_=k32[:, :, :]),
                 reads=[k32], writes=[B.ckb])
            P.op("pool", lambda e, c4=c4, v32=v32: e.tensor_copy(out=B.cvb[:, c4 * 4:(c4 + 1) * 4, :], in_=v32[:, :, :]),
                 reads=[v32], writes=[B.cvb])
        for half in range(2):
            bk = psum()
            bkb = bk.ap.bitcast(BF16)
            P.op("pe", [lambda e, i=i: e.transpose(bkb[:, i * 128:(i + 1) * 128], B.ckb[:, half * 8 + i, :], C.ident[:, :])
                        for i in range(8)], reads=[B.ckb, C.ident], writes=[bk])
            P.op("act", lambda e, bkb=bkb: e.copy(out=B.kcT[:, half * 8:(half + 1) * 8, :],
                                                  in_=bkb.rearrange("p (a b) -> p a b", a=8)), reads=[bk],
                 writes=[B.kcT])
            pfree(bk)
        final_toks.append(P.dma(ks_d[:, 0:120, :], ck_d[:, 8:128, :], sem="kvc"))
        final_toks.append(P.dma(vs_d[:, 0:120, :], cv_d[:, 8:128, :], sem="kvc"))

        hg_front(B, 0, B.cmask_s)
        bkO = [psum() for _ in range(4)]
        P.op("pe", [lambda e, h=h: e.matmul(bkO[h][:, 0:128], lhsT=B.v_tok[:, 0, h * 128:(h + 1) * 128],
                                            rhs=B.ATm[:, h * 128:(h + 1) * 128], start=True, stop=False)
                    for h in range(4)], reads=[B.v_tok, B.ATm], writes=bkO)
        def stage_a(seq):
            s0 = B.S0[seq % 3]
            s0b = B.S0b[seq % 2]
            ktm = B.ktm[seq % 2]
            P.dma(s0.ap.rearrange("p (h v) -> p h v", h=4), st_d[seq].rearrange("h d v -> d h v"), writes=[s0],
                  sem="s0%d" % (seq % 3))
            P.op("pool", lambda e: e.tensor_copy(out=s0b[:, :], in_=s0[:, :]), reads=[s0], writes=[s0b])
            P.op("pe", [lambda e, h=h: e.matmul(bkO[h][:, seq * 8:(seq + 1) * 8], lhsT=s0b[:, h * 128:(h + 1) * 128],
                                                rhs=B.qtT[:, h, seq * 8:(seq + 1) * 8], start=False, stop=(seq == 15))
                        for h in range(4)], reads=[s0b, B.qtT], writes=bkO)
            P.op("dve", lambda e: e.tensor_scalar(out=ktm[:, :], in0=B.kt_tok[:, :], scalar1=C.seqmask[:, seq:seq + 1],
                                                  scalar2=None, op0=ALU.mult), reads=[B.kt_tok, C.seqmask], writes=[ktm])
            bkP = psum()
            P.op("pe", [lambda e, hs=slice(h * 128, (h + 1) * 128): e.matmul(bkP[:, hs], lhsT=ktm[:, hs],
                                                                             rhs=B.v_tok[:, 0, hs], start=True, stop=True)
                        for h in range(4)], reads=[ktm, B.v_tok], writes=[bkP])
            return bkP

        def stage_b(seq, bkP):
            s0 = B.S0[seq % 3]
            sn = B.Sn[seq % 2]
            for h in range(4):
                hs = slice(h * 128, (h + 1) * 128)
                dc = B.dec[:, h * 16 + seq:h * 16 + seq + 1]
                P.op("act", lambda e, h=h, hs=hs, dc=dc: e.activation(out=B.Pd[h][:, :], in_=bkP[:, hs], func=AF.Identity,
                                                                      scale=dc), reads=[bkP, B.dec], writes=[B.Pd[h]])
                P.op("dve", lambda e, h=h, hs=hs, dc=dc: e.scalar_tensor_tensor(out=sn[h][:, :], in0=s0[:, hs], scalar=dc,
                                                                                in1=B.Pd[h][:, :], op0=ALU.mult,
                                                                                op1=ALU.add),
                     reads=[s0, B.Pd[h], B.dec], writes=[sn[h]])
            pfree(bkP)
            final_toks.append(P.dma(ss_d[seq].rearrange("h d v -> d h v"), B.SnT[seq % 2], reads=sn,
                                    sem="sn%d" % (seq % 2)))

        pend = {0: stage_a(0)}
        for seq in range(16):
            if seq + 1 < 16:
                pend[seq + 1] = stage_a(seq + 1)
            stage_b(seq, pend.pop(seq))
            yield
        oaps = [bkO[h][:, 0:128] for h in range(4)]
        hg_square(B, oaps, bkO)
        hg_norm_out(B, 0, oaps, bkO)

        bO = psum()
        bD = psum()
        for kv in range(2):
            rows = slice(64 * kv, 64 * kv + 64)
            ptn = swa_st(B, rows, slice(TS * 128, (TS + 1) * 128),
                         B.qsT[rows, 0:4, 0:128].rearrange("p g (s i) -> p s g i", i=8), B.swm_scur)
            bkC = psum()
            P.op("pe", [lambda e, s=s: e.matmul(bkC[:, s * 32:(s + 1) * 32], lhsT=B.kcT[rows, s, :],
                                                rhs=B.qsT[rows, 0:4, s * 8:(s + 1) * 8], start=True, stop=True)
                        for s in range(16)], reads=[B.kcT, B.qsT], writes=[bkC])
            pc = B.PT[ptc["i"] % 4]
            ptc["i"] += 1
            P.op("act", lambda e, pc=pc, bkC=bkC: e.activation(out=pc[:, :], in_=bkC[:, :], func=AF.Exp, scale=0.125),
                 reads=[bkC], writes=[pc])
            pfree(bkC)
            P.op("pool", lambda e, pc=pc: e.tensor_tensor(out=pc[:, :], in0=pc[:, :], in1=B.swm_c[:, :], op=ALU.mult),
                 reads=[B.swm_c], writes=[pc])
            fns = [lambda e: e.matmul(bO[rows, :], lhsT=vS_all[:, TS, rows], rhs=ptn[:, :], start=True, stop=False),
                   lambda e: e.matmul(bD[rows, :], lhsT=C.ones64[:, :], rhs=ptn[:, :], start=True, stop=False)]
            for s in range(16):
                fns.append(lambda e, s=s: e.matmul(bO[rows, s * 32:(s + 1) * 32], lhsT=B.cvb[:, s, rows],
                                                   rhs=pc[:, s * 32:(s + 1) * 32], start=False, stop=(s == 15)))
                fns.append(lambda e, s=s: e.matmul(bD[rows, s * 32:(s + 1) * 32], lhsT=C.ones64[:, :],
                                                   rhs=pc[:, s * 32:(s + 1) * 32], start=False, stop=(s == 15)))
            P.op("pe", fns, reads=[vS_all, ptn, pc, B.cvb, C.ones64], writes=[bO, bD])
        swa_finish(B, 0, bO, bD, sample=True)
        yield

    def p45(B, slots):
        nt = B.TB // 128
        W6, u6 = w_use(6)
        W7, u7 = w_use(7)
        pend = []
        for j in range(nt):
            for nh, W in enumerate((W6, W7)):
                bk = psum()
                P.op("pe", [lambda e, k=k, W=W, bk=bk: e.matmul(bk[:, :], lhsT=omT[:, k, j * 128:(j + 1) * 128],
                                                                rhs=W[:, k, :], start=(k == 0), stop=(k == 7))
                            for k in range(8)], reads=[omT, W], writes=[bk])
                xs_ = slots[j]
                P.op("dve", lambda e, nh=nh, bk=bk, xs_=xs_: e.tensor_tensor(
                    out=xs_[:, nh * 512:(nh + 1) * 512], in0=bk[:, :], in1=xs_[:, nh * 512:(nh + 1) * 512], op=ALU.add),
                     reads=[bk], writes=[xs_])
                pfree(bk)
            xb = norm_pre(slots[j])
            pend.append((xb, j))
            if len(pend) > 2:
                xb0, j0 = pend.pop(0)
                norm_post(xb0, hnT, j0)
        w_release(u6)
        w_release(u7)
        return pend

    def mlp_steps(B, slots, xb_last):
        TB = B.TB
        nt = TB // 128
        for xb0, j0 in xb_last:
            norm_post(xb0, hnT, j0)
            yield
        for q in range(4):
            for rr in range(2):
                W, u = w_use(8 + 2 * q + rr)
                for qq in range(4):
                    fl = rr * 4 + qq
                    bk = fm_mm(W, qq * 128, hnT, TB)
                    rl = B.rl[fl % 2]
                    P.op("act", lambda e, bk=bk, rl=rl: e.activation(out=rl[:, 0:TB], in_=bk[:, 0:TB], func=AF.Relu),
                         reads=[bk], writes=[rl])
                    pfree(bk)
                    P.op("act", lambda e, fl=fl, rl=rl: e.activation(out=aT[:, fl, 0:TB], in_=rl[:, 0:TB],
                                                                     func=AF.Square), reads=[rl], writes=[aT])
                    yield
                w_release(u)
            for nh in range(2):
                W, u = w_use(16 + nh * 4 + q)
                for j in range(nt):
                    bk = psum()
                    P.op("pe", [lambda e, f=f, W=W, j=j, bk=bk: e.matmul(
                        bk[:, :], lhsT=aT[:, f, j * 128:(j + 1) * 128], rhs=W[:, f, :], start=(f == 0), stop=(f == 7))
                        for f in range(8)], reads=[aT, W], writes=[bk])
                    xs_ = slots[j]
                    P.op("dve", lambda e, nh=nh, bk=bk, xs_=xs_: e.tensor_tensor(
                        out=xs_[:, nh * 512:(nh + 1) * 512], in0=bk[:, :], in1=xs_[:, nh * 512:(nh + 1) * 512],
                        op=ALU.add), reads=[bk], writes=[xs_])
                    pfree(bk)
                    yield
                w_release(u)

    def p8(slots, ydst):
        for j in range(len(slots)):
            rs = norm_stats(slots[j])
            xs_ = slots[j]
            ysem = "yx%d" % xres_index(xs_)
            P.op("dve", lambda e, xs_=xs_, rs=rs: e.scalar_tensor_tensor(
                out=xs_[:, :], in0=xs_[:, :], scalar=rs[:, 0:1], in1=C.gfin[:, :], op0=ALU.mult, op1=ALU.mult),
                 reads=[rs, C.gfin], writes=[xs_])
            final_toks.append(P.dma(ydst[j * 128:(j + 1) * 128, :], xs_.ap, reads=[xs_], sem=ysem))

    def xres_index(b):
        for i_, bb in enumerate(xres):
            if bb is b:
                return i_
        raise AssertionError("not an xres slot")

    def drain(gen):
        for _ in gen:
            pass

    def interleave(genA, genB):
        a_ok = b_ok = True
        while a_ok or b_ok:
            if b_ok:
                try:
                    next(genB)
                except StopIteration:
                    b_ok = False
            if a_ok:
                try:
                    next(genA)
                except StopIteration:
                    a_ok = False

    def chain(*gens):
        for g_ in gens:
            for _ in g_:
                yield

    def main_schedule():
        xload = {}

        def load_x(n):
            s = xres[n % NXS]
            P.dma(s.ap, x_d[n * 128:(n + 1) * 128, :], writes=[s], sem="x%d" % (n % NXS))
            xload[n] = s

        def gtiles(g):
            return list(range(g * G, (g + 1) * G))

        Bs = carve(128, True)
        REG["B"] = Bs
        xsmp = xres[7]
        P.dma(xsmp.ap, xs_d, writes=[xsmp], sem="x7")
        w_prefetch()
        p3_sample(Bs)
        drain(head12_steps(Bs, [NT], [xsmp], True))
        chk("p2s")
        drain(p3_sample_body(Bs))
        chk("p3s")
        P.wait_engines(["act", "dve", "pool", "pe", "sp"], ["act", "dve", "pool", "pe"])
        wait_final()
        Bp = carve(512, False)
        REG["B"] = Bp
        for n in gtiles(0):
            load_x(n)

        def sample_tail():
            xb_last = p45(Bs, [xsmp])
            yield
            for _ in mlp_steps(Bs, [xsmp], xb_last):
                yield
            p8([xsmp], ys_d)
            yield

        interleave(sample_tail(),
                   chain(head12_steps(Bp, gtiles(0), [xload[n] for n in gtiles(0)], False), head3_steps(Bp, gtiles(0))))
        chk("h0")
        for g in range(NG):
            slots = [xload[n] for n in gtiles(g)]
            if g + 1 < NG:
                for n in gtiles(g + 1):
                    load_x(n)
            xb_last = p45(Bp, slots)
            genA = mlp_steps(Bp, slots, xb_last)
            if g + 1 < NG:
                nslots = [xload[n] for n in gtiles(g + 1)]
                genB = chain(head12_steps(Bp, gtiles(g + 1), nslots, False), head3_steps(Bp, gtiles(g + 1)))
                interleave(genA, genB)
            else:
                drain(genA)
            p8(slots, y_d[g * G * 128:(g + 1) * G * 128, :])
            chk("g%d" % g)

    try:
        main_schedule()
    except _Stop:
        pass
    REG.update(dict(xnT=xnT, hnT=hnT, omT=omT, kT_all=kT_all, vS_all=vS_all, S_bf=S_bf, xres0=xres[0], xres1=xres[1],
                    lb=C.lb, seqmask=C.seqmask))
    for dn in dumps:
        b = REG[dn] if dn in REG else getattr(REG["B"], dn)
        shp = [int(v) for v in b.ap.shape]
        dd = nc.dram_tensor("dbg_" + dn, shp, b.ap.dtype, kind="ExternalOutput").ap()
        final_toks.append(P.dma(dd, b.ap, reads=[b], sem="dbg"))

    wait_final()
    P.wait_engines(["sp"], ["act", "dve", "pool", "pe"])
    return nc, WORDER


def _chunk_k(a):
    return np.ascontiguousarray(a.reshape(8, 128, 512).transpose(1, 0, 2)).reshape(128, 4096)


def _build_wall(w_in, w_out, w_up, w_down):
    zq, zf, zi, zg = w_in[:, 0:512], w_in[:, 512:1024], w_in[:, 1024:1536], w_in[:, 1536:2048]
    sq, sk, sv = w_in[:, 2048:2560], w_in[:, 2560:2688], w_in[:, 2688:2816]
    sqP = np.concatenate([np.concatenate([sq[:, g * 64:(g + 1) * 64], sq[:, (4 + g) * 64:(5 + g) * 64]], axis=1)
                          for g in range(4)], axis=1)
    c5 = np.concatenate([sk, sv, np.zeros((1024, 256), np.float32)], axis=1)
    chunks = [_chunk_k(a) for a in (zq, zg, zf, sqP, zi, c5)]
    perm = list(range(512)) + [512 + (kv * 4 + g) * 64 + d for g in range(4) for kv in range(2) for d in range(64)]
    wo = w_out[perm]
    chunks += [_chunk_k(wo[:, nh * 512:(nh + 1) * 512]) for nh in range(2)]
    chunks += [_chunk_k(w_up[:, r * 512:(r + 1) * 512]) for r in range(8)]
    for nh in range(2):
        for r4 in range(4):
            chunks.append(_chunk_k(w_down[r4 * 1024:(r4 + 1) * 1024, nh * 512:(nh + 1) * 512]))
    return np.ascontiguousarray(np.stack(chunks, axis=0), dtype=np.float32)


def _make_in_maps(x_prompt, x_sample, state_hgrn, cache_swa_k, cache_swa_v, ln_mix, w_in, lb_logits, hg_norm, sinks,
           w_out, ln_mlp, w_up, w_down, ln_final):
    f = lambda a: np.ascontiguousarray(np.asarray(a), dtype=np.float32)
    x_prompt, x_sample, state_hgrn = f(x_prompt), f(x_sample), f(state_hgrn)
    cache_swa_k, cache_swa_v = f(cache_swa_k), f(cache_swa_v)
    wall = _build_wall(f(w_in)[0], f(w_out)[0], f(w_up)[0], f(w_down)[0])
    lnmix = np.ascontiguousarray(f(ln_mix)[0].reshape(8, 128).T)
    lnmlp = np.ascontiguousarray(f(ln_mlp)[0].reshape(8, 128).T)
    lbl = np.ascontiguousarray(f(lb_logits).reshape(2, 4, 128).transpose(2, 0, 1).reshape(128, 8))
    hgn = np.ascontiguousarray(f(hg_norm)[0].reshape(128, 1))
    sk_ = f(sinks)[0]
    sinkT = np.ascontiguousarray(np.stack([sk_[(p // 64) * 4:(p // 64) * 4 + 4] for p in range(128)], axis=0))
    gfin = np.ascontiguousarray(np.broadcast_to(f(ln_final)[None, :], (128, D)))
    in_maps = []
    for c in range(NCORES):
        in_maps.append({
            "x": x_prompt[c],
            "xs": np.ascontiguousarray(x_sample[16 * c:16 * (c + 1)].reshape(128, D)),
            "st": state_hgrn[0, 16 * c:16 * (c + 1)],
            "ck": np.ascontiguousarray(cache_swa_k[0, 16 * c:16 * (c + 1)].reshape(16, 128, 128)),
            "cv": np.ascontiguousarray(cache_swa_v[0, 16 * c:16 * (c + 1)].reshape(16, 128, 128)),
            "wall": wall, "lnmix": lnmix, "lnmlp": lnmlp, "lbl": lbl, "hgn": hgn, "sinkT": sinkT, "gfin": gfin,
        })
    return in_maps


def kernel(x_prompt, x_sample, state_hgrn, cache_swa_k, cache_swa_v, ln_mix, w_in, lb_logits, hg_norm, sinks,
           w_out, ln_mlp, w_up, w_down, ln_final):
    in_maps = _make_in_maps(x_prompt, x_sample, state_hgrn, cache_swa_k, cache_swa_v, ln_mix, w_in, lb_logits, hg_norm,
                            sinks, w_out, ln_mlp, w_up, w_down, ln_final)
    _, order = build_program()
    nc, _ = build_program(worder=order)
    res = run_bass_kernel_spmd(nc, in_maps, core_ids=list(range(NCORES)))
    R = res.results
    y_prompt = np.stack([R[c]["y"] for c in range(NCORES)], axis=0).reshape(8, 4096, D)
    y_sample = np.concatenate([R[c]["ys"].reshape(16, 8, D) for c in range(NCORES)], axis=0)
    sp = np.stack([R[c]["sp"] for c in range(NCORES)], axis=0)[None]
    kp = np.stack([R[c]["kp"].reshape(128, 2, 64) for c in range(NCORES)], axis=0)[None]
    vp = np.stack([R[c]["vp"].reshape(128, 2, 64) for c in range(NCORES)], axis=0)[None]
    ss = np.concatenate([R[c]["ss"] for c in range(NCORES)], axis=0)[None]
    ks = np.concatenate([R[c]["ks"].reshape(16, 128, 2, 64) for c in range(NCORES)], axis=0)[None]
    vs = np.concatenate([R[c]["vs"].reshape(16, 128, 2, 64) for c in range(NCORES)], axis=0)[None]
    out = (y_prompt, y_sample, sp, kp, vp, ss, ks, vs)
    return tuple(np.ascontiguousarray(o, dtype=np.float32) for o in out)
```

```python
import os
import numpy as np
import concourse.bass as bass
import concourse.mybir as mybir
from concourse.bass_utils import run_bass_kernel_spmd

F32 = mybir.dt.float32
BF16 = mybir.dt.bfloat16
AF = mybir.ActivationFunctionType
ALU = mybir.AluOpType

NCORES = 8
D = 1024
NT = 32
G = 4
NG = NT // G
NSLOT = 6
NCH = 24
EPS = 1e-6


class Buf:
    __slots__ = ("ap", "w", "r")

    def __init__(self, ap):
        self.ap = ap
        self.w = None
        self.r = {}

    def __getitem__(self, k):
        return self.ap[k]


class Eng:
    def __init__(self, h, semname):
        self.h = h
        self.semname = semname
        self.n = 0
        self.waited = {}


class Prog:
    def __init__(self, nc):
        self.nc = nc
        self.sems = {}
        self.dcnt = {}
        self.E = {}
        for name, h in (("pe", nc.tensor), ("act", nc.scalar), ("dve", nc.vector),
                        ("pool", nc.gpsimd), ("sp", nc.sync)):
            self.sems["e_" + name] = nc.alloc_semaphore("e_" + name)
            self.E[name] = Eng(h, "e_" + name)

    def dsem(self, name):
        if name not in self.sems:
            self.sems[name] = self.nc.alloc_semaphore(name)
            self.dcnt[name] = 0
        return name

    def _wait(self, e, tok):
        if tok is None:
            return
        sn, v = tok
        if e is self.E["pe"] and sn == "e_pe":
            return
        if e.waited.get(sn, 0) >= v:
            return
        e.h.wait_ge(self.sems[sn], v)
        e.waited[sn] = v

    def _deps(self, e, reads, writes):
        for b in reads:
            self._wait(e, b.w)
        for b in writes:
            self._wait(e, b.w)
            for sn, v in b.r.items():
                self._wait(e, (sn, v))

    @staticmethod
    def _mark(tok, reads, writes):
        sn, v = tok
        for b in reads:
            if b.r.get(sn, 0) < v:
                b.r[sn] = v
        for b in writes:
            b.w = tok
            b.r = {}

    def op(self, en, fns, reads=(), writes=()):
        e = self.E[en]
        if callable(fns):
            fns = [fns]
        self._deps(e, reads, writes)
        ins = None
        for f in fns:
            ins = f(e.h)
        ins.then_inc(self.sems[e.semname], 1)
        e.n += 1
        tok = (e.semname, e.n)
        self._mark(tok, reads, writes)
        return tok

    def dma(self, out, in_, reads=(), writes=(), sem=None):
        e = self.E["sp"]
        self.dsem(sem)
        self._deps(e, reads, writes)
        ins = e.h.dma_start(out=out, in_=in_)
        ins.then_inc(self.sems[sem], 16)
        self.dcnt[sem] += 16
        tok = (sem, self.dcnt[sem])
        self._mark(tok, reads, writes)
        return tok

    def wait_engines(self, waiters, targets):
        for wn in waiters:
            e = self.E[wn]
            for tn in targets:
                t = self.E[tn]
                if t.n > 0:
                    self._wait(e, (t.semname, t.n))


class Carver:
    def __init__(self, pool_ap, nbytes):
        self.pool = pool_ap
        self.nbytes = nbytes
        self.off = 0

    def take(self, free_shape, dt):
        n = int(np.prod(free_shape))
        nb = n * (4 if dt == F32 else 2)
        s = self.off // 2
        assert self.off + nb <= self.nbytes, ("carve overflow", self.off, nb, self.nbytes)
        ap = self.pool[:, s:s + nb // 2]
        if dt == F32:
            ap = ap.bitcast(F32)
        if len(free_shape) > 1:
            names = "abcd"[:len(free_shape)]
            ap = ap.rearrange("p (%s) -> p %s" % (" ".join(names), " ".join(names)),
                              **{k: int(v) for k, v in zip(names, free_shape)})
        self.off += (nb + 31) // 32 * 32
        return Buf(ap)


class NS:
    pass


class _Stop(Exception):
    pass


def build_program(worder=None, stop=None, dumps=()):
    record = worder is None
    WORDER = [] if record else list(worder)
    nc = bass.Bass("TRN2", target_bir_lowering=False)
    P = Prog(nc)
    REG = {}

    def chk(stage):
        if stop == stage:
            raise _Stop()

    def din(name, shape, dt=F32):
        return nc.dram_tensor(name, list(shape), dt, kind="ExternalInput").ap()

    def dout(name, shape, dt=F32):
        return nc.dram_tensor(name, list(shape), dt, kind="ExternalOutput").ap()

    x_d = din("x", [NT * 128, D])
    xs_d = din("xs", [128, D])
    st_d = din("st", [16, 4, 128, 128])
    ck_d = din("ck", [16, 128, 128])
    cv_d = din("cv", [16, 128, 128])
    wall_d = din("wall", [NCH, 128, 4096])
    lnmix_d = din("lnmix", [128, 8])
    lnmlp_d = din("lnmlp", [128, 8])
    lbl_d = din("lbl", [128, 8])
    hgn_d = din("hgn", [128, 1])
    sinkT_d = din("sinkT", [128, 4])
    gfin_d = din("gfin", [128, D])
    y_d = dout("y", [NT * 128, D])
    ys_d = dout("ys", [128, D])
    sp_d = dout("sp", [4, 128, 128])
    kp_d = dout("kp", [128, 128])
    vp_d = dout("vp", [128, 128])
    ss_d = dout("ss", [16, 4, 128, 128])
    ks_d = dout("ks", [16, 128, 128])
    vs_d = dout("vs", [16, 128, 128])
    wbf_d = nc.dram_tensor("wbf", [NCH, 128, 4096], BF16).ap()

    final_toks = []

    def wait_final():
        mx = {}
        for sn_, v_ in final_toks:
            mx[sn_] = max(mx.get(sn_, 0), v_)
        for sn_, v_ in mx.items():
            P._wait(P.E["sp"], (sn_, v_))

    def sb(name, free_shape, dt):
        return Buf(nc.alloc_sbuf_tensor("s_" + name, [128] + list(free_shape), dt)[:])

    C = NS()
    C.ident = sb("ident", [128], BF16)
    C.onesm = sb("onesm", [128], BF16)
    C.ones64 = sb("ones64", [64], BF16)
    C.mhalf = sb("mhalf", [1], F32)
    C.epsc = sb("epsc", [1], F32)
    C.cmask = sb("cmask", [512], F32)
    C.smask = sb("smask", [512], F32)
    C.swm_cur = sb("swm_cur", [512], BF16)
    C.swm_prev = sb("swm_prev", [512], BF16)
    C.seqmask = sb("seqmask", [16], F32)
    C.gfin = sb("gfin", [D], F32)
    C.lnmix = sb("lnmix", [8], F32)
    C.lnmlp = sb("lnmlp", [8], F32)
    C.wos = sb("wos", [8], F32)
    C.lbl = sb("lbl", [8], F32)
    C.hgn = sb("hgn", [1], F32)
    C.sinkT = sb("sinkT", [4], F32)
    C.esinkT = sb("esinkT", [4], F32)
    C.dl = sb("dl", [4], F32)
    C.lb = sb("lb", [4], F32)
    C.oml = sb("oml", [4], F32)
    C.noml = sb("noml", [4], F32)

    NXS = 8
    XR = nc.alloc_sbuf_tensor("XR", [128, NXS * 2048], BF16)
    xres = [Buf(XR[:, i * 2048:(i + 1) * 2048].bitcast(F32)) for i in range(NXS)]
    xnb = [sb("xnb%d" % i, [D], BF16) for i in range(3)]
    xnb_free = [0, 1, 2]
    junk = sb("junk", [D], BF16)
    NST = 6
    stat = [(sb("ssq%d" % i, [1], F32), sb("tv%d" % i, [1], F32), sb("rs%d" % i, [1], F32)) for i in range(NST)]
    xnT = sb("xnT", [8, 512], BF16)
    hnT = sb("hnT", [8, 512], BF16)
    NKV = 8
    kT_all = sb("kT_all", [NKV * 128], BF16)
    vS_all = sb("vS_all", [NKV, 128], BF16)
    Sall = nc.alloc_sbuf_tensor("Sst", [128, 512], F32)
    S = [Buf(Sall[:, h * 128:(h + 1) * 128]) for h in range(4)]
    S_bf = sb("S_bf", [512], BF16)
    omT = sb("omT", [8, 512], BF16)
    aT = sb("aT", [8, 512], BF16)
    wring = [sb("wr%d" % i, [8, 512], BF16) for i in range(NSLOT)]
    kvout_p = sb("kvout", [256], F32)
    rl_p = [sb("rl%d" % i, [512], BF16) for i in range(2)]
    stg32 = [sb("stg%d" % i, [2, 512], F32) for i in range(4)]
    UBYTES = 50 * 1024
    U = nc.alloc_sbuf_tensor("U", [128, UBYTES // 2], BF16)

    banks = [Buf(nc.alloc_psum_tensor("pb%d" % i, [128, 512], F32)[:]) for i in range(8)]
    pfreeq = list(range(8))

    def psum():
        assert pfreeq, "out of PSUM banks"
        return banks[pfreeq.pop(0)]

    def pfree(b):
        for i, bb in enumerate(banks):
            if bb is b:
                assert i not in pfreeq
                pfreeq.append(i)
                return
        raise AssertionError("not a bank")

    def pool1(fn, writes, reads=()):
        return P.op("pool", fn, reads=reads, writes=writes)

    cl = [(C.lnmix, lnmix_d), (C.lnmlp, lnmlp_d), (C.lbl, lbl_d), (C.hgn, hgn_d), (C.sinkT, sinkT_d),
          (C.gfin, gfin_d)]
    for b, d_ in cl:
        P.dma(b.ap, d_, writes=[b], sem="cst")
    ctok = ("cst", P.dcnt["cst"])
    for b, _ in cl:
        b.w = ctok

    pool1(lambda e: e.memset(C.ident[:, :], 0.0), [C.ident])
    pool1(lambda e: e.affine_select(out=C.ident[:, :], in_=C.ident[:, :], pattern=[[-1, 128]], base=0,
                                    channel_multiplier=1, compare_op=ALU.not_equal, fill=1.0), [C.ident])
    pool1(lambda e: e.memset(C.onesm[:, :], 1.0 / 128.0), [C.onesm])
    pool1(lambda e: e.memset(C.ones64[:, :], 1.0), [C.ones64])
    pool1(lambda e: e.memset(C.mhalf[:, :], -0.5), [C.mhalf])
    pool1(lambda e: e.memset(C.epsc[:, :], EPS), [C.epsc])
    pool1(lambda e: e.memset(C.cmask[:, :], 1.0), [C.cmask])
    pool1(lambda e: e.affine_select(out=C.cmask[:, :], in_=C.cmask[:, :], pattern=[[0, 4], [1, 128]], base=0,
                                    channel_multiplier=-1, compare_op=ALU.is_ge, fill=0.0), [C.cmask])
    pool1(lambda e: e.memset(C.smask[:, :], 1.0), [C.smask])
    pool1(lambda e: e.memset(C.smask[:, :].rearrange("p (c t) -> p c t", t=128)[:, :, 0:1], 0.0), [C.smask])
    pool1(lambda e: e.memset(C.swm_cur[:, :], 1.0), [C.swm_cur])
    pool1(lambda e: e.affine_select(out=C.swm_cur[:, :], in_=C.swm_cur[:, :], pattern=[[0, 4], [1, 128]], base=0,
                                    channel_multiplier=-1, compare_op=ALU.is_ge, fill=0.0), [C.swm_cur])
    pool1(lambda e: e.memset(C.swm_prev[:, :], 1.0), [C.swm_prev])
    pool1(lambda e: e.affine_select(out=C.swm_prev[:, :], in_=C.swm_prev[:, :], pattern=[[0, 4], [-1, 128]], base=-1,
                                    channel_multiplier=1, compare_op=ALU.is_ge, fill=0.0), [C.swm_prev])
    pool1(lambda e: e.memset(C.seqmask[:, :], 1.0), [C.seqmask])
    pool1(lambda e: e.affine_select(out=C.seqmask[:, :], in_=C.seqmask[:, :], pattern=[[-8, 16]], base=0,
                                    channel_multiplier=1, compare_op=ALU.is_ge, fill=0.0), [C.seqmask])
    pool1(lambda e: e.affine_select(out=C.seqmask[:, :], in_=C.seqmask[:, :], pattern=[[8, 16]], base=7,
                                    channel_multiplier=-1, compare_op=ALU.is_ge, fill=0.0), [C.seqmask])
    pool1(lambda e: e.memset(C.wos[:, :], 1.0), [C.wos])
    for c in range(4):
        P.op("dve", lambda e, c=c: e.tensor_copy(out=C.wos[:, c:c + 1], in_=C.hgn[:, 0:1]), reads=[C.hgn], writes=[C.wos])
    P.op("dve", lambda e: e.tensor_tensor(out=C.dl[:, :], in0=C.lbl[:, 0:4], in1=C.lbl[:, 4:8], op=ALU.subtract),
         reads=[C.lbl], writes=[C.dl])
    P.op("act", lambda e: e.activation(out=C.lb[:, :], in_=C.dl[:, :], func=AF.Sigmoid), reads=[C.dl], writes=[C.lb])
    P.op("dve", lambda e: e.tensor_scalar(out=C.oml[:, :], in0=C.lb[:, :], scalar1=-1.0, scalar2=1.0, op0=ALU.mult,
                                          op1=ALU.add), reads=[C.lb], writes=[C.oml])
    P.op("dve", lambda e: e.tensor_scalar(out=C.noml[:, :], in0=C.lb[:, :], scalar1=-1.0, scalar2=None, op0=ALU.add),
         reads=[C.lb], writes=[C.noml])
    P.op("act", lambda e: e.activation(out=C.esinkT[:, :], in_=C.sinkT[:, :], func=AF.Exp), reads=[C.sinkT],
         writes=[C.esinkT])
    for h in range(4):
        pool1(lambda e, h=h: e.memset(S[h][:, :], 0.0), [S[h]])
    pool1(lambda e: e.memset(S_bf[:, :], 0.0), [S_bf])

    wslot_free = list(range(NSLOT))
    loaded = {}
    wr = {"next": 0, "use": 0}

    converted = set()
    wbfb = [Buf(None) for _ in range(NCH)]
    pending = []
    qc = {"i": 0}

    def flush_casts():
        while pending:
            cid, s_, qs = pending.pop(0)
            slot = wring[s_]
            sc = C.lnmix if cid < 6 else (C.wos if cid < 8 else (C.lnmlp if cid < 16 else None))
            for qi, st in enumerate(qs):
                en = ("act", "dve")[qi % 2]
                if sc is not None:
                    if en == "act":
                        fns = [lambda e, k=k, st=st: e.activation(out=slot[:, 2 * qi + k, :], in_=st[:, k, :],
                                                                  func=AF.Identity, scale=sc[:, 2 * qi + k:2 * qi + k + 1])
                               for k in range(2)]
                    else:
                        fns = [lambda e, k=k, st=st: e.tensor_scalar(out=slot[:, 2 * qi + k, :], in0=st[:, k, :],
                                                                     scalar1=sc[:, 2 * qi + k:2 * qi + k + 1], scalar2=None,
                                                                     op0=ALU.mult) for k in range(2)]
                    P.op(en, fns, reads=[st, sc], writes=[slot])
                else:
                    if en == "act":
                        P.op(en, lambda e, st=st: e.copy(out=slot[:, 2 * qi:2 * qi + 2, :], in_=st[:, :, :]), reads=[st],
                             writes=[slot])
                    else:
                        P.op(en, lambda e, st=st: e.tensor_copy(out=slot[:, 2 * qi:2 * qi + 2, :], in_=st[:, :, :]),
                             reads=[st], writes=[slot])
            P.dma(wbf_d[cid], slot.ap.rearrange("p a b -> p (a b)"), reads=[slot], writes=[wbfb[cid]], sem="ws%d" % s_)

    def w_load(l):
        cid = WORDER[l]
        s_ = wslot_free.pop(0)
        if cid in converted:
            P.dma(wring[s_].ap.rearrange("p a b -> p (a b)"), wbf_d[cid], reads=[wbfb[cid]], writes=[wring[s_]],
                  sem="wr%d" % s_)
        else:
            flush_casts()
            converted.add(cid)
            qs = []
            for qi in range(4):
                st = stg32[qc["i"] % 4]
                qc["i"] += 1
                P.dma(st.ap.rearrange("p a b -> p (a b)"), wall_d[cid][:, qi * 1024:(qi + 1) * 1024], writes=[st],
                      sem="pl%d" % (qc["i"] % 4))
                qs.append(st)
            pending.append((cid, s_, qs))
        loaded[l] = s_

    def w_prefetch():
        if record:
            return
        while wslot_free and wr["next"] < len(WORDER):
            w_load(wr["next"])
            wr["next"] += 1

    def w_use(cid):
        flush_casts()
        u = wr["use"]
        wr["use"] += 1
        if record:
            WORDER.append(cid)
        else:
            assert WORDER[u] == cid, (u, cid, WORDER[u])
        if u not in loaded:
            assert wr["next"] == u
            w_load(u)
            wr["next"] += 1
        flush_casts()
        w_prefetch()
        return wring[loaded[u]], u

    def w_release(u):
        wslot_free.append(loaded.pop(u))
        w_prefetch()

    def carve(TB, sample):
        cvr = Carver(U[:, :], UBYTES)
        B = NS()
        B.TB = TB
        nt = TB // 128
        B.sg4 = [cvr.take([TB], F32) for _ in range(2)]
        B.gt = cvr.take([TB], F32)
        B.kf = cvr.take([TB], F32)
        B.bt = cvr.take([TB], F32)
        B.Eb = cvr.take([TB], F32)
        B.Enb = cvr.take([TB], F32)
        B.qtT = cvr.take([4, TB], BF16)
        B.ktT = cvr.take([4, TB], BF16)
        B.sgT = cvr.take([4, TB], BF16)
        B.qsT = cvr.take([4, TB], BF16)
        B.v_tok = cvr.take([nt, 512], BF16)
        B.dec = cvr.take([64], F32)
        B.ATm = cvr.take([512], BF16)
        B.kt_tok = cvr.take([512], BF16)
        pd = cvr.take([512], F32)
        B.Pd = [Buf(pd[:, h * 128:(h + 1) * 128]) for h in range(4)]
        B.sqn = cvr.take([512], BF16)
        B.lnv = cvr.take([512], F32)
        B.rstdn = B.lnv
        B.t1 = cvr.take([512], BF16)
        B.PT = [cvr.take([512], BF16) for _ in range(4)]
        B.dsum = cvr.take([512], F32)
        B.rden = B.dsum
        B.rl = rl_p
        B.kvout = kvout_p
        if sample:
            B.cmask_s = cvr.take([512], F32)
            B.smask_s = cvr.take([128], F32)
            B.swm_scur = cvr.take([512], BF16)
            B.swm_c = cvr.take([512], BF16)
            B.ckb = cvr.take([16, 128], BF16)
            B.cvb = cvr.take([16, 128], BF16)
            B.kcT = cvr.take([16, 128], BF16)
            B.ktm = [cvr.take([512], BF16) for _ in range(2)]
            B.S0b = [cvr.take([512], BF16) for _ in range(2)]
            xc = Carver(XR[:, 4 * 2048:7 * 2048], 3 * 4096)
            B.ck32 = [xc.take([4, 128], F32)] * 2
            B.cv32 = [cvr.take([4, 128], F32)] * 2
            B.S0 = [xc.take([512], F32) for _ in range(3)]
            B.Sn = []
            B.SnT = []
            for _ in range(2):
                t_ = xc.take([512], F32)
                B.Sn.append([Buf(t_[:, h * 128:(h + 1) * 128]) for h in range(4)])
                B.SnT.append(t_.ap.rearrange("p (h v) -> p h v", h=4))
        return B

    stc = {"i": 0, "x": 0, "y": 0}

    def norm_stats(src):
        ssq, tv, rs = stat[stc["i"] % NST]
        stc["i"] += 1
        P.op("act", lambda e: e.activation(out=junk[:, :], in_=src[:, :], func=AF.Square, accum_out=ssq[:, 0:1]),
             reads=[src], writes=[ssq, junk])
        P.op("act", lambda e: e.activation(out=tv[:, :], in_=ssq[:, :], func=AF.Ln, scale=1.0 / D, bias=C.epsc[:, 0:1]),
             reads=[ssq, C.epsc], writes=[tv])
        P.op("act", lambda e: e.activation(out=rs[:, :], in_=tv[:, :], func=AF.Exp, scale=-0.5), reads=[tv], writes=[rs])
        return rs

    def norm_pre(src):
        rs = norm_stats(src)
        assert xnb_free, "xnb ring exhausted"
        xb = xnb[xnb_free.pop(0)]
        P.op("dve", lambda e: e.tensor_scalar(out=xb[:, :], in0=src[:, :], scalar1=rs[:, 0:1], scalar2=None,
                                              op0=ALU.mult), reads=[src, rs], writes=[xb])
        return xb

    def norm_post(xb, dstT, j):
        bk = psum()
        bkb = bk.ap.bitcast(BF16)
        P.op("pe", [lambda e, k=k: e.transpose(bkb[:, k * 128:(k + 1) * 128], xb[:, k * 128:(k + 1) * 128], C.ident[:, :])
                    for k in range(8)], reads=[xb, C.ident], writes=[bk])
        P.op("dve", lambda e: e.tensor_copy(out=dstT[:, 0:8, j * 128:(j + 1) * 128],
                                            in_=bkb.rearrange("p (a b) -> p a b", a=8)), reads=[bk], writes=[dstT])
        pfree(bk)
        for i_, b_ in enumerate(xnb):
            if b_ is xb:
                xnb_free.append(i_)

    def fm_mm(W, col0, srcT, TB, ncols=128):
        bk = psum()
        P.op("pe", [lambda e, k=k: e.matmul(bk[0:ncols, 0:TB], lhsT=W[:, k, col0:col0 + ncols], rhs=srcT[:, k, 0:TB],
                                            start=(k == 0), stop=(k == 7)) for k in range(8)],
             reads=[W, srcT], writes=[bk])
        return bk

    def tm_mm(W, ncols, srcT, j):
        bk = psum()
        P.op("pe", [lambda e, k=k: e.matmul(bk[:, 0:ncols], lhsT=srcT[:, k, j * 128:(j + 1) * 128], rhs=W[:, k, 0:ncols],
                                            start=(k == 0), stop=(k == 7)) for k in range(8)],
             reads=[W, srcT], writes=[bk])
        return bk

    def head12_steps(B, tiles, slots, sample):
        TB = B.TB
        nt = TB // 128
        xb_prev = None
        for j in range(nt):
            xb = norm_pre(slots[j])
            if xb_prev is not None:
                norm_post(xb_prev, xnT, j - 1)
            xb_prev = xb
            yield
        norm_post(xb_prev, xnT, nt - 1)
        yield
        W, u = w_use(0)
        for h in range(4):
            bk = fm_mm(W, h * 128, xnT, TB)
            P.op("act", lambda e, h=h, bk=bk: e.activation(out=B.qtT[:, h, :], in_=bk[:, 0:TB], func=AF.Silu),
                 reads=[bk], writes=[B.qtT])
            pfree(bk)
            yield
        w_release(u)
        W, u = w_use(1)
        for h in range(4):
            bk = fm_mm(W, h * 128, xnT, TB)
            P.op("act", lambda e, h=h, bk=bk: e.activation(out=B.sgT[:, h, :], in_=bk[:, 0:TB], func=AF.Silu),
                 reads=[bk], writes=[B.sgT])
            pfree(bk)
            yield
        w_release(u)
        W, u = w_use(2)
        sm = B.smask_s if sample else C.smask
        for hp in range(2):
            for hh in range(2):
                h = 2 * hp + hh
                bk = fm_mm(W, h * 128, xnT, TB)
                P.op("act", lambda e, hh=hh, bk=bk: e.activation(out=B.sg4[hh][:, :], in_=bk[:, 0:TB], func=AF.Sigmoid),
                     reads=[bk], writes=[B.sg4[hh]])
                pfree(bk)
                yield
            if hp == 1:
                w_release(u)
            for hh in range(2):
                h = 2 * hp + hh
                sg = B.sg4[hh]
                P.op("act", lambda e, h=h, sg=sg: e.activation(out=B.gt[:, :], in_=sg[:, :], func=AF.Ln,
                                                               scale=C.oml[:, h:h + 1], bias=C.lb[:, h:h + 1]),
                     reads=[sg, C.oml, C.lb], writes=[B.gt])
                P.op("dve", lambda e, h=h, sg=sg: e.tensor_scalar(out=B.kf[:, :], in0=sg[:, :], scalar1=C.noml[:, h:h + 1],
                                                                  scalar2=C.oml[:, h:h + 1], op0=ALU.mult, op1=ALU.add),
                     reads=[sg, C.noml, C.oml], writes=[B.kf])
                P.op("dve", lambda e: e.tensor_tensor_scan(out=B.bt[:, :], data0=sm[:, 0:TB], data1=B.gt[:, :],
                                                           initial=0.0, op0=ALU.mult, op1=ALU.add), reads=[sm, B.gt],
                     writes=[B.bt])
                P.op("act", lambda e: e.activation(out=B.Eb[:, :], in_=B.bt[:, :], func=AF.Exp), reads=[B.bt],
                     writes=[B.Eb])
                P.op("act", lambda e: e.activation(out=B.Enb[:, :], in_=B.bt[:, :], func=AF.Exp, scale=-1.0),
                     reads=[B.bt], writes=[B.Enb])
                P.op("dve", lambda e, h=h: e.tensor_tensor(out=B.ktT[:, h, :], in0=B.kf[:, :], in1=B.Enb[:, :],
                                                           op=ALU.mult), reads=[B.kf, B.Enb], writes=[B.ktT])
                P.op("dve", lambda e, h=h: e.tensor_tensor(out=B.qtT[:, h, :], in0=B.qtT[:, h, :], in1=B.Eb[:, :],
                                                           op=ALU.mult), reads=[B.Eb], writes=[B.qtT])
                if sample:
                    P.op("act", lambda e, h=h: e.copy(out=B.dec[:, h * 16:(h + 1) * 16],
                                                              in_=B.Eb[:, :].rearrange("p (s i) -> p s i", i=8)[:, :, 7]),
                         reads=[B.Eb], writes=[B.dec])
                else:
                    P.op("act", lambda e, h=h: e.copy(
                        out=B.dec[:, h * nt:(h + 1) * nt],
                        in_=B.Eb[:, :].rearrange("p (j t) -> p j t", t=128)[:, :, 127]), reads=[B.Eb], writes=[B.dec])
                yield
        W, u = w_use(3)
        for g in range(4):
            bk = fm_mm(W, g * 128, xnT, TB)
            P.op("dve", lambda e, g=g, bk=bk: e.tensor_copy(out=B.qsT[:, g, :], in_=bk[:, 0:TB]), reads=[bk],
                 writes=[B.qsT])
            pfree(bk)
            yield
        w_release(u)
        W4, u = w_use(4)
        for j, n in enumerate(tiles):
            bk = tm_mm(W4, 512, xnT, j)
            P.op("act", lambda e, j=j, bk=bk: e.copy(out=B.v_tok[:, j, :], in_=bk[:, :]), reads=[bk], writes=[B.v_tok])
            pfree(bk)
            yield
        w_release(u)
        W5, u = w_use(5)
        bk = fm_mm(W5, 0, xnT, TB)
        s0_ = tiles[0] % NKV
        P.op("act", lambda e, bk=bk: e.copy(out=kT_all[:, s0_ * 128:s0_ * 128 + TB], in_=bk[:, 0:TB]), reads=[bk],
             writes=[kT_all])
        pfree(bk)
        yield
        for j, n in enumerate(tiles):
            bk = tm_mm(W5, 256, xnT, j)
            ns = n % NKV
            if sample or n == NT - 1:
                P.op("act", lambda e, bk=bk: e.copy(out=B.kvout[:, :], in_=bk[:, 0:256]), reads=[bk], writes=[B.kvout])
                P.op("dve", lambda e, ns=ns: e.tensor_copy(out=vS_all[:, ns, :], in_=B.kvout[:, 128:256]),
                     reads=[B.kvout], writes=[vS_all])
                if sample:
                    final_toks.append(P.dma(ks_d[:, 120:128, :], B.kvout[:, 0:128], reads=[B.kvout], sem="kvo"))
                    final_toks.append(P.dma(vs_d[:, 120:128, :], B.kvout[:, 128:256], reads=[B.kvout], sem="kvo"))
                else:
                    final_toks.append(P.dma(kp_d, B.kvout[:, 0:128], reads=[B.kvout], sem="kvo"))
                    final_toks.append(P.dma(vp_d, B.kvout[:, 128:256], reads=[B.kvout], sem="kvo"))
            else:
                P.op("dve", lambda e, ns=ns, bk=bk: e.tensor_copy(out=vS_all[:, ns, :], in_=bk[:, 128:256]), reads=[bk],
                     writes=[vS_all])
            pfree(bk)
            yield
        w_release(u)

    def hg_front(B, j, cmask):
        cs = slice(j * 128, (j + 1) * 128)
        bkA = psum()
        P.op("pe", [lambda e, h=h: e.matmul(bkA[:, h * 128:(h + 1) * 128], lhsT=B.ktT[:, h, cs], rhs=B.qtT[:, h, cs],
                                            start=True, stop=True) for h in range(4)],
             reads=[B.ktT, B.qtT], writes=[bkA])
        P.op("dve", lambda e: e.tensor_tensor(out=B.ATm[:, :], in0=bkA[:, :], in1=cmask[:, :], op=ALU.mult),
             reads=[bkA, cmask], writes=[B.ATm])
        pfree(bkA)
        bkT = psum()
        bkTb = bkT.ap.bitcast(BF16)
        P.op("pe", [lambda e, h=h: e.transpose(bkTb[:, h * 128:(h + 1) * 128], B.ktT[:, h, cs], C.ident[:, :])
                    for h in range(4)], reads=[B.ktT, C.ident], writes=[bkT])
        P.op("act", lambda e: e.copy(out=B.kt_tok[:, :], in_=bkTb[:, 0:512]), reads=[bkT], writes=[B.kt_tok])
        pfree(bkT)

    def hg_square(B, oaps, obufs):
        ub = list({id(b): b for b in obufs}.values())
        if len(ub) == 1:
            P.op("act", lambda e: e.activation(out=B.sqn[:, :], in_=ub[0][:, :], func=AF.Square), reads=ub,
                 writes=[B.sqn])
        else:
            P.op("act", [lambda e, h=h: e.activation(out=B.sqn[:, h * 128:(h + 1) * 128], in_=oaps[h], func=AF.Square)
                         for h in range(4)], reads=ub, writes=[B.sqn])

    def hg_norm_out(B, j, oaps, obufs):
        cs = slice(j * 128, (j + 1) * 128)
        ub = list({id(b): b for b in obufs}.values())
        bkM = psum()
        P.op("pe", lambda e: e.matmul(bkM[:, :], lhsT=C.onesm[:, :], rhs=B.sqn[:, :], start=True, stop=True),
             reads=[C.onesm, B.sqn], writes=[bkM])
        P.op("act", lambda e: e.activation(out=B.lnv[:, :], in_=bkM[:, :], func=AF.Ln, bias=C.epsc[:, 0:1]),
             reads=[bkM, C.epsc], writes=[B.lnv])
        pfree(bkM)
        P.op("act", lambda e: e.activation(out=B.rstdn[:, :], in_=B.lnv[:, :], func=AF.Exp, scale=-0.5),
             reads=[B.lnv], writes=[B.rstdn])
        if len(ub) == 1:
            P.op("dve", lambda e: e.tensor_tensor(out=B.t1[:, :], in0=ub[0][:, :], in1=B.rstdn[:, :], op=ALU.mult),
                 reads=ub + [B.rstdn], writes=[B.t1])
        else:
            P.op("dve", [lambda e, h=h: e.tensor_tensor(out=B.t1[:, h * 128:(h + 1) * 128], in0=oaps[h],
                                                        in1=B.rstdn[:, h * 128:(h + 1) * 128], op=ALU.mult)
                         for h in range(4)], reads=ub + [B.rstdn], writes=[B.t1])
        P.op("dve", lambda e: e.tensor_tensor(out=omT[:, 0:4, cs], in0=B.t1[:, :].rearrange("p (a b) -> p a b", a=4),
                                              in1=B.sgT[:, 0:4, cs], op=ALU.mult), reads=[B.t1, B.sgT], writes=[omT])
        for b in ub:
            pfree(b)

    def hg_state(B, j):
        nt = B.TB // 128
        cs = slice(j * 128, (j + 1) * 128)
        bkO = psum()
        fns = []
        for h in range(4):
            hs = slice(h * 128, (h + 1) * 128)
            fns.append(lambda e, h=h, hs=hs: e.matmul(bkO[:, hs], lhsT=B.v_tok[:, j, hs], rhs=B.ATm[:, hs], start=True,
                                                      stop=False))
            fns.append(lambda e, h=h, hs=hs: e.matmul(bkO[:, hs], lhsT=S_bf[:, hs], rhs=B.qtT[:, h, cs], start=False,
                                                      stop=True))
        P.op("pe", fns, reads=[B.v_tok, B.ATm, S_bf, B.qtT], writes=[bkO])
        bkP = psum()
        P.op("pe", [lambda e, hs=slice(h * 128, (h + 1) * 128): e.matmul(bkP[:, hs], lhsT=B.kt_tok[:, hs],
                                                                         rhs=B.v_tok[:, j, hs], start=True, stop=True)
                    for h in range(4)], reads=[B.kt_tok, B.v_tok], writes=[bkP])
        for h in range(4):
            hs = slice(h * 128, (h + 1) * 128)
            dc = B.dec[:, h * nt + j:h * nt + j + 1]
            P.op("act", lambda e, h=h, hs=hs, dc=dc: e.activation(out=B.Pd[h][:, :], in_=bkP[:, hs], func=AF.Identity,
                                                                  scale=dc), reads=[bkP, B.dec], writes=[B.Pd[h]])
            P.op("dve", lambda e, h=h, dc=dc: e.scalar_tensor_tensor(out=S[h][:, :], in0=S[h][:, :], scalar=dc,
                                                                     in1=B.Pd[h][:, :], op0=ALU.mult, op1=ALU.add),
                 reads=[B.Pd[h], B.dec], writes=[S[h]])
        pfree(bkP)
        P.op("act", lambda e: e.copy(out=S_bf[:, :], in_=Sall[:, :]), reads=S, writes=[S_bf])
        hg_square(B, None, [bkO])
        return bkO

    def swa_finish(B, j, bkO, bkD, sample=False):
        cs = slice(j * 128, (j + 1) * 128)
        if sample:
            dv = B.dsum[:, :].rearrange("p (s g i) -> p s g i", s=16, g=4)
            bv = bkD[:, :].rearrange("p (s g i) -> p s g i", s=16, g=4)
            P.op("dve", [lambda e, g=g: e.tensor_scalar(out=dv[:, :, g, :], in0=bv[:, :, g, :],
                                                        scalar1=C.esinkT[:, g:g + 1], scalar2=None, op0=ALU.add)
                         for g in range(4)], reads=[bkD, C.esinkT], writes=[B.dsum])
        else:
            P.op("dve", [lambda e, g=g: e.tensor_scalar(out=B.dsum[:, g * 128:(g + 1) * 128],
                                                        in0=bkD[:, g * 128:(g + 1) * 128], scalar1=C.esinkT[:, g:g + 1],
                                                        scalar2=None, op0=ALU.add) for g in range(4)],
                 reads=[bkD, C.esinkT], writes=[B.dsum])
        pfree(bkD)
        P.op("dve", lambda e: e.reciprocal(out=B.rden[:, :], in_=B.dsum[:, :]), reads=[B.dsum], writes=[B.rden])
        if sample:
            P.op("dve", lambda e: e.tensor_tensor(
                out=omT[:, 4:8, cs].rearrange("p g (s i) -> p g s i", i=8),
                in0=bkO[:, :].rearrange("p (s g i) -> p g s i", s=16, g=4),
                in1=B.rden[:, :].rearrange("p (s g i) -> p g s i", s=16, g=4), op=ALU.mult),
                 reads=[bkO, B.rden], writes=[omT])
        else:
            P.op("dve", lambda e: e.tensor_tensor(out=omT[:, 4:8, cs], in0=bkO[:, :].rearrange("p (a b) -> p a b", a=4),
                                                  in1=B.rden[:, :].rearrange("p (a b) -> p a b", a=4), op=ALU.mult),
                 reads=[bkO, B.rden], writes=[omT])
        pfree(bkO)

    ptc = {"i": 0}

    def swa_st(B, rows, kcols, qrhs, mask):
        bkS = psum()
        P.op("pe", lambda e: e.matmul(bkS[:, :], lhsT=kT_all[rows, kcols], rhs=qrhs, start=True, stop=True),
             reads=[kT_all, B.qsT], writes=[bkS])
        pt = B.PT[ptc["i"] % 4]
        ptc["i"] += 1
        P.op("act", lambda e: e.activation(out=pt[:, :], in_=bkS[:, :], func=AF.Exp, scale=0.125), reads=[bkS],
             writes=[pt])
        pfree(bkS)
        P.op("dve", lambda e: e.tensor_tensor(out=pt[:, :], in0=pt[:, :], in1=mask[:, :], op=ALU.mult), reads=[mask],
             writes=[pt])
        return pt

    def p3_tile_steps(B, j, n):
        cs = slice(j * 128, (j + 1) * 128)
        blocks = []
        for kv in range(2):
            rows = slice(64 * kv, 64 * kv + 64)
            bl = ([(n - 1, C.swm_prev)] if n > 0 else []) + [(n, C.swm_cur)]
            for bi, (kt, mask) in enumerate(bl):
                blocks.append((rows, kt % NKV, mask, bi == 0, bi == len(bl) - 1))

        def st(bi):
            rows, ks, mask, _, _ = blocks[bi]
            return swa_st(B, rows, slice(ks * 128, (ks + 1) * 128), B.qsT[rows, 0:4, cs], mask)

        def pv(bi, pt):
            rows, ks, mask, first, last = blocks[bi]
            P.op("pe", [lambda e: e.matmul(swO[rows, :], lhsT=vS_all[:, ks, rows], rhs=pt[:, :], start=first, stop=last),
                        lambda e: e.matmul(swD[rows, :], lhsT=C.ones64[:, :], rhs=pt[:, :], start=first, stop=last)],
                 reads=[vS_all, pt, C.ones64], writes=[swO, swD])

        hg_front(B, j, C.cmask)
        yield
        swO = psum()
        swD = psum()
        pts = {0: st(0)}
        yield
        hgO = hg_state(B, j)
        yield
        pv(0, pts[0])
        pts[1] = st(1)
        yield
        hg_norm_out(B, j, [hgO[:, h * 128:(h + 1) * 128] for h in range(4)], [hgO] * 4)
        yield
        for bi in range(1, len(blocks)):
            pv(bi, pts[bi])
            if bi + 1 < len(blocks):
                pts[bi + 1] = st(bi + 1)
            yield
        swa_finish(B, j, swO, swD)
        yield

    def head3_steps(B, tiles):
        for j, n in enumerate(tiles):
            for _ in p3_tile_steps(B, j, n):
                yield
        if tiles[-1] == NT - 1:
            for h in range(4):
                final_toks.append(P.dma(sp_d[h], S[h].ap, reads=[S[h]], sem="spo"))

    def p3_sample(B):
        TS = NT % NKV
        pool1(lambda e: e.memset(B.smask_s[:, :], 1.0), [B.smask_s])
        pool1(lambda e: e.memset(B.smask_s[:, :].rearrange("p (c t) -> p c t", t=8)[:, :, 0:1], 0.0), [B.smask_s])
        for mk, pa, pb in ((B.cmask_s, [[0, 4], [8, 16], [1, 8]], [[0, 4], [-8, 16], [0, 8]]),
                           (B.swm_scur, [[8, 16], [0, 4], [1, 8]], [[-8, 16], [0, 4], [0, 8]])):
            pool1(lambda e, mk=mk: e.memset(mk[:, :], 1.0), [mk])
            pool1(lambda e, mk=mk, pa=pa: e.affine_select(out=mk[:, :], in_=mk[:, :], pattern=pa, base=0,
                                                          channel_multiplier=-1, compare_op=ALU.is_ge, fill=0.0), [mk])
            pool1(lambda e, mk=mk, pb=pb: e.affine_select(out=mk[:, :], in_=mk[:, :], pattern=pb, base=0,
                                                          channel_multiplier=1, compare_op=ALU.is_ge, fill=0.0), [mk])
        pool1(lambda e: e.memset(B.swm_c[:, :], 1.0), [B.swm_c])
        pool1(lambda e: e.affine_select(out=B.swm_c[:, :], in_=B.swm_c[:, :], pattern=[[0, 64], [-1, 8]], base=-1,
                                        channel_multiplier=1, compare_op=ALU.is_ge, fill=0.0), [B.swm_c])

    def p3_sample_body(B):
        TS = NT % NKV
        for c4 in range(4):
            k32 = B.ck32[c4 % 2]
            v32 = B.cv32[c4 % 2]
            P.dma(k32.ap, ck_d[c4 * 4:(c4 + 1) * 4].rearrange("s k c -> k s c"), writes=[k32], sem="ck%d" % (c4 % 2))
            P.dma(v32.ap, cv_d[c4 * 4:(c4 + 1) * 4].rearrange("s k c -> k s c"), writes=[v32], sem="cv%d" % (c4 % 2))
            P.op("dve", lambda e, c4=c4, k32=k32: e.tensor_copy(out=B.ckb[:, c4 * 4:(c4 + 1) * 4, :], in_=k32[:, :, :]),
                 reads=[k32], writes=[B.ckb])
            P.op("pool", lambda e, c4=c4, v32=v32: e.tensor_copy(out=B.cvb[:, c4 * 4:(c4 + 1) * 4, :], in_=v32[:, :, :]),
                 reads=[v32], writes=[B.cvb])
        for half in range(2):
            bk = psum()
            bkb = bk.ap.bitcast(BF16)
            P.op("pe", [lambda e, i=i: e.transpose(bkb[:, i * 128:(i + 1) * 128], B.ckb[:, half * 8 + i, :], C.ident[:, :])
                        for i in range(8)], reads=[B.ckb, C.ident], writes=[bk])
            P.op("act", lambda e, bkb=bkb: e.copy(out=B.kcT[:, half * 8:(half + 1) * 8, :],
                                                  in_=bkb.rearrange("p (a b) -> p a b", a=8)), reads=[bk],
                 writes=[B.kcT])
            pfree(bk)
        final_toks.append(P.dma(ks_d[:, 0:120, :], ck_d[:, 8:128, :], sem="kvc"))
        final_toks.append(P.dma(vs_d[:, 0:120, :], cv_d[:, 8:128, :], sem="kvc"))

        hg_front(B, 0, B.cmask_s)
        bkO = [psum() for _ in range(4)]
        P.op("pe", [lambda e, h=h: e.matmul(bkO[h][:, 0:128], lhsT=B.v_tok[:, 0, h * 128:(h + 1) * 128],
                                            rhs=B.ATm[:, h * 128:(h + 1) * 128], start=True, stop=False)
                    for h in range(4)], reads=[B.v_tok, B.ATm], writes=bkO)
        def stage_a(seq):
            s0 = B.S0[seq % 3]
            s0b = B.S0b[seq % 2]
            ktm = B.ktm[seq % 2]
            P.dma(s0.ap.rearrange("p (h v) -> p h v", h=4), st_d[seq].rearrange("h d v -> d h v"), writes=[s0],
                  sem="s0%d" % (seq % 3))
            P.op("pool", lambda e: e.tensor_copy(out=s0b[:, :], in_=s0[:, :]), reads=[s0], writes=[s0b])
            P.op("pe", [lambda e, h=h: e.matmul(bkO[h][:, seq * 8:(seq + 1) * 8], lhsT=s0b[:, h * 128:(h + 1) * 128],
                                                rhs=B.qtT[:, h, seq * 8:(seq + 1) * 8], start=False, stop=(seq == 15))
                        for h in range(4)], reads=[s0b, B.qtT], writes=bkO)
            P.op("dve", lambda e: e.tensor_scalar(out=ktm[:, :], in0=B.kt_tok[:, :], scalar1=C.seqmask[:, seq:seq + 1],
                                                  scalar2=None, op0=ALU.mult), reads=[B.kt_tok, C.seqmask], writes=[ktm])
            bkP = psum()
            P.op("pe", [lambda e, hs=slice(h * 128, (h + 1) * 128): e.matmul(bkP[:, hs], lhsT=ktm[:, hs],
                                                                             rhs=B.v_tok[:, 0, hs], start=True, stop=True)
                        for h in range(4)], reads=[ktm, B.v_tok], writes=[bkP])
            return bkP

        def stage_b(seq, bkP):
            s0 = B.S0[seq % 3]
            sn = B.Sn[seq % 2]
            for h in range(4):
                hs = slice(h * 128, (h + 1) * 128)
                dc = B.dec[:, h * 16 + seq:h * 16 + seq + 1]
                P.op("act", lambda e, h=h, hs=hs, dc=dc: e.activation(out=B.Pd[h][:, :], in_=bkP[:, hs], func=AF.Identity,
                                                                      scale=dc), reads=[bkP, B.dec], writes=[B.Pd[h]])
                P.op("dve", lambda e, h=h, hs=hs, dc=dc: e.scalar_tensor_tensor(out=sn[h][:, :], in0=s0[:, hs], scalar=dc,
                                                                                in1=B.Pd[h][:, :], op0=ALU.mult,
                                                                                op1=ALU.add),
                     reads=[s0, B.Pd[h], B.dec], writes=[sn[h]])
            pfree(bkP)
            final_toks.append(P.dma(ss_d[seq].rearrange("h d v -> d h v"), B.SnT[seq % 2], reads=sn,
                                    sem="sn%d" % (seq % 2)))

        pend = {0: stage_a(0)}
        for seq in range(16):
            if seq + 1 < 16:
                pend[seq + 1] = stage_a(seq + 1)
            stage_b(seq, pend.pop(seq))
            yield
        oaps = [bkO[h][:, 0:128] for h in range(4)]
        hg_square(B, oaps, bkO)
        hg_norm_out(B, 0, oaps, bkO)

        bO = psum()
        bD = psum()
        for kv in range(2):
            rows = slice(64 * kv, 64 * kv + 64)
            ptn = swa_st(B, rows, slice(TS * 128, (TS + 1) * 128),
                         B.qsT[rows, 0:4, 0:128].rearrange("p g (s i) -> p s g i", i=8), B.swm_scur)
            bkC = psum()
            P.op("pe", [lambda e, s=s: e.matmul(bkC[:, s * 32:(s + 1) * 32], lhsT=B.kcT[rows, s, :],
                                                rhs=B.qsT[rows, 0:4, s * 8:(s + 1) * 8], start=True, stop=True)
                        for s in range(16)], reads=[B.kcT, B.qsT], writes=[bkC])
            pc = B.PT[ptc["i"] % 4]
            ptc["i"] += 1
            P.op("act", lambda e, pc=pc, bkC=bkC: e.activation(out=pc[:, :], in_=bkC[:, :], func=AF.Exp, scale=0.125),
                 reads=[bkC], writes=[pc])
            pfree(bkC)
            P.op("pool", lambda e, pc=pc: e.tensor_tensor(out=pc[:, :], in0=pc[:, :], in1=B.swm_c[:, :], op=ALU.mult),
                 reads=[B.swm_c], writes=[pc])
            fns = [lambda e: e.matmul(bO[rows, :], lhsT=vS_all[:, TS, rows], rhs=ptn[:, :], start=True, stop=False),
                   lambda e: e.matmul(bD[rows, :], lhsT=C.ones64[:, :], rhs=ptn[:, :], start=True, stop=False)]
            for s in range(16):
                fns.append(lambda e, s=s: e.matmul(bO[rows, s * 32:(s + 1) * 32], lhsT=B.cvb[:, s, rows],
                                                   rhs=pc[:, s * 32:(s + 1) * 32], start=False, stop=(s == 15)))
                fns.append(lambda e, s=s: e.matmul(bD[rows, s * 32:(s + 1) * 32], lhsT=C.ones64[:, :],
                                                   rhs=pc[:, s * 32:(s + 1) * 32], start=False, stop=(s == 15)))
            P.op("pe", fns, reads=[vS_all, ptn, pc, B.cvb, C.ones64], writes=[bO, bD])
        swa_finish(B, 0, bO, bD, sample=True)
        yield

    def p45(B, slots):
        nt = B.TB // 128
        W6, u6 = w_use(6)
        W7, u7 = w_use(7)
        pend = []
        for j in range(nt):
            for nh, W in enumerate((W6, W7)):
                bk = psum()
                P.op("pe", [lambda e, k=k, W=W, bk=bk: e.matmul(bk[:, :], lhsT=omT[:, k, j * 128:(j + 1) * 128],
                                                                rhs=W[:, k, :], start=(k == 0), stop=(k == 7))
                            for k in range(8)], reads=[omT, W], writes=[bk])
                xs_ = slots[j]
                P.op("dve", lambda e, nh=nh, bk=bk, xs_=xs_: e.tensor_tensor(
                    out=xs_[:, nh * 512:(nh + 1) * 512], in0=bk[:, :], in1=xs_[:, nh * 512:(nh + 1) * 512], op=ALU.add),
                     reads=[bk], writes=[xs_])
                pfree(bk)
            xb = norm_pre(slots[j])
            pend.append((xb, j))
            if len(pend) > 2:
                xb0, j0 = pend.pop(0)
                norm_post(xb0, hnT, j0)
        w_release(u6)
        w_release(u7)
        return pend

    def mlp_steps(B, slots, xb_last):
        TB = B.TB
        nt = TB // 128
        for xb0, j0 in xb_last:
            norm_post(xb0, hnT, j0)
            yield
        for q in range(4):
            for rr in range(2):
                W, u = w_use(8 + 2 * q + rr)
                for qq in range(4):
                    fl = rr * 4 + qq
                    bk = fm_mm(W, qq * 128, hnT, TB)
                    rl = B.rl[fl % 2]
                    P.op("act", lambda e, bk=bk, rl=rl: e.activation(out=rl[:, 0:TB], in_=bk[:, 0:TB], func=AF.Relu),
                         reads=[bk], writes=[rl])
                    pfree(bk)
                    P.op("act", lambda e, fl=fl, rl=rl: e.activation(out=aT[:, fl, 0:TB], in_=rl[:, 0:TB],
                                                                     func=AF.Square), reads=[rl], writes=[aT])
                    yield
                w_release(u)
            for nh in range(2):
                W, u = w_use(16 + nh * 4 + q)
                for j in range(nt):
                    bk = psum()
                    P.op("pe", [lambda e, f=f, W=W, j=j, bk=bk: e.matmul(
                        bk[:, :], lhsT=aT[:, f, j * 128:(j + 1) * 128], rhs=W[:, f, :], start=(f == 0), stop=(f == 7))
                        for f in range(8)], reads=[aT, W], writes=[bk])
                    xs_ = slots[j]
                    P.op("dve", lambda e, nh=nh, bk=bk, xs_=xs_: e.tensor_tensor(
                        out=xs_[:, nh * 512:(nh + 1) * 512], in0=bk[:, :], in1=xs_[:, nh * 512:(nh + 1) * 512],
                        op=ALU.add), reads=[bk], writes=[xs_])
                    pfree(bk)
                    yield
                w_release(u)

    def p8(slots, ydst):
        for j in range(len(slots)):
            rs = norm_stats(slots[j])
            xs_ = slots[j]
            ysem = "yx%d" % xres_index(xs_)
            P.op("dve", lambda e, xs_=xs_, rs=rs: e.scalar_tensor_tensor(
                out=xs_[:, :], in0=xs_[:, :], scalar=rs[:, 0:1], in1=C.gfin[:, :], op0=ALU.mult, op1=ALU.mult),
                 reads=[rs, C.gfin], writes=[xs_])
            final_toks.append(P.dma(ydst[j * 128:(j + 1) * 128, :], xs_.ap, reads=[xs_], sem=ysem))

    def xres_index(b):
        for i_, bb in enumerate(xres):
            if bb is b:
                return i_
        raise AssertionError("not an xres slot")

    def drain(gen):
        for _ in gen:
            pass

    def interleave(genA, genB):
        a_ok = b_ok = True
        while a_ok or b_ok:
            if b_ok:
                try:
                    next(genB)
                except StopIteration:
                    b_ok = False
            if a_ok:
                try:
                    next(genA)
                except StopIteration:
                    a_ok = False

    def chain(*gens):
        for g_ in gens:
            for _ in g_:
                yield

    def main_schedule():
        xload = {}

        def load_x(n):
            s = xres[n % NXS]
            P.dma(s.ap, x_d[n * 128:(n + 1) * 128, :], writes=[s], sem="x%d" % (n % NXS))
            xload[n] = s

        def gtiles(g):
            return list(range(g * G, (g + 1) * G))

        Bs = carve(128, True)
        REG["B"] = Bs
        xsmp = xres[7]
        P.dma(xsmp.ap, xs_d, writes=[xsmp], sem="x7")
        w_prefetch()
        p3_sample(Bs)
        drain(head12_steps(Bs, [NT], [xsmp], True))
        chk("p2s")
        drain(p3_sample_body(Bs))
        chk("p3s")
        P.wait_engines(["act", "dve", "pool", "pe", "sp"], ["act", "dve", "pool", "pe"])
        wait_final()
        Bp = carve(512, False)
        REG["B"] = Bp
        for n in gtiles(0):
            load_x(n)

        def sample_tail():
            xb_last = p45(Bs, [xsmp])
            yield
            for _ in mlp_steps(Bs, [xsmp], xb_last):
                yield
            p8([xsmp], ys_d)
            yield

        interleave(sample_tail(),
                   chain(head12_steps(Bp, gtiles(0), [xload[n] for n in gtiles(0)], False), head3_steps(Bp, gtiles(0))))
        chk("h0")
        for g in range(NG):
            slots = [xload[n] for n in gtiles(g)]
            if g + 1 < NG:
                for n in gtiles(g + 1):
                    load_x(n)
            xb_last = p45(Bp, slots)
            genA = mlp_steps(Bp, slots, xb_last)
            if g + 1 < NG:
                nslots = [xload[n] for n in gtiles(g + 1)]
                genB = chain(head12_steps(Bp, gtiles(g + 1), nslots, False), head3_steps(Bp, gtiles(g + 1)))
                interleave(genA, genB)
            else:
                drain(genA)
            p8(slots, y_d[g * G * 128:(g + 1) * G * 128, :])
            chk("g%d" % g)

    try:
        main_schedule()
    except _Stop:
        pass
    REG.update(dict(xnT=xnT, hnT=hnT, omT=omT, kT_all=kT_all, vS_all=vS_all, S_bf=S_bf, xres0=xres[0], xres1=xres[1],
                    lb=C.lb, seqmask=C.seqmask))
    for dn in dumps:
        b = REG[dn] if dn in REG else getattr(REG["B"], dn)
        shp = [int(v) for v in b.ap.shape]
        dd = nc.dram_tensor("dbg_" + dn, shp, b.ap.dtype, kind="ExternalOutput").ap()
        final_toks.append(P.dma(dd, b.ap, reads=[b], sem="dbg"))

    wait_final()
    P.wait_engines(["sp"], ["act", "dve", "pool", "pe"])
    return nc, WORDER


def _chunk_k(a):
    return np.ascontiguousarray(a.reshape(8, 128, 512).transpose(1, 0, 2)).reshape(128, 4096)


def _build_wall(w_in, w_out, w_up, w_down):
    zq, zf, zi, zg = w_in[:, 0:512], w_in[:, 512:1024], w_in[:, 1024:1536], w_in[:, 1536:2048]
    sq, sk, sv = w_in[:, 2048:2560], w_in[:, 2560:2688], w_in[:, 2688:2816]
    sqP = np.concatenate([np.concatenate([sq[:, g * 64:(g + 1) * 64], sq[:, (4 + g) * 64:(5 + g) * 64]], axis=1)
                          for g in range(4)], axis=1)
    c5 = np.concatenate([sk, sv, np.zeros((1024, 256), np.float32)], axis=1)
    chunks = [_chunk_k(a) for a in (zq, zg, zf, sqP, zi, c5)]
    perm = list(range(512)) + [512 + (kv * 4 + g) * 64 + d for g in range(4) for kv in range(2) for d in range(64)]
    wo = w_out[perm]
    chunks += [_chunk_k(wo[:, nh * 512:(nh + 1) * 512]) for nh in range(2)]
    chunks += [_chunk_k(w_up[:, r * 512:(r + 1) * 512]) for r in range(8)]
    for nh in range(2):
        for r4 in range(4):
            chunks.append(_chunk_k(w_down[r4 * 1024:(r4 + 1) * 1024, nh * 512:(nh + 1) * 512]))
    return np.ascontiguousarray(np.stack(chunks, axis=0), dtype=np.float32)


def _make_in_maps(x_prompt, x_sample, state_hgrn, cache_swa_k, cache_swa_v, ln_mix, w_in, lb_logits, hg_norm, sinks,
           w_out, ln_mlp, w_up, w_down, ln_final):
    f = lambda a: np.ascontiguousarray(np.asarray(a), dtype=np.float32)
    x_prompt, x_sample, state_hgrn = f(x_prompt), f(x_sample), f(state_hgrn)
    cache_swa_k, cache_swa_v = f(cache_swa_k), f(cache_swa_v)
    wall = _build_wall(f(w_in)[0], f(w_out)[0], f(w_up)[0], f(w_down)[0])
    lnmix = np.ascontiguousarray(f(ln_mix)[0].reshape(8, 128).T)
    lnmlp = np.ascontiguousarray(f(ln_mlp)[0].reshape(8, 128).T)
    lbl = np.ascontiguousarray(f(lb_logits).reshape(2, 4, 128).transpose(2, 0, 1).reshape(128, 8))
    hgn = np.ascontiguousarray(f(hg_norm)[0].reshape(128, 1))
    sk_ = f(sinks)[0]
    sinkT = np.ascontiguousarray(np.stack([sk_[(p // 64) * 4:(p // 64) * 4 + 4] for p in range(128)], axis=0))
    gfin = np.ascontiguousarray(np.broadcast_to(f(ln_final)[None, :], (128, D)))
    in_maps = []
    for c in range(NCORES):
        in_maps.append({
            "x": x_prompt[c],
            "xs": np.ascontiguousarray(x_sample[16 * c:16 * (c + 1)].reshape(128, D)),
            "st": state_hgrn[0, 16 * c:16 * (c + 1)],
            "ck": np.ascontiguousarray(cache_swa_k[0, 16 * c:16 * (c + 1)].reshape(16, 128, 128)),
            "cv": np.ascontiguousarray(cache_swa_v[0, 16 * c:16 * (c + 1)].reshape(16, 128, 128)),
            "wall": wall, "lnmix": lnmix, "lnmlp": lnmlp, "lbl": lbl, "hgn": hgn, "sinkT": sinkT, "gfin": gfin,
        })
    return in_maps


def kernel(x_prompt, x_sample, state_hgrn, cache_swa_k, cache_swa_v, ln_mix, w_in, lb_logits, hg_norm, sinks,
           w_out, ln_mlp, w_up, w_down, ln_final):
    in_maps = _make_in_maps(x_prompt, x_sample, state_hgrn, cache_swa_k, cache_swa_v, ln_mix, w_in, lb_logits, hg_norm,
                            sinks, w_out, ln_mlp, w_up, w_down, ln_final)
    _, order = build_program()
    nc, _ = build_program(worder=order)
    res = run_bass_kernel_spmd(nc, in_maps, core_ids=list(range(NCORES)))
    R = res.results
    y_prompt = np.stack([R[c]["y"] for c in range(NCORES)], axis=0).reshape(8, 4096, D)
    y_sample = np.concatenate([R[c]["ys"].reshape(16, 8, D) for c in range(NCORES)], axis=0)
    sp = np.stack([R[c]["sp"] for c in range(NCORES)], axis=0)[None]
    kp = np.stack([R[c]["kp"].reshape(128, 2, 64) for c in range(NCORES)], axis=0)[None]
    vp = np.stack([R[c]["vp"].reshape(128, 2, 64) for c in range(NCORES)], axis=0)[None]
    ss = np.concatenate([R[c]["ss"] for c in range(NCORES)], axis=0)[None]
    ks = np.concatenate([R[c]["ks"].reshape(16, 128, 2, 64) for c in range(NCORES)], axis=0)[None]
    vs = np.concatenate([R[c]["vs"].reshape(16, 128, 2, 64) for c in range(NCORES)], axis=0)[None]
    out = (y_prompt, y_sample, sp, kp, vp, ss, ks, vs)
    return tuple(np.ascontiguousarray(o, dtype=np.float32) for o in out)
```

```python
import os
import numpy as np
import concourse.bass as bass
import concourse.mybir as mybir
from concourse.bass_utils import run_bass_kernel_spmd

F32 = mybir.dt.float32
BF16 = mybir.dt.bfloat16
AF = mybir.ActivationFunctionType
ALU = mybir.AluOpType

NCORES = 8
D = 1024
NT = 32
G = 4
NG = NT // G
NSLOT = 5
NCH = 24
EPS = 1e-6


class Buf:
    __slots__ = ("ap", "w", "r")

    def __init__(self, ap):
        self.ap = ap
        self.w = None
        self.r = {}

    def __getitem__(self, k):
        return self.ap[k]


class Eng:
    def __init__(self, h, semname):
        self.h = h
        self.semname = semname
        self.n = 0
        self.waited = {}


class Prog:
    def __init__(self, nc):
        self.nc = nc
        self.sems = {}
        self.dcnt = {}
        self.E = {}
        for name, h in (("pe", nc.tensor), ("act", nc.scalar), ("dve", nc.vector),
                        ("pool", nc.gpsimd), ("sp", nc.sync)):
            self.sems["e_" + name] = nc.alloc_semaphore("e_" + name)
            self.E[name] = Eng(h, "e_" + name)

    def dsem(self, name):
        if name not in self.sems:
            self.sems[name] = self.nc.alloc_semaphore(name)
            self.dcnt[name] = 0
        return name

    def _wait(self, e, tok):
        if tok is None:
            return
        sn, v = tok
        if e is self.E["pe"] and sn == "e_pe":
            return
        if e.waited.get(sn, 0) >= v:
            return
        e.h.wait_ge(self.sems[sn], v)
        e.waited[sn] = v

    def _deps(self, e, reads, writes):
        for b in reads:
            self._wait(e, b.w)
        for b in writes:
            self._wait(e, b.w)
            for sn, v in b.r.items():
                self._wait(e, (sn, v))

    @staticmethod
    def _mark(tok, reads, writes):
        sn, v = tok
        for b in reads:
            if b.r.get(sn, 0) < v:
                b.r[sn] = v
        for b in writes:
            b.w = tok
            b.r = {}

    def op(self, en, fns, reads=(), writes=()):
        e = self.E[en]
        if callable(fns):
            fns = [fns]
        self._deps(e, reads, writes)
        ins = None
        for f in fns:
            ins = f(e.h)
        ins.then_inc(self.sems[e.semname], 1)
        e.n += 1
        tok = (e.semname, e.n)
        self._mark(tok, reads, writes)
        return tok

    def dma(self, out, in_, reads=(), writes=(), sem=None):
        e = self.E["sp"]
        self.dsem(sem)
        self._deps(e, reads, writes)
        ins = e.h.dma_start(out=out, in_=in_)
        ins.then_inc(self.sems[sem], 16)
        self.dcnt[sem] += 16
        tok = (sem, self.dcnt[sem])
        self._mark(tok, reads, writes)
        return tok

    def wait_engines(self, waiters, targets):
        for wn in waiters:
            e = self.E[wn]
            for tn in targets:
                t = self.E[tn]
                if t.n > 0:
                    self._wait(e, (t.semname, t.n))


class Carver:
    def __init__(self, pool_ap, nbytes):
        self.pool = pool_ap
        self.nbytes = nbytes
        self.off = 0

    def take(self, free_shape, dt):
        n = int(np.prod(free_shape))
        nb = n * (4 if dt == F32 else 2)
        s = self.off // 2
        assert self.off + nb <= self.nbytes, ("carve overflow", self.off, nb, self.nbytes)
        ap = self.pool[:, s:s + nb // 2]
        if dt == F32:
            ap = ap.bitcast(F32)
        if len(free_shape) > 1:
            names = "abcd"[:len(free_shape)]
            ap = ap.rearrange("p (%s) -> p %s" % (" ".join(names), " ".join(names)),
                              **{k: int(v) for k, v in zip(names, free_shape)})
        self.off += (nb + 31) // 32 * 32
        return Buf(ap)


class NS:
    pass


class _Stop(Exception):
    pass


def build_program(worder=None, stop=None, dumps=()):
    record = worder is None
    WORDER = [] if record else list(worder)
    nc = bass.Bass("TRN2", target_bir_lowering=False)
    P = Prog(nc)
    REG = {}

    def chk(stage):
        if stop == stage:
            raise _Stop()

    def din(name, shape, dt=F32):
        return nc.dram_tensor(name, list(shape), dt, kind="ExternalInput").ap()

    def dout(name, shape, dt=F32):
        return nc.dram_tensor(name, list(shape), dt, kind="ExternalOutput").ap()

    x_d = din("x", [NT * 128, D])
    xs_d = din("xs", [128, D])
    st_d = din("st", [16, 4, 128, 128])
    ck_d = din("ck", [16, 128, 128])
    cv_d = din("cv", [16, 128, 128])
    wall_d = din("wall", [NCH, 128, 4096])
    lnmix_d = din("lnmix", [128, 8])
    lnmlp_d = din("lnmlp", [128, 8])
    lbl_d = din("lbl", [128, 8])
    hgn_d = din("hgn", [128, 1])
    sinkT_d = din("sinkT", [128, 4])
    gfin_d = din("gfin", [128, D])
    y_d = dout("y", [NT * 128, D])
    ys_d = dout("ys", [128, D])
    sp_d = dout("sp", [4, 128, 128])
    kp_d = dout("kp", [128, 128])
    vp_d = dout("vp", [128, 128])
    ss_d = dout("ss", [16, 4, 128, 128])
    ks_d = dout("ks", [16, 128, 128])
    vs_d = dout("vs", [16, 128, 128])
    wbf_d = nc.dram_tensor("wbf", [NCH, 128, 4096], BF16).ap()

    final_toks = []

    def wait_final():
        mx = {}
        for sn_, v_ in final_toks:
            mx[sn_] = max(mx.get(sn_, 0), v_)
        for sn_, v_ in mx.items():
            P._wait(P.E["sp"], (sn_, v_))

    def sb(name, free_shape, dt):
        return Buf(nc.alloc_sbuf_tensor("s_" + name, [128] + list(free_shape), dt)[:])

    C = NS()
    C.ident = sb("ident", [128], BF16)
    C.onesm = sb("onesm", [128], BF16)
    C.ones64 = sb("ones64", [64], BF16)
    C.mhalf = sb("mhalf", [1], F32)
    C.epsc = sb("epsc", [1], F32)
    C.cmask = sb("cmask", [512], F32)
    C.smask = sb("smask", [512], F32)
    C.swm_cur = sb("swm_cur", [512], BF16)
    C.swm_prev = sb("swm_prev", [512], BF16)
    C.seqmask = sb("seqmask", [16], F32)
    C.gfin = sb("gfin", [D], F32)
    C.lnmix = sb("lnmix", [8], F32)
    C.lnmlp = sb("lnmlp", [8], F32)
    C.wos = sb("wos", [8], F32)
    C.lbl = sb("lbl", [8], F32)
    C.hgn = sb("hgn", [1], F32)
    C.sinkT = sb("sinkT", [4], F32)
    C.esinkT = sb("esinkT", [4], F32)
    C.dl = sb("dl", [4], F32)
    C.lb = sb("lb", [4], F32)
    C.oml = sb("oml", [4], F32)
    C.noml = sb("noml", [4], F32)

    NXS = 8
    XR = nc.alloc_sbuf_tensor("XR", [128, NXS * 2048], BF16)
    xres = [Buf(XR[:, i * 2048:(i + 1) * 2048].bitcast(F32)) for i in range(NXS)]
    xnb = [sb("xnb%d" % i, [D], BF16) for i in range(3)]
    xnb_free = [0, 1, 2]
    junk = sb("junk", [D], BF16)
    NST = 6
    stat = [(sb("ssq%d" % i, [1], F32), sb("tv%d" % i, [1], F32), sb("rs%d" % i, [1], F32)) for i in range(NST)]
    xnT = sb("xnT", [8, 512], BF16)
    hnT = sb("hnT", [8, 512], BF16)
    NKV = 8
    kT_all = sb("kT_all", [NKV * 128], BF16)
    vS_all = sb("vS_all", [NKV, 128], BF16)
    Sall = nc.alloc_sbuf_tensor("Sst", [128, 512], F32)
    S = [Buf(Sall[:, h * 128:(h + 1) * 128]) for h in range(4)]
    S_bf = sb("S_bf", [512], BF16)
    omT = sb("omT", [8, 512], BF16)
    aT = sb("aT", [8, 512], BF16)
    wring = [sb("wr%d" % i, [8, 512], BF16) for i in range(NSLOT)]
    kvout_p = sb("kvout", [256], F32)
    rl_p = [sb("rl%d" % i, [512], BF16) for i in range(2)]
    stg32 = [sb("stg%d" % i, [2, 512], F32) for i in range(4)]
    UBYTES = 54 * 1024
    U = nc.alloc_sbuf_tensor("U", [128, UBYTES // 2], BF16)

    banks = [Buf(nc.alloc_psum_tensor("pb%d" % i, [128, 512], F32)[:]) for i in range(8)]
    pfreeq = list(range(8))

    def psum():
        assert pfreeq, "out of PSUM banks"
        return banks[pfreeq.pop(0)]

    def pfree(b):
        for i, bb in enumerate(banks):
            if bb is b:
                assert i not in pfreeq
                pfreeq.append(i)
                return
        raise AssertionError("not a bank")

    def pool1(fn, writes, reads=()):
        return P.op("pool", fn, reads=reads, writes=writes)

    cl = [(C.lnmix, lnmix_d), (C.lnmlp, lnmlp_d), (C.lbl, lbl_d), (C.hgn, hgn_d), (C.sinkT, sinkT_d),
          (C.gfin, gfin_d)]
    for b, d_ in cl:
        P.dma(b.ap, d_, writes=[b], sem="cst")
    ctok = ("cst", P.dcnt["cst"])
    for b, _ in cl:
        b.w = ctok

    pool1(lambda e: e.memset(C.ident[:, :], 0.0), [C.ident])
    pool1(lambda e: e.affine_select(out=C.ident[:, :], in_=C.ident[:, :], pattern=[[-1, 128]], base=0,
                                    channel_multiplier=1, compare_op=ALU.not_equal, fill=1.0), [C.ident])
    pool1(lambda e: e.memset(C.onesm[:, :], 1.0 / 128.0), [C.onesm])
    pool1(lambda e: e.memset(C.ones64[:, :], 1.0), [C.ones64])
    pool1(lambda e: e.memset(C.mhalf[:, :], -0.5), [C.mhalf])
    pool1(lambda e: e.memset(C.epsc[:, :], EPS), [C.epsc])
    pool1(lambda e: e.memset(C.cmask[:, :], 1.0), [C.cmask])
    pool1(lambda e: e.affine_select(out=C.cmask[:, :], in_=C.cmask[:, :], pattern=[[0, 4], [1, 128]], base=0,
                                    channel_multiplier=-1, compare_op=ALU.is_ge, fill=0.0), [C.cmask])
    pool1(lambda e: e.memset(C.smask[:, :], 1.0), [C.smask])
    pool1(lambda e: e.memset(C.smask[:, :].rearrange("p (c t) -> p c t", t=128)[:, :, 0:1], 0.0), [C.smask])
    pool1(lambda e: e.memset(C.swm_cur[:, :], 1.0), [C.swm_cur])
    pool1(lambda e: e.affine_select(out=C.swm_cur[:, :], in_=C.swm_cur[:, :], pattern=[[0, 4], [1, 128]], base=0,
                                    channel_multiplier=-1, compare_op=ALU.is_ge, fill=0.0), [C.swm_cur])
    pool1(lambda e: e.memset(C.swm_prev[:, :], 1.0), [C.swm_prev])
    pool1(lambda e: e.affine_select(out=C.swm_prev[:, :], in_=C.swm_prev[:, :], pattern=[[0, 4], [-1, 128]], base=-1,
                                    channel_multiplier=1, compare_op=ALU.is_ge, fill=0.0), [C.swm_prev])
    pool1(lambda e: e.memset(C.seqmask[:, :], 1.0), [C.seqmask])
    pool1(lambda e: e.affine_select(out=C.seqmask[:, :], in_=C.seqmask[:, :], pattern=[[-8, 16]], base=0,
                                    channel_multiplier=1, compare_op=ALU.is_ge, fill=0.0), [C.seqmask])
    pool1(lambda e: e.affine_select(out=C.seqmask[:, :], in_=C.seqmask[:, :], pattern=[[8, 16]], base=7,
                                    channel_multiplier=-1, compare_op=ALU.is_ge, fill=0.0), [C.seqmask])
    pool1(lambda e: e.memset(C.wos[:, :], 1.0), [C.wos])
    for c in range(4):
        P.op("dve", lambda e, c=c: e.tensor_copy(out=C.wos[:, c:c + 1], in_=C.hgn[:, 0:1]), reads=[C.hgn], writes=[C.wos])
    P.op("dve", lambda e: e.tensor_tensor(out=C.dl[:, :], in0=C.lbl[:, 0:4], in1=C.lbl[:, 4:8], op=ALU.subtract),
         reads=[C.lbl], writes=[C.dl])
    P.op("act", lambda e: e.activation(out=C.lb[:, :], in_=C.dl[:, :], func=AF.Sigmoid), reads=[C.dl], writes=[C.lb])
    P.op("dve", lambda e: e.tensor_scalar(out=C.oml[:, :], in0=C.lb[:, :], scalar1=-1.0, scalar2=1.0, op0=ALU.mult,
                                          op1=ALU.add), reads=[C.lb], writes=[C.oml])
    P.op("dve", lambda e: e.tensor_scalar(out=C.noml[:, :], in0=C.lb[:, :], scalar1=-1.0, scalar2=None, op0=ALU.add),
         reads=[C.lb], writes=[C.noml])
    P.op("act", lambda e: e.activation(out=C.esinkT[:, :], in_=C.sinkT[:, :], func=AF.Exp), reads=[C.sinkT],
         writes=[C.esinkT])
    for h in range(4):
        pool1(lambda e, h=h: e.memset(S[h][:, :], 0.0), [S[h]])
    pool1(lambda e: e.memset(S_bf[:, :], 0.0), [S_bf])

    wslot_free = list(range(NSLOT))
    loaded = {}
    wr = {"next": 0, "use": 0}

    converted = set()
    wbfb = [Buf(None) for _ in range(NCH)]
    pending = []
    qc = {"i": 0}

    def flush_casts():
        while pending:
            cid, s_, qs = pending.pop(0)
            slot = wring[s_]
            sc = C.lnmix if cid < 6 else (C.wos if cid < 8 else (C.lnmlp if cid < 16 else None))
            for qi, st in enumerate(qs):
                en = ("act", "dve")[qi % 2]
                if sc is not None:
                    if en == "act":
                        fns = [lambda e, k=k, st=st: e.activation(out=slot[:, 2 * qi + k, :], in_=st[:, k, :],
                                                                  func=AF.Identity, scale=sc[:, 2 * qi + k:2 * qi + k + 1])
                               for k in range(2)]
                    else:
                        fns = [lambda e, k=k, st=st: e.tensor_scalar(out=slot[:, 2 * qi + k, :], in0=st[:, k, :],
                                                                     scalar1=sc[:, 2 * qi + k:2 * qi + k + 1], scalar2=None,
                                                                     op0=ALU.mult) for k in range(2)]
                    P.op(en, fns, reads=[st, sc], writes=[slot])
                else:
                    if en == "act":
                        P.op(en, lambda e, st=st: e.copy(out=slot[:, 2 * qi:2 * qi + 2, :], in_=st[:, :, :]), reads=[st],
                             writes=[slot])
                    else:
                        P.op(en, lambda e, st=st: e.tensor_copy(out=slot[:, 2 * qi:2 * qi + 2, :], in_=st[:, :, :]),
                             reads=[st], writes=[slot])
            P.dma(wbf_d[cid], slot.ap.rearrange("p a b -> p (a b)"), reads=[slot], writes=[wbfb[cid]], sem="ws%d" % s_)

    def w_load(l):
        cid = WORDER[l]
        s_ = wslot_free.pop(0)
        if cid in converted:
            P.dma(wring[s_].ap.rearrange("p a b -> p (a b)"), wbf_d[cid], reads=[wbfb[cid]], writes=[wring[s_]],
                  sem="wr%d" % s_)
        else:
            flush_casts()
            converted.add(cid)
            qs = []
            for qi in range(4):
                st = stg32[qc["i"] % 4]
                qc["i"] += 1
                P.dma(st.ap.rearrange("p a b -> p (a b)"), wall_d[cid][:, qi * 1024:(qi + 1) * 1024], writes=[st],
                      sem="pl%d" % (qc["i"] % 4))
                qs.append(st)
            pending.append((cid, s_, qs))
        loaded[l] = s_

    def w_prefetch():
        if record:
            return
        while wslot_free and wr["next"] < len(WORDER):
            w_load(wr["next"])
            wr["next"] += 1

    def w_use(cid):
        flush_casts()
        u = wr["use"]
        wr["use"] += 1
        if record:
            WORDER.append(cid)
        else:
            assert WORDER[u] == cid, (u, cid, WORDER[u])
        if u not in loaded:
            assert wr["next"] == u
            w_load(u)
            wr["next"] += 1
        flush_casts()
        w_prefetch()
        return wring[loaded[u]], u

    def w_release(u):
        wslot_free.append(loaded.pop(u))
        w_prefetch()

    def carve(TB, sample):
        cvr = Carver(U[:, :], UBYTES)
        B = NS()
        B.TB = TB
        nt = TB // 128
        B.sg4 = [cvr.take([TB], F32) for _ in range(4)]
        B.gt = cvr.take([TB], F32)
        B.kf = cvr.take([TB], F32)
        B.bt = cvr.take([TB], F32)
        B.Eb = cvr.take([TB], F32)
        B.Enb = cvr.take([TB], F32)
        B.qtT = cvr.take([4, TB], BF16)
        B.ktT = cvr.take([4, TB], BF16)
        B.sgT = cvr.take([4, TB], BF16)
        B.qsT = cvr.take([4, TB], BF16)
        B.v_tok = cvr.take([nt, 512], BF16)
        B.dec = cvr.take([64], F32)
        B.ATm = cvr.take([512], BF16)
        B.kt_tok = cvr.take([512], BF16)
        pd = cvr.take([512], F32)
        B.Pd = [Buf(pd[:, h * 128:(h + 1) * 128]) for h in range(4)]
        B.sqn = cvr.take([512], BF16)
        B.lnv = cvr.take([512], F32)
        B.rstdn = B.lnv
        B.t1 = cvr.take([512], BF16)
        B.PT = [cvr.take([512], BF16) for _ in range(4)]
        B.dsum = cvr.take([512], F32)
        B.rden = B.dsum
        B.rl = rl_p
        B.kvout = kvout_p
        if sample:
            B.cmask_s = cvr.take([512], F32)
            B.smask_s = cvr.take([128], F32)
            B.swm_scur = cvr.take([512], BF16)
            B.swm_c = cvr.take([512], BF16)
            B.ckb = cvr.take([16, 128], BF16)
            B.cvb = cvr.take([16, 128], BF16)
            B.kcT = cvr.take([16, 128], BF16)
            B.ktm = [cvr.take([512], BF16) for _ in range(2)]
            B.S0b = [cvr.take([512], BF16) for _ in range(2)]
            xc = Carver(XR[:, 4 * 2048:7 * 2048], 3 * 4096)
            B.ck32 = [xc.take([4, 128], F32)] * 2
            B.cv32 = [cvr.take([4, 128], F32)] * 2
            B.S0 = [xc.take([512], F32) for _ in range(3)]
            B.Sn = []
            B.SnT = []
            for _ in range(2):
                t_ = xc.take([512], F32)
                B.Sn.append([Buf(t_[:, h * 128:(h + 1) * 128]) for h in range(4)])
                B.SnT.append(t_.ap.rearrange("p (h v) -> p h v", h=4))
        return B

    stc = {"i": 0, "x": 0, "y": 0}

    def norm_stats(src):
        ssq, tv, rs = stat[stc["i"] % NST]
        stc["i"] += 1
        P.op("act", lambda e: e.activation(out=junk[:, :], in_=src[:, :], func=AF.Square, accum_out=ssq[:, 0:1]),
             reads=[src], writes=[ssq, junk])
        P.op("act", lambda e: e.activation(out=tv[:, :], in_=ssq[:, :], func=AF.Ln, scale=1.0 / D, bias=C.epsc[:, 0:1]),
             reads=[ssq, C.epsc], writes=[tv])
        P.op("act", lambda e: e.activation(out=rs[:, :], in_=tv[:, :], func=AF.Exp, scale=-0.5), reads=[tv], writes=[rs])
        return rs

    def norm_pre(src):
        rs = norm_stats(src)
        assert xnb_free, "xnb ring exhausted"
        xb = xnb[xnb_free.pop(0)]
        P.op("dve", lambda e: e.tensor_scalar(out=xb[:, :], in0=src[:, :], scalar1=rs[:, 0:1], scalar2=None,
                                              op0=ALU.mult), reads=[src, rs], writes=[xb])
        return xb

    def norm_post(xb, dstT, j):
        bk = psum()
        bkb = bk.ap.bitcast(BF16)
        P.op("pe", [lambda e, k=k: e.transpose(bkb[:, k * 128:(k + 1) * 128], xb[:, k * 128:(k + 1) * 128], C.ident[:, :])
                    for k in range(8)], reads=[xb, C.ident], writes=[bk])
        P.op("dve", lambda e: e.tensor_copy(out=dstT[:, 0:8, j * 128:(j + 1) * 128],
                                            in_=bkb.rearrange("p (a b) -> p a b", a=8)), reads=[bk], writes=[dstT])
        pfree(bk)
        for i_, b_ in enumerate(xnb):
            if b_ is xb:
                xnb_free.append(i_)

    def fm_mm(W, col0, srcT, TB, ncols=128):
        bk = psum()
        P.op("pe", [lambda e, k=k: e.matmul(bk[0:ncols, 0:TB], lhsT=W[:, k, col0:col0 + ncols], rhs=srcT[:, k, 0:TB],
                                            start=(k == 0), stop=(k == 7)) for k in range(8)],
             reads=[W, srcT], writes=[bk])
        return bk

    def tm_mm(W, ncols, srcT, j):
        bk = psum()
        P.op("pe", [lambda e, k=k: e.matmul(bk[:, 0:ncols], lhsT=srcT[:, k, j * 128:(j + 1) * 128], rhs=W[:, k, 0:ncols],
                                            start=(k == 0), stop=(k == 7)) for k in range(8)],
             reads=[W, srcT], writes=[bk])
        return bk

    def head12_steps(B, tiles, slots, sample):
        TB = B.TB
        nt = TB // 128
        xb_prev = None
        for j in range(nt):
            xb = norm_pre(slots[j])
            if xb_prev is not None:
                norm_post(xb_prev, xnT, j - 1)
            xb_prev = xb
            yield
        norm_post(xb_prev, xnT, nt - 1)
        yield
        W, u = w_use(0)
        for h in range(4):
            bk = fm_mm(W, h * 128, xnT, TB)
            P.op("act", lambda e, h=h, bk=bk: e.activation(out=B.qtT[:, h, :], in_=bk[:, 0:TB], func=AF.Silu),
                 reads=[bk], writes=[B.qtT])
            pfree(bk)
            yield
        w_release(u)
        W, u = w_use(1)
        for h in range(4):
            bk = fm_mm(W, h * 128, xnT, TB)
            P.op("act", lambda e, h=h, bk=bk: e.activation(out=B.sgT[:, h, :], in_=bk[:, 0:TB], func=AF.Silu),
                 reads=[bk], writes=[B.sgT])
            pfree(bk)
            yield
        w_release(u)
        W, u = w_use(2)
        sm = B.smask_s if sample else C.smask
        for hp in range(1):
            for hh in range(4):
                h = hh
                bk = fm_mm(W, h * 128, xnT, TB)
                P.op("act", lambda e, hh=hh, bk=bk: e.activation(out=B.sg4[hh][:, :], in_=bk[:, 0:TB], func=AF.Sigmoid),
                     reads=[bk], writes=[B.sg4[hh]])
                pfree(bk)
                yield
            w_release(u)
            for hh in range(4):
                h = hh
                sg = B.sg4[hh]
                P.op("act", lambda e, h=h, sg=sg: e.activation(out=B.gt[:, :], in_=sg[:, :], func=AF.Ln,
                                                               scale=C.oml[:, h:h + 1], bias=C.lb[:, h:h + 1]),
                     reads=[sg, C.oml, C.lb], writes=[B.gt])
                P.op("dve", lambda e, h=h, sg=sg: e.tensor_scalar(out=B.kf[:, :], in0=sg[:, :], scalar1=C.noml[:, h:h + 1],
                                                                  scalar2=C.oml[:, h:h + 1], op0=ALU.mult, op1=ALU.add),
                     reads=[sg, C.noml, C.oml], writes=[B.kf])
                P.op("dve", lambda e: e.tensor_tensor_scan(out=B.bt[:, :], data0=sm[:, 0:TB], data1=B.gt[:, :],
                                                           initial=0.0, op0=ALU.mult, op1=ALU.add), reads=[sm, B.gt],
                     writes=[B.bt])
                P.op("act", lambda e: e.activation(out=B.Eb[:, :], in_=B.bt[:, :], func=AF.Exp), reads=[B.bt],
                     writes=[B.Eb])
                P.op("act", lambda e: e.activation(out=B.Enb[:, :], in_=B.bt[:, :], func=AF.Exp, scale=-1.0),
                     reads=[B.bt], writes=[B.Enb])
                P.op("dve", lambda e, h=h: e.tensor_tensor(out=B.ktT[:, h, :], in0=B.kf[:, :], in1=B.Enb[:, :],
                                                           op=ALU.mult), reads=[B.kf, B.Enb], writes=[B.ktT])
                P.op("dve", lambda e, h=h: e.tensor_tensor(out=B.qtT[:, h, :], in0=B.qtT[:, h, :], in1=B.Eb[:, :],
                                                           op=ALU.mult), reads=[B.Eb], writes=[B.qtT])
                if sample:
                    P.op("act", lambda e, h=h: e.copy(out=B.dec[:, h * 16:(h + 1) * 16],
                                                              in_=B.Eb[:, :].rearrange("p (s i) -> p s i", i=8)[:, :, 7]),
                         reads=[B.Eb], writes=[B.dec])
                else:
                    P.op("act", lambda e, h=h: e.copy(
                        out=B.dec[:, h * nt:(h + 1) * nt],
                        in_=B.Eb[:, :].rearrange("p (j t) -> p j t", t=128)[:, :, 127]), reads=[B.Eb], writes=[B.dec])
                yield
        W, u = w_use(3)
        for g in range(4):
            bk = fm_mm(W, g * 128, xnT, TB)
            P.op("dve", lambda e, g=g, bk=bk: e.tensor_copy(out=B.qsT[:, g, :], in_=bk[:, 0:TB]), reads=[bk],
                 writes=[B.qsT])
            pfree(bk)
            yield
        w_release(u)
        W4, u = w_use(4)
        for j, n in enumerate(tiles):
            bk = tm_mm(W4, 512, xnT, j)
            P.op("act", lambda e, j=j, bk=bk: e.copy(out=B.v_tok[:, j, :], in_=bk[:, :]), reads=[bk], writes=[B.v_tok])
            pfree(bk)
            yield
        w_release(u)
        W5, u = w_use(5)
        bk = fm_mm(W5, 0, xnT, TB)
        s0_ = tiles[0] % NKV
        P.op("act", lambda e, bk=bk: e.copy(out=kT_all[:, s0_ * 128:s0_ * 128 + TB], in_=bk[:, 0:TB]), reads=[bk],
             writes=[kT_all])
        pfree(bk)
        yield
        for j, n in enumerate(tiles):
            bk = tm_mm(W5, 256, xnT, j)
            ns = n % NKV
            if sample or n == NT - 1:
                P.op("act", lambda e, bk=bk: e.copy(out=B.kvout[:, :], in_=bk[:, 0:256]), reads=[bk], writes=[B.kvout])
                P.op("dve", lambda e, ns=ns: e.tensor_copy(out=vS_all[:, ns, :], in_=B.kvout[:, 128:256]),
                     reads=[B.kvout], writes=[vS_all])
                if sample:
                    final_toks.append(P.dma(ks_d[:, 120:128, :], B.kvout[:, 0:128], reads=[B.kvout], sem="kvo"))
                    final_toks.append(P.dma(vs_d[:, 120:128, :], B.kvout[:, 128:256], reads=[B.kvout], sem="kvo"))
                else:
                    final_toks.append(P.dma(kp_d, B.kvout[:, 0:128], reads=[B.kvout], sem="kvo"))
                    final_toks.append(P.dma(vp_d, B.kvout[:, 128:256], reads=[B.kvout], sem="kvo"))
            else:
                P.op("dve", lambda e, ns=ns, bk=bk: e.tensor_copy(out=vS_all[:, ns, :], in_=bk[:, 128:256]), reads=[bk],
                     writes=[vS_all])
            pfree(bk)
            yield
        w_release(u)

    def hg_front(B, j, cmask):
        cs = slice(j * 128, (j + 1) * 128)
        bkA = psum()
        P.op("pe", [lambda e, h=h: e.matmul(bkA[:, h * 128:(h + 1) * 128], lhsT=B.ktT[:, h, cs], rhs=B.qtT[:, h, cs],
                                            start=True, stop=True) for h in range(4)],
             reads=[B.ktT, B.qtT], writes=[bkA])
        P.op("dve", lambda e: e.tensor_tensor(out=B.ATm[:, :], in0=bkA[:, :], in1=cmask[:, :], op=ALU.mult),
             reads=[bkA, cmask], writes=[B.ATm])
        pfree(bkA)
        bkT = psum()
        bkTb = bkT.ap.bitcast(BF16)
        P.op("pe", [lambda e, h=h: e.transpose(bkTb[:, h * 128:(h + 1) * 128], B.ktT[:, h, cs], C.ident[:, :])
                    for h in range(4)], reads=[B.ktT, C.ident], writes=[bkT])
        P.op("act", lambda e: e.copy(out=B.kt_tok[:, :], in_=bkTb[:, 0:512]), reads=[bkT], writes=[B.kt_tok])
        pfree(bkT)

    def hg_square(B, oaps, obufs):
        ub = list({id(b): b for b in obufs}.values())
        if len(ub) == 1:
            P.op("act", lambda e: e.activation(out=B.sqn[:, :], in_=ub[0][:, :], func=AF.Square), reads=ub,
                 writes=[B.sqn])
        else:
            P.op("act", [lambda e, h=h: e.activation(out=B.sqn[:, h * 128:(h + 1) * 128], in_=oaps[h], func=AF.Square)
                         for h in range(4)], reads=ub, writes=[B.sqn])

    def hg_norm_out(B, j, oaps, obufs):
        cs = slice(j * 128, (j + 1) * 128)
        ub = list({id(b): b for b in obufs}.values())
        bkM = psum()
        P.op("pe", lambda e: e.matmul(bkM[:, :], lhsT=C.onesm[:, :], rhs=B.sqn[:, :], start=True, stop=True),
             reads=[C.onesm, B.sqn], writes=[bkM])
        P.op("act", lambda e: e.activation(out=B.lnv[:, :], in_=bkM[:, :], func=AF.Ln, bias=C.epsc[:, 0:1]),
             reads=[bkM, C.epsc], writes=[B.lnv])
        pfree(bkM)
        P.op("act", lambda e: e.activation(out=B.rstdn[:, :], in_=B.lnv[:, :], func=AF.Exp, scale=-0.5),
             reads=[B.lnv], writes=[B.rstdn])
        if len(ub) == 1:
            P.op("dve", lambda e: e.tensor_tensor(out=B.t1[:, :], in0=ub[0][:, :], in1=B.rstdn[:, :], op=ALU.mult),
                 reads=ub + [B.rstdn], writes=[B.t1])
        else:
            P.op("dve", [lambda e, h=h: e.tensor_tensor(out=B.t1[:, h * 128:(h + 1) * 128], in0=oaps[h],
                                                        in1=B.rstdn[:, h * 128:(h + 1) * 128], op=ALU.mult)
                         for h in range(4)], reads=ub + [B.rstdn], writes=[B.t1])
        P.op("dve", lambda e: e.tensor_tensor(out=omT[:, 0:4, cs], in0=B.t1[:, :].rearrange("p (a b) -> p a b", a=4),
                                              in1=B.sgT[:, 0:4, cs], op=ALU.mult), reads=[B.t1, B.sgT], writes=[omT])
        for b in ub:
            pfree(b)

    def hg_state(B, j):
        nt = B.TB // 128
        cs = slice(j * 128, (j + 1) * 128)
        bkO = psum()
        fns = []
        for h in range(4):
            hs = slice(h * 128, (h + 1) * 128)
            fns.append(lambda e, h=h, hs=hs: e.matmul(bkO[:, hs], lhsT=B.v_tok[:, j, hs], rhs=B.ATm[:, hs], start=True,
                                                      stop=False))
            fns.append(lambda e, h=h, hs=hs: e.matmul(bkO[:, hs], lhsT=S_bf[:, hs], rhs=B.qtT[:, h, cs], start=False,
                                                      stop=True))
        P.op("pe", fns, reads=[B.v_tok, B.ATm, S_bf, B.qtT], writes=[bkO])
        bkP = psum()
        P.op("pe", [lambda e, hs=slice(h * 128, (h + 1) * 128): e.matmul(bkP[:, hs], lhsT=B.kt_tok[:, hs],
                                                                         rhs=B.v_tok[:, j, hs], start=True, stop=True)
                    for h in range(4)], reads=[B.kt_tok, B.v_tok], writes=[bkP])
        for h in range(4):
            hs = slice(h * 128, (h + 1) * 128)
            dc = B.dec[:, h * nt + j:h * nt + j + 1]
            P.op("act", lambda e, h=h, hs=hs, dc=dc: e.activation(out=B.Pd[h][:, :], in_=bkP[:, hs], func=AF.Identity,
                                                                  scale=dc), reads=[bkP, B.dec], writes=[B.Pd[h]])
            P.op("dve", lambda e, h=h, dc=dc: e.scalar_tensor_tensor(out=S[h][:, :], in0=S[h][:, :], scalar=dc,
                                                                     in1=B.Pd[h][:, :], op0=ALU.mult, op1=ALU.add),
                 reads=[B.Pd[h], B.dec], writes=[S[h]])
        pfree(bkP)
        P.op("act", lambda e: e.copy(out=S_bf[:, :], in_=Sall[:, :]), reads=S, writes=[S_bf])
        hg_square(B, None, [bkO])
        return bkO

    def swa_finish(B, j, bkO, bkD, sample=False):
        cs = slice(j * 128, (j + 1) * 128)
        if sample:
            dv = B.dsum[:, :].rearrange("p (s g i) -> p s g i", s=16, g=4)
            bv = bkD[:, :].rearrange("p (s g i) -> p s g i", s=16, g=4)
            P.op("dve", [lambda e, g=g: e.tensor_scalar(out=dv[:, :, g, :], in0=bv[:, :, g, :],
                                                        scalar1=C.esinkT[:, g:g + 1], scalar2=None, op0=ALU.add)
                         for g in range(4)], reads=[bkD, C.esinkT], writes=[B.dsum])
        else:
            P.op("dve", [lambda e, g=g: e.tensor_scalar(out=B.dsum[:, g * 128:(g + 1) * 128],
                                                        in0=bkD[:, g * 128:(g + 1) * 128], scalar1=C.esinkT[:, g:g + 1],
                                                        scalar2=None, op0=ALU.add) for g in range(4)],
                 reads=[bkD, C.esinkT], writes=[B.dsum])
        pfree(bkD)
        P.op("dve", lambda e: e.reciprocal(out=B.rden[:, :], in_=B.dsum[:, :]), reads=[B.dsum], writes=[B.rden])
        if sample:
            P.op("dve", lambda e: e.tensor_tensor(
                out=omT[:, 4:8, cs].rearrange("p g (s i) -> p g s i", i=8),
                in0=bkO[:, :].rearrange("p (s g i) -> p g s i", s=16, g=4),
                in1=B.rden[:, :].rearrange("p (s g i) -> p g s i", s=16, g=4), op=ALU.mult),
                 reads=[bkO, B.rden], writes=[omT])
        else:
            P.op("dve", lambda e: e.tensor_tensor(out=omT[:, 4:8, cs], in0=bkO[:, :].rearrange("p (a b) -> p a b", a=4),
                                                  in1=B.rden[:, :].rearrange("p (a b) -> p a b", a=4), op=ALU.mult),
                 reads=[bkO, B.rden], writes=[omT])
        pfree(bkO)

    ptc = {"i": 0}

    def swa_st(B, rows, kcols, qrhs, mask):
        bkS = psum()
        P.op("pe", lambda e: e.matmul(bkS[:, :], lhsT=kT_all[rows, kcols], rhs=qrhs, start=True, stop=True),
             reads=[kT_all, B.qsT], writes=[bkS])
        pt = B.PT[ptc["i"] % 4]
        ptc["i"] += 1
        P.op("act", lambda e: e.activation(out=pt[:, :], in_=bkS[:, :], func=AF.Exp, scale=0.125), reads=[bkS],
             writes=[pt])
        pfree(bkS)
        P.op("dve", lambda e: e.tensor_tensor(out=pt[:, :], in0=pt[:, :], in1=mask[:, :], op=ALU.mult), reads=[mask],
             writes=[pt])
        return pt

    def p3_tile_steps(B, j, n):
        cs = slice(j * 128, (j + 1) * 128)
        blocks = []
        for kv in range(2):
            rows = slice(64 * kv, 64 * kv + 64)
            bl = ([(n - 1, C.swm_prev)] if n > 0 else []) + [(n, C.swm_cur)]
            for bi, (kt, mask) in enumerate(bl):
                blocks.append((rows, kt % NKV, mask, bi == 0, bi == len(bl) - 1))

        def st(bi):
            rows, ks, mask, _, _ = blocks[bi]
            return swa_st(B, rows, slice(ks * 128, (ks + 1) * 128), B.qsT[rows, 0:4, cs], mask)

        def pv(bi, pt):
            rows, ks, mask, first, last = blocks[bi]
            P.op("pe", [lambda e: e.matmul(swO[rows, :], lhsT=vS_all[:, ks, rows], rhs=pt[:, :], start=first, stop=last),
                        lambda e: e.matmul(swD[rows, :], lhsT=C.ones64[:, :], rhs=pt[:, :], start=first, stop=last)],
                 reads=[vS_all, pt, C.ones64], writes=[swO, swD])

        hg_front(B, j, C.cmask)
        yield
        swO = psum()
        swD = psum()
        pts = {0: st(0)}
        yield
        hgO = hg_state(B, j)
        yield
        pv(0, pts[0])
        pts[1] = st(1)
        yield
        hg_norm_out(B, j, [hgO[:, h * 128:(h + 1) * 128] for h in range(4)], [hgO] * 4)
        yield
        for bi in range(1, len(blocks)):
            pv(bi, pts[bi])
            if bi + 1 < len(blocks):
                pts[bi + 1] = st(bi + 1)
            yield
        swa_finish(B, j, swO, swD)
        yield

    def head3_steps(B, tiles):
        for j, n in enumerate(tiles):
            for _ in p3_tile_steps(B, j, n):
                yield
        if tiles[-1] == NT - 1:
            for h in range(4):
                final_toks.append(P.dma(sp_d[h], S[h].ap, reads=[S[h]], sem="spo"))

    def p3_sample(B):
        TS = NT % NKV
        pool1(lambda e: e.memset(B.smask_s[:, :], 1.0), [B.smask_s])
        pool1(lambda e: e.memset(B.smask_s[:, :].rearrange("p (c t) -> p c t", t=8)[:, :, 0:1], 0.0), [B.smask_s])
        for mk, pa, pb in ((B.cmask_s, [[0, 4], [8, 16], [1, 8]], [[0, 4], [-8, 16], [0, 8]]),
                           (B.swm_scur, [[8, 16], [0, 4], [1, 8]], [[-8, 16], [0, 4], [0, 8]])):
            pool1(lambda e, mk=mk: e.memset(mk[:, :], 1.0), [mk])
            pool1(lambda e, mk=mk, pa=pa: e.affine_select(out=mk[:, :], in_=mk[:, :], pattern=pa, base=0,
                                                          channel_multiplier=-1, compare_op=ALU.is_ge, fill=0.0), [mk])
            pool1(lambda e, mk=mk, pb=pb: e.affine_select(out=mk[:, :], in_=mk[:, :], pattern=pb, base=0,
                                                          channel_multiplier=1, compare_op=ALU.is_ge, fill=0.0), [mk])
        pool1(lambda e: e.memset(B.swm_c[:, :], 1.0), [B.swm_c])
        pool1(lambda e: e.affine_select(out=B.swm_c[:, :], in_=B.swm_c[:, :], pattern=[[0, 64], [-1, 8]], base=-1,
                                        channel_multiplier=1, compare_op=ALU.is_ge, fill=0.0), [B.swm_c])

    def p3_sample_body(B):
        TS = NT % NKV
        for c4 in range(4):
            k32 = B.ck32[c4 % 2]
            v32 = B.cv32[c4 % 2]
            P.dma(k32.ap, ck_d[c4 * 4:(c4 + 1) * 4].rearrange("s k c -> k s c"), writes=[k32], sem="ck%d" % (c4 % 2))
            P.dma(v32.ap, cv_d[c4 * 4:(c4 + 1) * 4].rearrange("s k c -> k s c"), writes=[v32], sem="cv%d" % (c4 % 2))
            P.op("dve", lambda e, c4=c4, k32=k32: e.tensor_copy(out=B.ckb[:, c4 * 4:(c4 + 1) * 4, :], in_=k32[:, :, :]),
                 reads=[k32], writes=[B.ckb])
            P.op("pool", lambda e, c4=c4, v32=v32: e.tensor_copy(out=B.cvb[:, c4 * 4:(c4 + 1) * 4, :], in_=v32[:, :, :]),
                 reads=[v32], writes=[B.cvb])
        for half in range(2):
            bk = psum()
            bkb = bk.ap.bitcast(BF16)
            P.op("pe", [lambda e, i=i: e.transpose(bkb[:, i * 128:(i + 1) * 128], B.ckb[:, half * 8 + i, :], C.ident[:, :])
                        for i in range(8)], reads=[B.ckb, C.ident], writes=[bk])
            P.op("act", lambda e, bkb=bkb: e.copy(out=B.kcT[:, half * 8:(half + 1) * 8, :],
                                                  in_=bkb.rearrange("p (a b) -> p a b", a=8)), reads=[bk],
                 writes=[B.kcT])
            pfree(bk)
        final_toks.append(P.dma(ks_d[:, 0:120, :], ck_d[:, 8:128, :], sem="kvc"))
        final_toks.append(P.dma(vs_d[:, 0:120, :], cv_d[:, 8:128, :], sem="kvc"))

        hg_front(B, 0, B.cmask_s)
        bkO = [psum() for _ in range(4)]
        P.op("pe", [lambda e, h=h: e.matmul(bkO[h][:, 0:128], lhsT=B.v_tok[:, 0, h * 128:(h + 1) * 128],
                                            rhs=B.ATm[:, h * 128:(h + 1) * 128], start=True, stop=False)
                    for h in range(4)], reads=[B.v_tok, B.ATm], writes=bkO)
        def stage_a(seq):
            s0 = B.S0[seq % 3]
            s0b = B.S0b[seq % 2]
            ktm = B.ktm[seq % 2]
            P.dma(s0.ap.rearrange("p (h v) -> p h v", h=4), st_d[seq].rearrange("h d v -> d h v"), writes=[s0],
                  sem="s0%d" % (seq % 3))
            P.op("pool", lambda e: e.tensor_copy(out=s0b[:, :], in_=s0[:, :]), reads=[s0], writes=[s0b])
            P.op("pe", [lambda e, h=h: e.matmul(bkO[h][:, seq * 8:(seq + 1) * 8], lhsT=s0b[:, h * 128:(h + 1) * 128],
                                                rhs=B.qtT[:, h, seq * 8:(seq + 1) * 8], start=False, stop=(seq == 15))
                        for h in range(4)], reads=[s0b, B.qtT], writes=bkO)
            P.op("dve", lambda e: e.tensor_scalar(out=ktm[:, :], in0=B.kt_tok[:, :], scalar1=C.seqmask[:, seq:seq + 1],
                                                  scalar2=None, op0=ALU.mult), reads=[B.kt_tok, C.seqmask], writes=[ktm])
            bkP = psum()
            P.op("pe", [lambda e, hs=slice(h * 128, (h + 1) * 128): e.matmul(bkP[:, hs], lhsT=ktm[:, hs],
                                                                             rhs=B.v_tok[:, 0, hs], start=True, stop=True)
                        for h in range(4)], reads=[ktm, B.v_tok], writes=[bkP])
            return bkP

        def stage_b(seq, bkP):
            s0 = B.S0[seq % 3]
            sn = B.Sn[seq % 2]
            for h in range(4):
                hs = slice(h * 128, (h + 1) * 128)
                dc = B.dec[:, h * 16 + seq:h * 16 + seq + 1]
                P.op("act", lambda e, h=h, hs=hs, dc=dc: e.activation(out=B.Pd[h][:, :], in_=bkP[:, hs], func=AF.Identity,
                                                                      scale=dc), reads=[bkP, B.dec], writes=[B.Pd[h]])
                P.op("dve", lambda e, h=h, hs=hs, dc=dc: e.scalar_tensor_tensor(out=sn[h][:, :], in0=s0[:, hs], scalar=dc,
                                                                                in1=B.Pd[h][:, :], op0=ALU.mult,
                                                                                op1=ALU.add),
                     reads=[s0, B.Pd[h], B.dec], writes=[sn[h]])
            pfree(bkP)
            final_toks.append(P.dma(ss_d[seq].rearrange("h d v -> d h v"), B.SnT[seq % 2], reads=sn,
                                    sem="sn%d" % (seq % 2)))

        pend = {0: stage_a(0)}
        for seq in range(16):
            if seq + 1 < 16:
                pend[seq + 1] = stage_a(seq + 1)
            stage_b(seq, pend.pop(seq))
            yield
        oaps = [bkO[h][:, 0:128] for h in range(4)]
        hg_square(B, oaps, bkO)
        hg_norm_out(B, 0, oaps, bkO)

        bO = psum()
        bD = psum()
        for kv in range(2):
            rows = slice(64 * kv, 64 * kv + 64)
            ptn = swa_st(B, rows, slice(TS * 128, (TS + 1) * 128),
                         B.qsT[rows, 0:4, 0:128].rearrange("p g (s i) -> p s g i", i=8), B.swm_scur)
            bkC = psum()
            P.op("pe", [lambda e, s=s: e.matmul(bkC[:, s * 32:(s + 1) * 32], lhsT=B.kcT[rows, s, :],
                                                rhs=B.qsT[rows, 0:4, s * 8:(s + 1) * 8], start=True, stop=True)
                        for s in range(16)], reads=[B.kcT, B.qsT], writes=[bkC])
            pc = B.PT[ptc["i"] % 4]
            ptc["i"] += 1
            P.op("act", lambda e, pc=pc, bkC=bkC: e.activation(out=pc[:, :], in_=bkC[:, :], func=AF.Exp, scale=0.125),
                 reads=[bkC], writes=[pc])
            pfree(bkC)
            P.op("pool", lambda e, pc=pc: e.tensor_tensor(out=pc[:, :], in0=pc[:, :], in1=B.swm_c[:, :], op=ALU.mult),
                 reads=[B.swm_c], writes=[pc])
            fns = [lambda e: e.matmul(bO[rows, :], lhsT=vS_all[:, TS, rows], rhs=ptn[:, :], start=True, stop=False),
                   lambda e: e.matmul(bD[rows, :], lhsT=C.ones64[:, :], rhs=ptn[:, :], start=True, stop=False)]
            for s in range(16):
                fns.append(lambda e, s=s: e.matmul(bO[rows, s * 32:(s + 1) * 32], lhsT=B.cvb[:, s, rows],
                                                   rhs=pc[:, s * 32:(s + 1) * 32], start=False, stop=(s == 15)))
                fns.append(lambda e, s=s: e.matmul(bD[rows, s * 32:(s + 1) * 32], lhsT=C.ones64[:, :],
                                                   rhs=pc[:, s * 32:(s + 1) * 32], start=False, stop=(s == 15)))
            P.op("pe", fns, reads=[vS_all, ptn, pc, B.cvb, C.ones64], writes=[bO, bD])
        swa_finish(B, 0, bO, bD, sample=True)
        yield

    def p45(B, slots):
        nt = B.TB // 128
        W6, u6 = w_use(6)
        W7, u7 = w_use(7)
        pend = []
        for j in range(nt):
            for nh, W in enumerate((W6, W7)):
                bk = psum()
                P.op("pe", [lambda e, k=k, W=W, bk=bk: e.matmul(bk[:, :], lhsT=omT[:, k, j * 128:(j + 1) * 128],
                                                                rhs=W[:, k, :], start=(k == 0), stop=(k == 7))
                            for k in range(8)], reads=[omT, W], writes=[bk])
                xs_ = slots[j]
                P.op("dve", lambda e, nh=nh, bk=bk, xs_=xs_: e.tensor_tensor(
                    out=xs_[:, nh * 512:(nh + 1) * 512], in0=bk[:, :], in1=xs_[:, nh * 512:(nh + 1) * 512], op=ALU.add),
                     reads=[bk], writes=[xs_])
                pfree(bk)
            xb = norm_pre(slots[j])
            pend.append((xb, j))
            if len(pend) > 2:
                xb0, j0 = pend.pop(0)
                norm_post(xb0, hnT, j0)
        w_release(u6)
        w_release(u7)
        return pend

    def mlp_steps(B, slots, xb_last):
        TB = B.TB
        nt = TB // 128
        for xb0, j0 in xb_last:
            norm_post(xb0, hnT, j0)
            yield
        for q in range(4):
            for rr in range(2):
                W, u = w_use(8 + 2 * q + rr)
                for qq in range(4):
                    fl = rr * 4 + qq
                    bk = fm_mm(W, qq * 128, hnT, TB)
                    rl = B.rl[fl % 2]
                    P.op("act", lambda e, bk=bk, rl=rl: e.activation(out=rl[:, 0:TB], in_=bk[:, 0:TB], func=AF.Relu),
                         reads=[bk], writes=[rl])
                    pfree(bk)
                    P.op("act", lambda e, fl=fl, rl=rl: e.activation(out=aT[:, fl, 0:TB], in_=rl[:, 0:TB],
                                                                     func=AF.Square), reads=[rl], writes=[aT])
                    yield
                w_release(u)
            for nh in range(2):
                W, u = w_use(16 + nh * 4 + q)
                for j in range(nt):
                    bk = psum()
                    P.op("pe", [lambda e, f=f, W=W, j=j, bk=bk: e.matmul(
                        bk[:, :], lhsT=aT[:, f, j * 128:(j + 1) * 128], rhs=W[:, f, :], start=(f == 0), stop=(f == 7))
                        for f in range(8)], reads=[aT, W], writes=[bk])
                    xs_ = slots[j]
                    P.op("dve", lambda e, nh=nh, bk=bk, xs_=xs_: e.tensor_tensor(
                        out=xs_[:, nh * 512:(nh + 1) * 512], in0=bk[:, :], in1=xs_[:, nh * 512:(nh + 1) * 512],
                        op=ALU.add), reads=[bk], writes=[xs_])
                    pfree(bk)
                    yield
                w_release(u)

    def xres_index(b):
        for i_, bb in enumerate(xres):
            if bb is b:
                return i_
        raise AssertionError("not an xres slot")

    def p8(slots, ydst):
        for j in range(len(slots)):
            rs = norm_stats(slots[j])
            xs_ = slots[j]
            P.op("dve", lambda e, xs_=xs_, rs=rs: e.scalar_tensor_tensor(
                out=xs_[:, :], in0=xs_[:, :], scalar=rs[:, 0:1], in1=C.gfin[:, :], op0=ALU.mult, op1=ALU.mult),
                 reads=[rs, C.gfin], writes=[xs_])
            final_toks.append(P.dma(ydst[j * 128:(j + 1) * 128, :], xs_.ap, reads=[xs_], sem="yx%d" % xres_index(xs_)))

    def drain(gen):
        for _ in gen:
            pass

    def interleave(genA, genB):
        a_ok = b_ok = True
        while a_ok or b_ok:
            if b_ok:
                try:
                    next(genB)
                except StopIteration:
                    b_ok = False
            if a_ok:
                try:
                    next(genA)
                except StopIteration:
                    a_ok = False

    def chain(*gens):
        for g_ in gens:
            for _ in g_:
                yield

    def main_schedule():
        xload = {}

        def load_x(n):
            s = xres[n % NXS]
            P.dma(s.ap, x_d[n * 128:(n + 1) * 128, :], writes=[s], sem="x%d" % (n % NXS))
            xload[n] = s

        def gtiles(g):
            return list(range(g * G, (g + 1) * G))

        Bs = carve(128, True)
        REG["B"] = Bs
        xsmp = xres[7]
        P.dma(xsmp.ap, xs_d, writes=[xsmp], sem="x7")
        w_prefetch()
        p3_sample(Bs)
        drain(head12_steps(Bs, [NT], [xsmp], True))
        chk("p2s")
        drain(p3_sample_body(Bs))
        chk("p3s")
        P.wait_engines(["act", "dve", "pool", "pe", "sp"], ["act", "dve", "pool", "pe"])
        wait_final()
        Bp = carve(512, False)
        REG["B"] = Bp
        for n in gtiles(0):
            load_x(n)

        def sample_tail():
            xb_last = p45(Bs, [xsmp])
            yield
            for _ in mlp_steps(Bs, [xsmp], xb_last):
                yield
            p8([xsmp], ys_d)
            yield

        interleave(sample_tail(),
                   chain(head12_steps(Bp, gtiles(0), [xload[n] for n in gtiles(0)], False), head3_steps(Bp, gtiles(0))))
        chk("h0")
        for g in range(NG):
            slots = [xload[n] for n in gtiles(g)]
            if g + 1 < NG:
                for n in gtiles(g + 1):
                    load_x(n)
            xb_last = p45(Bp, slots)
            genA = mlp_steps(Bp, slots, xb_last)
            if g + 1 < NG:
                nslots = [xload[n] for n in gtiles(g + 1)]
                genB = chain(head12_steps(Bp, gtiles(g + 1), nslots, False), head3_steps(Bp, gtiles(g + 1)))
                interleave(genA, genB)
            else:
                drain(genA)
            p8(slots, y_d[g * G * 128:(g + 1) * G * 128, :])
            chk("g%d" % g)

    try:
        main_schedule()
    except _Stop:
        pass
    REG.update(dict(xnT=xnT, hnT=hnT, omT=omT, kT_all=kT_all, vS_all=vS_all, S_bf=S_bf, xres0=xres[0], xres1=xres[1],
                    lb=C.lb, seqmask=C.seqmask))
    for dn in dumps:
        b = REG[dn] if dn in REG else getattr(REG["B"], dn)
        shp = [int(v) for v in b.ap.shape]
        dd = nc.dram_tensor("dbg_" + dn, shp, b.ap.dtype, kind="ExternalOutput").ap()
        final_toks.append(P.dma(dd, b.ap, reads=[b], sem="dbg"))

    wait_final()
    P.wait_engines(["sp"], ["act", "dve", "pool", "pe"])
    return nc, WORDER


def _chunk_k(a):
    return np.ascontiguousarray(a.reshape(8, 128, 512).transpose(1, 0, 2)).reshape(128, 4096)


def _build_wall(w_in, w_out, w_up, w_down):
    zq, zf, zi, zg = w_in[:, 0:512], w_in[:, 512:1024], w_in[:, 1024:1536], w_in[:, 1536:2048]
    sq, sk, sv = w_in[:, 2048:2560], w_in[:, 2560:2688], w_in[:, 2688:2816]
    sqP = np.concatenate([np.concatenate([sq[:, g * 64:(g + 1) * 64], sq[:, (4 + g) * 64:(5 + g) * 64]], axis=1)
                          for g in range(4)], axis=1)
    c5 = np.concatenate([sk, sv, np.zeros((1024, 256), np.float32)], axis=1)
    chunks = [_chunk_k(a) for a in (zq, zg, zf, sqP, zi, c5)]
    perm = list(range(512)) + [512 + (kv * 4 + g) * 64 + d for g in range(4) for kv in range(2) for d in range(64)]
    wo = w_out[perm]
    chunks += [_chunk_k(wo[:, nh * 512:(nh + 1) * 512]) for nh in range(2)]
    chunks += [_chunk_k(w_up[:, r * 512:(r + 1) * 512]) for r in range(8)]
    for nh in range(2):
        for r4 in range(4):
            chunks.append(_chunk_k(w_down[r4 * 1024:(r4 + 1) * 1024, nh * 512:(nh + 1) * 512]))
    return np.ascontiguousarray(np.stack(chunks, axis=0), dtype=np.float32)


def _make_in_maps(x_prompt, x_sample, state_hgrn, cache_swa_k, cache_swa_v, ln_mix, w_in, lb_logits, hg_norm, sinks,
           w_out, ln_mlp, w_up, w_down, ln_final):
    f = lambda a: np.ascontiguousarray(np.asarray(a), dtype=np.float32)
    x_prompt, x_sample, state_hgrn = f(x_prompt), f(x_sample), f(state_hgrn)
    cache_swa_k, cache_swa_v = f(cache_swa_k), f(cache_swa_v)
    wall = _build_wall(f(w_in)[0], f(w_out)[0], f(w_up)[0], f(w_down)[0])
    lnmix = np.ascontiguousarray(f(ln_mix)[0].reshape(8, 128).T)
    lnmlp = np.ascontiguousarray(f(ln_mlp)[0].reshape(8, 128).T)
    lbl = np.ascontiguousarray(f(lb_logits).reshape(2, 4, 128).transpose(2, 0, 1).reshape(128, 8))
    hgn = np.ascontiguousarray(f(hg_norm)[0].reshape(128, 1))
    sk_ = f(sinks)[0]
    sinkT = np.ascontiguousarray(np.stack([sk_[(p // 64) * 4:(p // 64) * 4 + 4] for p in range(128)], axis=0))
    gfin = np.ascontiguousarray(np.broadcast_to(f(ln_final)[None, :], (128, D)))
    in_maps = []
    for c in range(NCORES):
        in_maps.append({
            "x": x_prompt[c],
            "xs": np.ascontiguousarray(x_sample[16 * c:16 * (c + 1)].reshape(128, D)),
            "st": state_hgrn[0, 16 * c:16 * (c + 1)],
            "ck": np.ascontiguousarray(cache_swa_k[0, 16 * c:16 * (c + 1)].reshape(16, 128, 128)),
            "cv": np.ascontiguousarray(cache_swa_v[0, 16 * c:16 * (c + 1)].reshape(16, 128, 128)),
            "wall": wall, "lnmix": lnmix, "lnmlp": lnmlp, "lbl": lbl, "hgn": hgn, "sinkT": sinkT, "gfin": gfin,
        })
    return in_maps


def kernel(x_prompt, x_sample, state_hgrn, cache_swa_k, cache_swa_v, ln_mix, w_in, lb_logits, hg_norm, sinks,
           w_out, ln_mlp, w_up, w_down, ln_final):
    in_maps = _make_in_maps(x_prompt, x_sample, state_hgrn, cache_swa_k, cache_swa_v, ln_mix, w_in, lb_logits, hg_norm,
                            sinks, w_out, ln_mlp, w_up, w_down, ln_final)
    _, order = build_program()
    nc, _ = build_program(worder=order)
    res = run_bass_kernel_spmd(nc, in_maps, core_ids=list(range(NCORES)))
    R = res.results
    y_prompt = np.stack([R[c]["y"] for c in range(NCORES)], axis=0).reshape(8, 4096, D)
    y_sample = np.concatenate([R[c]["ys"].reshape(16, 8, D) for c in range(NCORES)], axis=0)
    sp = np.stack([R[c]["sp"] for c in range(NCORES)], axis=0)[None]
    kp = np.stack([R[c]["kp"].reshape(128, 2, 64) for c in range(NCORES)], axis=0)[None]
    vp = np.stack([R[c]["vp"].reshape(128, 2, 64) for c in range(NCORES)], axis=0)[None]
    ss = np.concatenate([R[c]["ss"] for c in range(NCORES)], axis=0)[None]
    ks = np.concatenate([R[c]["ks"].reshape(16, 128, 2, 64) for c in range(NCORES)], axis=0)[None]
    vs = np.concatenate([R[c]["vs"].reshape(16, 128, 2, 64) for c in range(NCORES)], axis=0)[None]
    out = (y_prompt, y_sample, sp, kp, vp, ss, ks, vs)
    return tuple(np.ascontiguousarray(o, dtype=np.float32) for o in out)
```

```python
import os
import numpy as np
import concourse.bass as bass
import concourse.mybir as mybir
from concourse.bass_utils import run_bass_kernel_spmd

F32 = mybir.dt.float32
BF16 = mybir.dt.bfloat16
AF = mybir.ActivationFunctionType
ALU = mybir.AluOpType

NCORES = 8
D = 1024
NT = 32
G = 4
NG = NT // G
NSLOT = 5
NCH = 24
EPS = 1e-6


class Buf:
    __slots__ = ("ap", "w", "r")

    def __init__(self, ap):
        self.ap = ap
        self.w = None
        self.r = {}

    def __getitem__(self, k):
        return self.ap[k]


class Eng:
    def __init__(self, h, semname):
        self.h = h
        self.semname = semname
        self.n = 0
        self.waited = {}


class Prog:
    def __init__(self, nc):
        self.nc = nc
        self.sems = {}
        self.dcnt = {}
        self.E = {}
        for name, h in (("pe", nc.tensor), ("act", nc.scalar), ("dve", nc.vector),
                        ("pool", nc.gpsimd), ("sp", nc.sync)):
            self.sems["e_" + name] = nc.alloc_semaphore("e_" + name)
            self.E[name] = Eng(h, "e_" + name)

    def dsem(self, name):
        if name not in self.sems:
            self.sems[name] = self.nc.alloc_semaphore(name)
            self.dcnt[name] = 0
        return name

    def _wait(self, e, tok):
        if tok is None:
            return
        sn, v = tok
        if e is self.E["pe"] and sn == "e_pe":
            return
        if e.waited.get(sn, 0) >= v:
            return
        e.h.wait_ge(self.sems[sn], v)
        e.waited[sn] = v

    def _deps(self, e, reads, writes):
        for b in reads:
            self._wait(e, b.w)
        for b in writes:
            self._wait(e, b.w)
            for sn, v in b.r.items():
                self._wait(e, (sn, v))

    @staticmethod
    def _mark(tok, reads, writes):
        sn, v = tok
        for b in reads:
            if b.r.get(sn, 0) < v:
                b.r[sn] = v
        for b in writes:
            b.w = tok
            b.r = {}

    def op(self, en, fns, reads=(), writes=()):
        e = self.E[en]
        if callable(fns):
            fns = [fns]
        self._deps(e, reads, writes)
        ins = None
        for f in fns:
            ins = f(e.h)
        ins.then_inc(self.sems[e.semname], 1)
        e.n += 1
        tok = (e.semname, e.n)
        self._mark(tok, reads, writes)
        return tok

    def dma(self, out, in_, reads=(), writes=(), sem=None):
        e = self.E["sp"]
        self.dsem(sem)
        self._deps(e, reads, writes)
        ins = e.h.dma_start(out=out, in_=in_)
        ins.then_inc(self.sems[sem], 16)
        self.dcnt[sem] += 16
        tok = (sem, self.dcnt[sem])
        self._mark(tok, reads, writes)
        return tok

    def wait_engines(self, waiters, targets):
        for wn in waiters:
            e = self.E[wn]
            for tn in targets:
                t = self.E[tn]
                if t.n > 0:
                    self._wait(e, (t.semname, t.n))


class Carver:
    def __init__(self, pool_ap, nbytes):
        self.pool = pool_ap
        self.nbytes = nbytes
        self.off = 0

    def take(self, free_shape, dt):
        n = int(np.prod(free_shape))
        nb = n * (4 if dt == F32 else 2)
        s = self.off // 2
        assert self.off + nb <= self.nbytes, ("carve overflow", self.off, nb, self.nbytes)
        ap = self.pool[:, s:s + nb // 2]
        if dt == F32:
            ap = ap.bitcast(F32)
        if len(free_shape) > 1:
            names = "abcd"[:len(free_shape)]
            ap = ap.rearrange("p (%s) -> p %s" % (" ".join(names), " ".join(names)),
                              **{k: int(v) for k, v in zip(names, free_shape)})
        self.off += (nb + 31) // 32 * 32
        return Buf(ap)


class NS:
    pass


class _Stop(Exception):
    pass


def build_program(worder=None, stop=None, dumps=()):
    record = worder is None
    WORDER = [] if record else list(worder)
    nc = bass.Bass("TRN2", target_bir_lowering=False)
    P = Prog(nc)
    REG = {}

    def chk(stage):
        if stop == stage:
            raise _Stop()

    def din(name, shape, dt=F32):
        return nc.dram_tensor(name, list(shape), dt, kind="ExternalInput").ap()

    def dout(name, shape, dt=F32):
        return nc.dram_tensor(name, list(shape), dt, kind="ExternalOutput").ap()

    x_d = din("x", [NT * 128, D])
    xs_d = din("xs", [128, D])
    st_d = din("st", [16, 4, 128, 128])
    ck_d = din("ck", [16, 128, 128])
    cv_d = din("cv", [16, 128, 128])
    wall_d = din("wall", [NCH, 128, 4096])
    lnmix_d = din("lnmix", [128, 8])
    lnmlp_d = din("lnmlp", [128, 8])
    lbl_d = din("lbl", [128, 8])
    hgn_d = din("hgn", [128, 1])
    sinkT_d = din("sinkT", [128, 4])
    gfin_d = din("gfin", [128, D])
    y_d = dout("y", [NT * 128, D])
    ys_d = dout("ys", [128, D])
    sp_d = dout("sp", [4, 128, 128])
    kp_d = dout("kp", [128, 128])
    vp_d = dout("vp", [128, 128])
    ss_d = dout("ss", [16, 4, 128, 128])
    ks_d = dout("ks", [16, 128, 128])
    vs_d = dout("vs", [16, 128, 128])
    wbf_d = nc.dram_tensor("wbf", [NCH, 128, 4096], BF16).ap()

    final_toks = []

    def wait_final():
        mx = {}
        for sn_, v_ in final_toks:
            mx[sn_] = max(mx.get(sn_, 0), v_)
        for sn_, v_ in mx.items():
            P._wait(P.E["sp"], (sn_, v_))

    def sb(name, free_shape, dt):
        return Buf(nc.alloc_sbuf_tensor("s_" + name, [128] + list(free_shape), dt)[:])

    C = NS()
    C.ident = sb("ident", [128], BF16)
    C.onesm = sb("onesm", [128], BF16)
    C.ones64 = sb("ones64", [64], BF16)
    C.mhalf = sb("mhalf", [1], F32)
    C.epsc = sb("epsc", [1], F32)
    C.cmask = sb("cmask", [512], F32)
    C.smask = sb("smask", [512], F32)
    C.swm_cur = sb("swm_cur", [512], BF16)
    C.swm_prev = sb("swm_prev", [512], BF16)
    C.seqmask = sb("seqmask", [16], F32)
    C.gfin = sb("gfin", [D], F32)
    C.lnmix = sb("lnmix", [8], F32)
    C.lnmlp = sb("lnmlp", [8], F32)
    C.wos = sb("wos", [8], F32)
    C.lbl = sb("lbl", [8], F32)
    C.hgn = sb("hgn", [1], F32)
    C.sinkT = sb("sinkT", [4], F32)
    C.esinkT = sb("esinkT", [4], F32)
    C.dl = sb("dl", [4], F32)
    C.lb = sb("lb", [4], F32)
    C.oml = sb("oml", [4], F32)
    C.noml = sb("noml", [4], F32)
    C.homl = sb("homl", [4], F32)
    C.nhoml = sb("nhoml", [4], F32)
    C.bfm = sb("bfm", [4], F32)

    NXS = 8
    XR = nc.alloc_sbuf_tensor("XR", [128, NXS * 2048], BF16)
    xres = [Buf(XR[:, i * 2048:(i + 1) * 2048].bitcast(F32)) for i in range(NXS)]
    xnb = [sb("xnb%d" % i, [D], BF16) for i in range(3)]
    xnb_free = [0, 1, 2]
    junk = sb("junk", [D], BF16)
    NST = 6
    stat = [(sb("ssq%d" % i, [1], F32), sb("tv%d" % i, [1], F32), sb("rs%d" % i, [1], F32)) for i in range(NST)]
    xnT = sb("xnT", [8, 512], BF16)
    hnT = sb("hnT", [8, 512], BF16)
    NKV = 8
    kT_all = sb("kT_all", [NKV * 128], BF16)
    vS_all = sb("vS_all", [NKV, 128], BF16)
    Sall = nc.alloc_sbuf_tensor("Sst", [128, 512], F32)
    S = [Buf(Sall[:, h * 128:(h + 1) * 128]) for h in range(4)]
    S_bf = sb("S_bf", [512], BF16)
    omT = sb("omT", [8, 512], BF16)
    aT = sb("aT", [8, 512], BF16)
    wring = [sb("wr%d" % i, [8, 512], BF16) for i in range(NSLOT)]
    kvout_p = sb("kvout", [256], F32)
    rl_p = [sb("rl%d" % i, [512], BF16) for i in range(2)]
    stg32 = [sb("stg%d" % i, [2, 512], F32) for i in range(4)]
    UBYTES = 54 * 1024
    U = nc.alloc_sbuf_tensor("U", [128, UBYTES // 2], BF16)

    banks = [Buf(nc.alloc_psum_tensor("pb%d" % i, [128, 512], F32)[:]) for i in range(8)]
    pfreeq = list(range(8))

    def psum():
        assert pfreeq, "out of PSUM banks"
        return banks[pfreeq.pop(0)]

    def pfree(b):
        for i, bb in enumerate(banks):
            if bb is b:
                assert i not in pfreeq
                pfreeq.append(i)
                return
        raise AssertionError("not a bank")

    def pool1(fn, writes, reads=()):
        return P.op("pool", fn, reads=reads, writes=writes)

    cl = [(C.lnmix, lnmix_d), (C.lnmlp, lnmlp_d), (C.lbl, lbl_d), (C.hgn, hgn_d), (C.sinkT, sinkT_d),
          (C.gfin, gfin_d)]
    for b, d_ in cl:
        P.dma(b.ap, d_, writes=[b], sem="cst")
    ctok = ("cst", P.dcnt["cst"])
    for b, _ in cl:
        b.w = ctok

    pool1(lambda e: e.memset(C.ident[:, :], 0.0), [C.ident])
    pool1(lambda e: e.affine_select(out=C.ident[:, :], in_=C.ident[:, :], pattern=[[-1, 128]], base=0,
                                    channel_multiplier=1, compare_op=ALU.not_equal, fill=1.0), [C.ident])
    pool1(lambda e: e.memset(C.onesm[:, :], 1.0 / 128.0), [C.onesm])
    pool1(lambda e: e.memset(C.ones64[:, :], 1.0), [C.ones64])
    pool1(lambda e: e.memset(C.mhalf[:, :], -0.5), [C.mhalf])
    pool1(lambda e: e.memset(C.epsc[:, :], EPS), [C.epsc])
    pool1(lambda e: e.memset(C.cmask[:, :], 1.0), [C.cmask])
    pool1(lambda e: e.affine_select(out=C.cmask[:, :], in_=C.cmask[:, :], pattern=[[0, 4], [1, 128]], base=0,
                                    channel_multiplier=-1, compare_op=ALU.is_ge, fill=0.0), [C.cmask])
    pool1(lambda e: e.memset(C.smask[:, :], 1.0), [C.smask])
    pool1(lambda e: e.memset(C.smask[:, :].rearrange("p (c t) -> p c t", t=128)[:, :, 0:1], 0.0), [C.smask])
    pool1(lambda e: e.memset(C.swm_cur[:, :], 1.0), [C.swm_cur])
    pool1(lambda e: e.affine_select(out=C.swm_cur[:, :], in_=C.swm_cur[:, :], pattern=[[0, 4], [1, 128]], base=0,
                                    channel_multiplier=-1, compare_op=ALU.is_ge, fill=0.0), [C.swm_cur])
    pool1(lambda e: e.memset(C.swm_prev[:, :], 1.0), [C.swm_prev])
    pool1(lambda e: e.affine_select(out=C.swm_prev[:, :], in_=C.swm_prev[:, :], pattern=[[0, 4], [-1, 128]], base=-1,
                                    channel_multiplier=1, compare_op=ALU.is_ge, fill=0.0), [C.swm_prev])
    pool1(lambda e: e.memset(C.seqmask[:, :], 1.0), [C.seqmask])
    pool1(lambda e: e.affine_select(out=C.seqmask[:, :], in_=C.seqmask[:, :], pattern=[[-8, 16]], base=0,
                                    channel_multiplier=1, compare_op=ALU.is_ge, fill=0.0), [C.seqmask])
    pool1(lambda e: e.affine_select(out=C.seqmask[:, :], in_=C.seqmask[:, :], pattern=[[8, 16]], base=7,
                                    channel_multiplier=-1, compare_op=ALU.is_ge, fill=0.0), [C.seqmask])
    pool1(lambda e: e.memset(C.wos[:, :], 1.0), [C.wos])
    for c in range(4):
        P.op("dve", lambda e, c=c: e.tensor_copy(out=C.wos[:, c:c + 1], in_=C.hgn[:, 0:1]), reads=[C.hgn], writes=[C.wos])
    P.op("dve", lambda e: e.tensor_tensor(out=C.dl[:, :], in0=C.lbl[:, 0:4], in1=C.lbl[:, 4:8], op=ALU.subtract),
         reads=[C.lbl], writes=[C.dl])
    P.op("act", lambda e: e.activation(out=C.lb[:, :], in_=C.dl[:, :], func=AF.Sigmoid), reads=[C.dl], writes=[C.lb])
    P.op("dve", lambda e: e.tensor_scalar(out=C.oml[:, :], in0=C.lb[:, :], scalar1=-1.0, scalar2=1.0, op0=ALU.mult,
                                          op1=ALU.add), reads=[C.lb], writes=[C.oml])
    P.op("dve", lambda e: e.tensor_scalar(out=C.noml[:, :], in0=C.lb[:, :], scalar1=-1.0, scalar2=None, op0=ALU.add),
         reads=[C.lb], writes=[C.noml])
    P.op("act", lambda e: e.activation(out=C.esinkT[:, :], in_=C.sinkT[:, :], func=AF.Exp), reads=[C.sinkT],
         writes=[C.esinkT])
    P.op("dve", lambda e: e.tensor_scalar(out=C.homl[:, :], in0=C.oml[:, :], scalar1=0.5, scalar2=None, op0=ALU.mult),
         reads=[C.oml], writes=[C.homl])
    P.op("dve", lambda e: e.tensor_scalar(out=C.nhoml[:, :], in0=C.oml[:, :], scalar1=-0.5, scalar2=None, op0=ALU.mult),
         reads=[C.oml], writes=[C.nhoml])
    P.op("dve", lambda e: e.tensor_tensor(out=C.bfm[:, :], in0=C.lb[:, :], in1=C.homl[:, :], op=ALU.add),
         reads=[C.lb, C.homl], writes=[C.bfm])
    for h in range(4):
        pool1(lambda e, h=h: e.memset(S[h][:, :], 0.0), [S[h]])
    pool1(lambda e: e.memset(S_bf[:, :], 0.0), [S_bf])

    wslot_free = list(range(NSLOT))
    loaded = {}
    wr = {"next": 0, "use": 0}

    converted = set()
    wbfb = [Buf(None) for _ in range(NCH)]
    pending = []
    qc = {"i": 0}

    def flush_casts():
        while pending:
            cid, s_, qs = pending.pop(0)
            slot = wring[s_]
            sc = C.lnmix if cid < 6 else (C.wos if cid < 8 else (C.lnmlp if cid < 16 else None))
            for qi, st in enumerate(qs):
                en = ("act", "dve")[qi % 2]
                if sc is not None:
                    if en == "act":
                        fns = [lambda e, k=k, st=st: e.activation(out=slot[:, 2 * qi + k, :], in_=st[:, k, :],
                                                                  func=AF.Identity, scale=sc[:, 2 * qi + k:2 * qi + k + 1])
                               for k in range(2)]
                    else:
                        fns = [lambda e, k=k, st=st: e.tensor_scalar(out=slot[:, 2 * qi + k, :], in0=st[:, k, :],
                                                                     scalar1=sc[:, 2 * qi + k:2 * qi + k + 1], scalar2=None,
                                                                     op0=ALU.mult) for k in range(2)]
                    P.op(en, fns, reads=[st, sc], writes=[slot])
                else:
                    if en == "act":
                        P.op(en, lambda e, st=st: e.copy(out=slot[:, 2 * qi:2 * qi + 2, :], in_=st[:, :, :]), reads=[st],
                             writes=[slot])
                    else:
                        P.op(en, lambda e, st=st: e.tensor_copy(out=slot[:, 2 * qi:2 * qi + 2, :], in_=st[:, :, :]),
                             reads=[st], writes=[slot])
            P.dma(wbf_d[cid], slot.ap.rearrange("p a b -> p (a b)"), reads=[slot], writes=[wbfb[cid]], sem="ws%d" % s_)

    def w_load(l):
        cid = WORDER[l]
        s_ = wslot_free.pop(0)
        if cid in converted:
            P.dma(wring[s_].ap.rearrange("p a b -> p (a b)"), wbf_d[cid], reads=[wbfb[cid]], writes=[wring[s_]],
                  sem="wr%d" % s_)
        else:
            flush_casts()
            converted.add(cid)
            qs = []
            for qi in range(4):
                st = stg32[qc["i"] % 4]
                qc["i"] += 1
                P.dma(st.ap.rearrange("p a b -> p (a b)"), wall_d[cid][:, qi * 1024:(qi + 1) * 1024], writes=[st],
                      sem="pl%d" % (qc["i"] % 4))
                qs.append(st)
            pending.append((cid, s_, qs))
        loaded[l] = s_

    def w_prefetch():
        if record:
            return
        while wslot_free and wr["next"] < len(WORDER):
            w_load(wr["next"])
            wr["next"] += 1

    def w_use(cid):
        flush_casts()
        u = wr["use"]
        wr["use"] += 1
        if record:
            WORDER.append(cid)
        else:
            assert WORDER[u] == cid, (u, cid, WORDER[u])
        if u not in loaded:
            assert wr["next"] == u
            w_load(u)
            wr["next"] += 1
        flush_casts()
        w_prefetch()
        return wring[loaded[u]], u

    def w_release(u):
        wslot_free.append(loaded.pop(u))
        w_prefetch()

    def carve(TB, sample):
        cvr = Carver(U[:, :], UBYTES)
        B = NS()
        B.TB = TB
        nt = TB // 128
        B.sg4 = [cvr.take([TB], F32) for _ in range(4)]
        B.gt = cvr.take([TB], F32)
        B.kf = cvr.take([TB], F32)
        B.bt = cvr.take([TB], F32)
        B.Eb = cvr.take([TB], F32)
        B.Enb = cvr.take([TB], F32)
        B.qtT = cvr.take([4, TB], BF16)
        B.ktT = cvr.take([4, TB], BF16)
        B.sgT = cvr.take([4, TB], BF16)
        B.qsT = cvr.take([4, TB], BF16)
        B.v_tok = cvr.take([nt, 512], BF16)
        B.dec = cvr.take([64], F32)
        B.ATm = cvr.take([512], BF16)
        B.kt_tok = cvr.take([512], BF16)
        pd = cvr.take([512], F32)
        B.Pd = [Buf(pd[:, h * 128:(h + 1) * 128]) for h in range(4)]
        B.sqn = cvr.take([512], BF16)
        B.lnv = cvr.take([512], F32)
        B.rstdn = B.lnv
        B.t1 = cvr.take([512], BF16)
        B.PT = [cvr.take([512], BF16) for _ in range(4)]
        B.dsum = cvr.take([512], F32)
        B.rden = B.dsum
        B.rl = rl_p
        B.kvout = kvout_p
        if sample:
            B.cmask_s = cvr.take([512], F32)
            B.smask_s = cvr.take([128], F32)
            B.swm_scur = cvr.take([512], BF16)
            B.swm_c = cvr.take([512], BF16)
            B.ckb = cvr.take([16, 128], BF16)
            B.cvb = cvr.take([16, 128], BF16)
            B.kcT = cvr.take([16, 128], BF16)
            B.ktm = [cvr.take([512], BF16) for _ in range(2)]
            B.S0b = [cvr.take([512], BF16) for _ in range(2)]
            xc = Carver(XR[:, 4 * 2048:7 * 2048], 3 * 4096)
            B.ck32 = [xc.take([4, 128], F32)] * 2
            B.cv32 = [cvr.take([4, 128], F32)] * 2
            B.S0 = [xc.take([512], F32) for _ in range(3)]
            B.Sn = []
            B.SnT = []
            for _ in range(2):
                t_ = xc.take([512], F32)
                B.Sn.append([Buf(t_[:, h * 128:(h + 1) * 128]) for h in range(4)])
                B.SnT.append(t_.ap.rearrange("p (h v) -> p h v", h=4))
        return B

    stc = {"i": 0, "x": 0, "y": 0}

    def norm_stats(src):
        ssq, tv, rs = stat[stc["i"] % NST]
        stc["i"] += 1
        P.op("act", lambda e: e.activation(out=junk[:, :], in_=src[:, :], func=AF.Square, accum_out=ssq[:, 0:1]),
             reads=[src], writes=[ssq, junk])
        P.op("act", lambda e: e.activation(out=tv[:, :], in_=ssq[:, :], func=AF.Ln, scale=1.0 / D, bias=C.epsc[:, 0:1]),
             reads=[ssq, C.epsc], writes=[tv])
        P.op("act", lambda e: e.activation(out=rs[:, :], in_=tv[:, :], func=AF.Exp, scale=-0.5), reads=[tv], writes=[rs])
        return rs

    def norm_pre(src):
        rs = norm_stats(src)
        assert xnb_free, "xnb ring exhausted"
        xb = xnb[xnb_free.pop(0)]
        P.op("dve", lambda e: e.tensor_scalar(out=xb[:, :], in0=src[:, :], scalar1=rs[:, 0:1], scalar2=None,
                                              op0=ALU.mult), reads=[src, rs], writes=[xb])
        return xb

    def norm_post(xb, dstT, j):
        bk = psum()
        bkb = bk.ap.bitcast(BF16)
        P.op("pe", [lambda e, k=k: e.transpose(bkb[:, k * 128:(k + 1) * 128], xb[:, k * 128:(k + 1) * 128], C.ident[:, :])
                    for k in range(8)], reads=[xb, C.ident], writes=[bk])
        P.op("dve", lambda e: e.tensor_copy(out=dstT[:, 0:8, j * 128:(j + 1) * 128],
                                            in_=bkb.rearrange("p (a b) -> p a b", a=8)), reads=[bk], writes=[dstT])
        pfree(bk)
        for i_, b_ in enumerate(xnb):
            if b_ is xb:
                xnb_free.append(i_)

    def fm_mm(W, col0, srcT, TB, ncols=128):
        bk = psum()
        P.op("pe", [lambda e, k=k: e.matmul(bk[0:ncols, 0:TB], lhsT=W[:, k, col0:col0 + ncols], rhs=srcT[:, k, 0:TB],
                                            start=(k == 0), stop=(k == 7)) for k in range(8)],
             reads=[W, srcT], writes=[bk])
        return bk

    def tm_mm(W, ncols, srcT, j):
        bk = psum()
        P.op("pe", [lambda e, k=k: e.matmul(bk[:, 0:ncols], lhsT=srcT[:, k, j * 128:(j + 1) * 128], rhs=W[:, k, 0:ncols],
                                            start=(k == 0), stop=(k == 7)) for k in range(8)],
             reads=[W, srcT], writes=[bk])
        return bk

    def head12_steps(B, tiles, slots, sample):
        TB = B.TB
        nt = TB // 128
        xb_prev = None
        for j in range(nt):
            xb = norm_pre(slots[j])
            if xb_prev is not None:
                norm_post(xb_prev, xnT, j - 1)
            xb_prev = xb
            yield
        norm_post(xb_prev, xnT, nt - 1)
        yield
        W, u = w_use(0)
        for h in range(4):
            bk = fm_mm(W, h * 128, xnT, TB)
            P.op("act", lambda e, h=h, bk=bk: e.activation(out=B.qtT[:, h, :], in_=bk[:, 0:TB], func=AF.Silu),
                 reads=[bk], writes=[B.qtT])
            pfree(bk)
            yield
        w_release(u)
        W, u = w_use(1)
        for h in range(4):
            bk = fm_mm(W, h * 128, xnT, TB)
            P.op("act", lambda e, h=h, bk=bk: e.activation(out=B.sgT[:, h, :], in_=bk[:, 0:TB], func=AF.Silu),
                 reads=[bk], writes=[B.sgT])
            pfree(bk)
            yield
        w_release(u)
        W, u = w_use(2)
        sm = B.smask_s if sample else C.smask
        for hp in range(1):
            for hh in range(4):
                h = hh
                bk = fm_mm(W, h * 128, xnT, TB)
                P.op("act", lambda e, hh=hh, bk=bk: e.activation(out=B.sg4[hh][:, :], in_=bk[:, 0:TB], func=AF.Tanh,
                                                                 scale=0.5), reads=[bk], writes=[B.sg4[hh]])
                pfree(bk)
                yield
            w_release(u)
            for hh in range(4):
                h = hh
                sg = B.sg4[hh]
                P.op("act", lambda e, h=h, sg=sg: e.activation(out=B.gt[:, :], in_=sg[:, :], func=AF.Ln,
                                                               scale=C.homl[:, h:h + 1], bias=C.bfm[:, h:h + 1]),
                     reads=[sg, C.homl, C.bfm], writes=[B.gt])
                P.op("dve", lambda e, h=h, sg=sg: e.tensor_scalar(out=B.kf[:, :], in0=sg[:, :], scalar1=C.nhoml[:, h:h + 1],
                                                                  scalar2=C.homl[:, h:h + 1], op0=ALU.mult, op1=ALU.add),
                     reads=[sg, C.nhoml, C.homl], writes=[B.kf])
                P.op("dve", lambda e: e.tensor_tensor_scan(out=B.bt[:, :], data0=sm[:, 0:TB], data1=B.gt[:, :],
                                                           initial=0.0, op0=ALU.mult, op1=ALU.add), reads=[sm, B.gt],
                     writes=[B.bt])
                P.op("act", lambda e: e.activation(out=B.Eb[:, :], in_=B.bt[:, :], func=AF.Exp), reads=[B.bt],
                     writes=[B.Eb])
                P.op("act", lambda e: e.activation(out=B.Enb[:, :], in_=B.bt[:, :], func=AF.Exp, scale=-1.0),
                     reads=[B.bt], writes=[B.Enb])
                P.op("dve", lambda e, h=h: e.tensor_tensor(out=B.ktT[:, h, :], in0=B.kf[:, :], in1=B.Enb[:, :],
                                                           op=ALU.mult), reads=[B.kf, B.Enb], writes=[B.ktT])
                P.op("dve", lambda e, h=h: e.tensor_tensor(out=B.qtT[:, h, :], in0=B.qtT[:, h, :], in1=B.Eb[:, :],
                                                           op=ALU.mult), reads=[B.Eb], writes=[B.qtT])
                if sample:
                    P.op("act", lambda e, h=h: e.copy(out=B.dec[:, h * 16:(h + 1) * 16],
                                                              in_=B.Eb[:, :].rearrange("p (s i) -> p s i", i=8)[:, :, 7]),
                         reads=[B.Eb], writes=[B.dec])
                else:
                    P.op("act", lambda e, h=h: e.copy(
                        out=B.dec[:, h * nt:(h + 1) * nt],
                        in_=B.Eb[:, :].rearrange("p (j t) -> p j t", t=128)[:, :, 127]), reads=[B.Eb], writes=[B.dec])
                yield
        W, u = w_use(3)
        for g in range(4):
            bk = fm_mm(W, g * 128, xnT, TB)
            P.op("dve", lambda e, g=g, bk=bk: e.tensor_copy(out=B.qsT[:, g, :], in_=bk[:, 0:TB]), reads=[bk],
                 writes=[B.qsT])
            pfree(bk)
            yield
        w_release(u)
        W4, u = w_use(4)
        for j, n in enumerate(tiles):
            bk = tm_mm(W4, 512, xnT, j)
            P.op("act", lambda e, j=j, bk=bk: e.copy(out=B.v_tok[:, j, :], in_=bk[:, :]), reads=[bk], writes=[B.v_tok])
            pfree(bk)
            yield
        w_release(u)
        W5, u = w_use(5)
        bk = fm_mm(W5, 0, xnT, TB)
        s0_ = tiles[0] % NKV
        P.op("act", lambda e, bk=bk: e.copy(out=kT_all[:, s0_ * 128:s0_ * 128 + TB], in_=bk[:, 0:TB]), reads=[bk],
             writes=[kT_all])
        pfree(bk)
        yield
        for j, n in enumerate(tiles):
            bk = tm_mm(W5, 256, xnT, j)
            ns = n % NKV
            if sample or n == NT - 1:
                P.op("act", lambda e, bk=bk: e.copy(out=B.kvout[:, :], in_=bk[:, 0:256]), reads=[bk], writes=[B.kvout])
                P.op("dve", lambda e, ns=ns: e.tensor_copy(out=vS_all[:, ns, :], in_=B.kvout[:, 128:256]),
                     reads=[B.kvout], writes=[vS_all])
                if sample:
                    final_toks.append(P.dma(ks_d[:, 120:128, :], B.kvout[:, 0:128], reads=[B.kvout], sem="kvo"))
                    final_toks.append(P.dma(vs_d[:, 120:128, :], B.kvout[:, 128:256], reads=[B.kvout], sem="kvo"))
                else:
                    final_toks.append(P.dma(kp_d, B.kvout[:, 0:128], reads=[B.kvout], sem="kvo"))
                    final_toks.append(P.dma(vp_d, B.kvout[:, 128:256], reads=[B.kvout], sem="kvo"))
            else:
                P.op("dve", lambda e, ns=ns, bk=bk: e.tensor_copy(out=vS_all[:, ns, :], in_=bk[:, 128:256]), reads=[bk],
                     writes=[vS_all])
            pfree(bk)
            yield
        w_release(u)

    def hg_front(B, j, cmask):
        cs = slice(j * 128, (j + 1) * 128)
        bkA = psum()
        P.op("pe", [lambda e, h=h: e.matmul(bkA[:, h * 128:(h + 1) * 128], lhsT=B.ktT[:, h, cs], rhs=B.qtT[:, h, cs],
                                            start=True, stop=True) for h in range(4)],
             reads=[B.ktT, B.qtT], writes=[bkA])
        P.op("dve", lambda e: e.tensor_tensor(out=B.ATm[:, :], in0=bkA[:, :], in1=cmask[:, :], op=ALU.mult),
             reads=[bkA, cmask], writes=[B.ATm])
        pfree(bkA)
        bkT = psum()
        bkTb = bkT.ap.bitcast(BF16)
        P.op("pe", [lambda e, h=h: e.transpose(bkTb[:, h * 128:(h + 1) * 128], B.ktT[:, h, cs], C.ident[:, :])
                    for h in range(4)], reads=[B.ktT, C.ident], writes=[bkT])
        P.op("act", lambda e: e.copy(out=B.kt_tok[:, :], in_=bkTb[:, 0:512]), reads=[bkT], writes=[B.kt_tok])
        pfree(bkT)

    def hg_square(B, oaps, obufs):
        ub = list({id(b): b for b in obufs}.values())
        if len(ub) == 1:
            P.op("act", lambda e: e.activation(out=B.sqn[:, :], in_=ub[0][:, :], func=AF.Square), reads=ub,
                 writes=[B.sqn])
        else:
            P.op("act", [lambda e, h=h: e.activation(out=B.sqn[:, h * 128:(h + 1) * 128], in_=oaps[h], func=AF.Square)
                         for h in range(4)], reads=ub, writes=[B.sqn])

    def hg_norm_out(B, j, oaps, obufs):
        cs = slice(j * 128, (j + 1) * 128)
        ub = list({id(b): b for b in obufs}.values())
        bkM = psum()
        P.op("pe", lambda e: e.matmul(bkM[:, :], lhsT=C.onesm[:, :], rhs=B.sqn[:, :], start=True, stop=True),
             reads=[C.onesm, B.sqn], writes=[bkM])
        P.op("act", lambda e: e.activation(out=B.lnv[:, :], in_=bkM[:, :], func=AF.Ln, bias=C.epsc[:, 0:1]),
             reads=[bkM, C.epsc], writes=[B.lnv])
        pfree(bkM)
        P.op("act", lambda e: e.activation(out=B.rstdn[:, :], in_=B.lnv[:, :], func=AF.Exp, scale=-0.5),
             reads=[B.lnv], writes=[B.rstdn])
        if len(ub) == 1:
            P.op("dve", lambda e: e.tensor_tensor(out=B.t1[:, :], in0=ub[0][:, :], in1=B.rstdn[:, :], op=ALU.mult),
                 reads=ub + [B.rstdn], writes=[B.t1])
        else:
            P.op("dve", [lambda e, h=h: e.tensor_tensor(out=B.t1[:, h * 128:(h + 1) * 128], in0=oaps[h],
                                                        in1=B.rstdn[:, h * 128:(h + 1) * 128], op=ALU.mult)
                         for h in range(4)], reads=ub + [B.rstdn], writes=[B.t1])
        P.op("dve", lambda e: e.tensor_tensor(out=omT[:, 0:4, cs], in0=B.t1[:, :].rearrange("p (a b) -> p a b", a=4),
                                              in1=B.sgT[:, 0:4, cs], op=ALU.mult), reads=[B.t1, B.sgT], writes=[omT])
        for b in ub:
            pfree(b)

    def hg_state(B, j):
        nt = B.TB // 128
        cs = slice(j * 128, (j + 1) * 128)
        bkO = psum()
        fns = []
        for h in range(4):
            hs = slice(h * 128, (h + 1) * 128)
            fns.append(lambda e, h=h, hs=hs: e.matmul(bkO[:, hs], lhsT=B.v_tok[:, j, hs], rhs=B.ATm[:, hs], start=True,
                                                      stop=False))
            fns.append(lambda e, h=h, hs=hs: e.matmul(bkO[:, hs], lhsT=S_bf[:, hs], rhs=B.qtT[:, h, cs], start=False,
                                                      stop=True))
        P.op("pe", fns, reads=[B.v_tok, B.ATm, S_bf, B.qtT], writes=[bkO])
        bkP = psum()
        P.op("pe", [lambda e, hs=slice(h * 128, (h + 1) * 128): e.matmul(bkP[:, hs], lhsT=B.kt_tok[:, hs],
                                                                         rhs=B.v_tok[:, j, hs], start=True, stop=True)
                    for h in range(4)], reads=[B.kt_tok, B.v_tok], writes=[bkP])
        for h in range(4):
            hs = slice(h * 128, (h + 1) * 128)
            dc = B.dec[:, h * nt + j:h * nt + j + 1]
            P.op("act", lambda e, h=h, hs=hs, dc=dc: e.activation(out=B.Pd[h][:, :], in_=bkP[:, hs], func=AF.Identity,
                                                                  scale=dc), reads=[bkP, B.dec], writes=[B.Pd[h]])
            P.op("dve", lambda e, h=h, dc=dc: e.scalar_tensor_tensor(out=S[h][:, :], in0=S[h][:, :], scalar=dc,
                                                                     in1=B.Pd[h][:, :], op0=ALU.mult, op1=ALU.add),
                 reads=[B.Pd[h], B.dec], writes=[S[h]])
        pfree(bkP)
        P.op("act", lambda e: e.copy(out=S_bf[:, :], in_=Sall[:, :]), reads=S, writes=[S_bf])
        hg_square(B, None, [bkO])
        return bkO

    def swa_finish(B, j, bkO, bkD, sample=False):
        cs = slice(j * 128, (j + 1) * 128)
        if sample:
            dv = B.dsum[:, :].rearrange("p (s g i) -> p s g i", s=16, g=4)
            bv = bkD[:, :].rearrange("p (s g i) -> p s g i", s=16, g=4)
            P.op("dve", [lambda e, g=g: e.tensor_scalar(out=dv[:, :, g, :], in0=bv[:, :, g, :],
                                                        scalar1=C.esinkT[:, g:g + 1], scalar2=None, op0=ALU.add)
                         for g in range(4)], reads=[bkD, C.esinkT], writes=[B.dsum])
        else:
            P.op("dve", [lambda e, g=g: e.tensor_scalar(out=B.dsum[:, g * 128:(g + 1) * 128],
                                                        in0=bkD[:, g * 128:(g + 1) * 128], scalar1=C.esinkT[:, g:g + 1],
                                                        scalar2=None, op0=ALU.add) for g in range(4)],
                 reads=[bkD, C.esinkT], writes=[B.dsum])
        pfree(bkD)
        P.op("dve", lambda e: e.reciprocal(out=B.rden[:, :], in_=B.dsum[:, :]), reads=[B.dsum], writes=[B.rden])
        if sample:
            P.op("dve", lambda e: e.tensor_tensor(
                out=omT[:, 4:8, cs].rearrange("p g (s i) -> p g s i", i=8),
                in0=bkO[:, :].rearrange("p (s g i) -> p g s i", s=16, g=4),
                in1=B.rden[:, :].rearrange("p (s g i) -> p g s i", s=16, g=4), op=ALU.mult),
                 reads=[bkO, B.rden], writes=[omT])
        else:
            P.op("dve", lambda e: e.tensor_tensor(out=omT[:, 4:8, cs], in0=bkO[:, :].rearrange("p (a b) -> p a b", a=4),
                                                  in1=B.rden[:, :].rearrange("p (a b) -> p a b", a=4), op=ALU.mult),
                 reads=[bkO, B.rden], writes=[omT])
        pfree(bkO)

    ptc = {"i": 0}

    def swa_st(B, rows, kcols, qrhs, mask):
        bkS = psum()
        P.op("pe", lambda e: e.matmul(bkS[:, :], lhsT=kT_all[rows, kcols], rhs=qrhs, start=True, stop=True),
             reads=[kT_all, B.qsT], writes=[bkS])
        pt = B.PT[ptc["i"] % 4]
        ptc["i"] += 1
        P.op("act", lambda e: e.activation(out=pt[:, :], in_=bkS[:, :], func=AF.Exp, scale=0.125), reads=[bkS],
             writes=[pt])
        pfree(bkS)
        P.op("dve", lambda e: e.tensor_tensor(out=pt[:, :], in0=pt[:, :], in1=mask[:, :], op=ALU.mult), reads=[mask],
             writes=[pt])
        return pt

    def p3_tile_steps(B, j, n):
        cs = slice(j * 128, (j + 1) * 128)
        blocks = []
        for kv in range(2):
            rows = slice(64 * kv, 64 * kv + 64)
            bl = ([(n - 1, C.swm_prev)] if n > 0 else []) + [(n, C.swm_cur)]
            for bi, (kt, mask) in enumerate(bl):
                blocks.append((rows, kt % NKV, mask, bi == 0, bi == len(bl) - 1))

        def st(bi):
            rows, ks, mask, _, _ = blocks[bi]
            return swa_st(B, rows, slice(ks * 128, (ks + 1) * 128), B.qsT[rows, 0:4, cs], mask)

        def pv(bi, pt):
            rows, ks, mask, first, last = blocks[bi]
            P.op("pe", [lambda e: e.matmul(swO[rows, :], lhsT=vS_all[:, ks, rows], rhs=pt[:, :], start=first, stop=last),
                        lambda e: e.matmul(swD[rows, :], lhsT=C.ones64[:, :], rhs=pt[:, :], start=first, stop=last)],
                 reads=[vS_all, pt, C.ones64], writes=[swO, swD])

        hg_front(B, j, C.cmask)
        yield
        swO = psum()
        swD = psum()
        pts = {0: st(0)}
        yield
        hgO = hg_state(B, j)
        yield
        pv(0, pts[0])
        pts[1] = st(1)
        yield
        hg_norm_out(B, j, [hgO[:, h * 128:(h + 1) * 128] for h in range(4)], [hgO] * 4)
        yield
        for bi in range(1, len(blocks)):
            pv(bi, pts[bi])
            if bi + 1 < len(blocks):
                pts[bi + 1] = st(bi + 1)
            yield
        swa_finish(B, j, swO, swD)
        yield

    def head3_steps(B, tiles):
        for j, n in enumerate(tiles):
            for _ in p3_tile_steps(B, j, n):
                yield
        if tiles[-1] == NT - 1:
            for h in range(4):
                final_toks.append(P.dma(sp_d[h], S[h].ap, reads=[S[h]], sem="spo"))

    def p3_sample(B):
        TS = NT % NKV
        pool1(lambda e: e.memset(B.smask_s[:, :], 1.0), [B.smask_s])
        pool1(lambda e: e.memset(B.smask_s[:, :].rearrange("p (c t) -> p c t", t=8)[:, :, 0:1], 0.0), [B.smask_s])
        for mk, pa, pb in ((B.cmask_s, [[0, 4], [8, 16], [1, 8]], [[0, 4], [-8, 16], [0, 8]]),
                           (B.swm_scur, [[8, 16], [0, 4], [1, 8]], [[-8, 16], [0, 4], [0, 8]])):
            pool1(lambda e, mk=mk: e.memset(mk[:, :], 1.0), [mk])
            pool1(lambda e, mk=mk, pa=pa: e.affine_select(out=mk[:, :], in_=mk[:, :], pattern=pa, base=0,
                                                          channel_multiplier=-1, compare_op=ALU.is_ge, fill=0.0), [mk])
            pool1(lambda e, mk=mk, pb=pb: e.affine_select(out=mk[:, :], in_=mk[:, :], pattern=pb, base=0,
                                                          channel_multiplier=1, compare_op=ALU.is_ge, fill=0.0), [mk])
        pool1(lambda e: e.memset(B.swm_c[:, :], 1.0), [B.swm_c])
        pool1(lambda e: e.affine_select(out=B.swm_c[:, :], in_=B.swm_c[:, :], pattern=[[0, 64], [-1, 8]], base=-1,
                                        channel_multiplier=1, compare_op=ALU.is_ge, fill=0.0), [B.swm_c])

    def p3_sample_body(B):
        TS = NT % NKV
        for c4 in range(4):
            k32 = B.ck32[c4 % 2]
            v32 = B.cv32[c4 % 2]
            P.dma(k32.ap, ck_d[c4 * 4:(c4 + 1) * 4].rearrange("s k c -> k s c"), writes=[k32], sem="ck%d" % (c4 % 2))
            P.dma(v32.ap, cv_d[c4 * 4:(c4 + 1) * 4].rearrange("s k c -> k s c"), writes=[v32], sem="cv%d" % (c4 % 2))
            P.op("dve", lambda e, c4=c4, k32=k32: e.tensor_copy(out=B.ckb[:, c4 * 4:(c4 + 1) * 4, :], in_=k32[:, :, :]),
                 reads=[k32], writes=[B.ckb])
            P.op("pool", lambda e, c4=c4, v32=v32: e.tensor_copy(out=B.cvb[:, c4 * 4:(c4 + 1) * 4, :], in_=v32[:, :, :]),
                 reads=[v32], writes=[B.cvb])
        for half in range(2):
            bk = psum()
            bkb = bk.ap.bitcast(BF16)
            P.op("pe", [lambda e, i=i: e.transpose(bkb[:, i * 128:(i + 1) * 128], B.ckb[:, half * 8 + i, :], C.ident[:, :])
                        for i in range(8)], reads=[B.ckb, C.ident], writes=[bk])
            P.op("act", lambda e, bkb=bkb: e.copy(out=B.kcT[:, half * 8:(half + 1) * 8, :],
                                                  in_=bkb.rearrange("p (a b) -> p a b", a=8)), reads=[bk],
                 writes=[B.kcT])
            pfree(bk)
        final_toks.append(P.dma(ks_d[:, 0:120, :], ck_d[:, 8:128, :], sem="kvc"))
        final_toks.append(P.dma(vs_d[:, 0:120, :], cv_d[:, 8:128, :], sem="kvc"))

        hg_front(B, 0, B.cmask_s)
        bkO = [psum() for _ in range(4)]
        P.op("pe", [lambda e, h=h: e.matmul(bkO[h][:, 0:128], lhsT=B.v_tok[:, 0, h * 128:(h + 1) * 128],
                                            rhs=B.ATm[:, h * 128:(h + 1) * 128], start=True, stop=False)
                    for h in range(4)], reads=[B.v_tok, B.ATm], writes=bkO)
        def stage_a(seq):
            s0 = B.S0[seq % 3]
            s0b = B.S0b[seq % 2]
            ktm = B.ktm[seq % 2]
            P.dma(s0.ap.rearrange("p (h v) -> p h v", h=4), st_d[seq].rearrange("h d v -> d h v"), writes=[s0],
                  sem="s0%d" % (seq % 3))
            P.op("pool", lambda e: e.tensor_copy(out=s0b[:, :], in_=s0[:, :]), reads=[s0], writes=[s0b])
            P.op("pe", [lambda e, h=h: e.matmul(bkO[h][:, seq * 8:(seq + 1) * 8], lhsT=s0b[:, h * 128:(h + 1) * 128],
                                                rhs=B.qtT[:, h, seq * 8:(seq + 1) * 8], start=False, stop=(seq == 15))
                        for h in range(4)], reads=[s0b, B.qtT], writes=bkO)
            P.op("dve", lambda e: e.tensor_scalar(out=ktm[:, :], in0=B.kt_tok[:, :], scalar1=C.seqmask[:, seq:seq + 1],
                                                  scalar2=None, op0=ALU.mult), reads=[B.kt_tok, C.seqmask], writes=[ktm])
            bkP = psum()
            P.op("pe", [lambda e, hs=slice(h * 128, (h + 1) * 128): e.matmul(bkP[:, hs], lhsT=ktm[:, hs],
                                                                             rhs=B.v_tok[:, 0, hs], start=True, stop=True)
                        for h in range(4)], reads=[ktm, B.v_tok], writes=[bkP])
            return bkP

        def stage_b(seq, bkP):
            s0 = B.S0[seq % 3]
            sn = B.Sn[seq % 2]
            for h in range(4):
                hs = slice(h * 128, (h + 1) * 128)
                dc = B.dec[:, h * 16 + seq:h * 16 + seq + 1]
                P.op("act", lambda e, h=h, hs=hs, dc=dc: e.activation(out=B.Pd[h][:, :], in_=bkP[:, hs], func=AF.Identity,
                                                                      scale=dc), reads=[bkP, B.dec], writes=[B.Pd[h]])
                P.op("dve", lambda e, h=h, hs=hs, dc=dc: e.scalar_tensor_tensor(out=sn[h][:, :], in0=s0[:, hs], scalar=dc,
                                                                                in1=B.Pd[h][:, :], op0=ALU.mult,
                                                                                op1=ALU.add),
                     reads=[s0, B.Pd[h], B.dec], writes=[sn[h]])
            pfree(bkP)
            final_toks.append(P.dma(ss_d[seq].rearrange("h d v -> d h v"), B.SnT[seq % 2], reads=sn,
                                    sem="sn%d" % (seq % 2)))

        pend = {0: stage_a(0)}
        for seq in range(16):
            if seq + 1 < 16:
                pend[seq + 1] = stage_a(seq + 1)
            stage_b(seq, pend.pop(seq))
            yield
        oaps = [bkO[h][:, 0:128] for h in range(4)]
        hg_square(B, oaps, bkO)
        hg_norm_out(B, 0, oaps, bkO)

        bO = psum()
        bD = psum()
        for kv in range(2):
            rows = slice(64 * kv, 64 * kv + 64)
            ptn = swa_st(B, rows, slice(TS * 128, (TS + 1) * 128),
                         B.qsT[rows, 0:4, 0:128].rearrange("p g (s i) -> p s g i", i=8), B.swm_scur)
            bkC = psum()
            P.op("pe", [lambda e, s=s: e.matmul(bkC[:, s * 32:(s + 1) * 32], lhsT=B.kcT[rows, s, :],
                                                rhs=B.qsT[rows, 0:4, s * 8:(s + 1) * 8], start=True, stop=True)
                        for s in range(16)], reads=[B.kcT, B.qsT], writes=[bkC])
            pc = B.PT[ptc["i"] % 4]
            ptc["i"] += 1
            P.op("act", lambda e, pc=pc, bkC=bkC: e.activation(out=pc[:, :], in_=bkC[:, :], func=AF.Exp, scale=0.125),
                 reads=[bkC], writes=[pc])
            pfree(bkC)
            P.op("pool", lambda e, pc=pc: e.tensor_tensor(out=pc[:, :], in0=pc[:, :], in1=B.swm_c[:, :], op=ALU.mult),
                 reads=[B.swm_c], writes=[pc])
            fns = [lambda e: e.matmul(bO[rows, :], lhsT=vS_all[:, TS, rows], rhs=ptn[:, :], start=True, stop=False),
                   lambda e: e.matmul(bD[rows, :], lhsT=C.ones64[:, :], rhs=ptn[:, :], start=True, stop=False)]
            for s in range(16):
                fns.append(lambda e, s=s: e.matmul(bO[rows, s * 32:(s + 1) * 32], lhsT=B.cvb[:, s, rows],
                                                   rhs=pc[:, s * 32:(s + 1) * 32], start=False, stop=(s == 15)))
                fns.append(lambda e, s=s: e.matmul(bD[rows, s * 32:(s + 1) * 32], lhsT=C.ones64[:, :],
                                                   rhs=pc[:, s * 32:(s + 1) * 32], start=False, stop=(s == 15)))
            P.op("pe", fns, reads=[vS_all, ptn, pc, B.cvb, C.ones64], writes=[bO, bD])
        swa_finish(B, 0, bO, bD, sample=True)
        yield

    def p45(B, slots):
        nt = B.TB // 128
        W6, u6 = w_use(6)
        W7, u7 = w_use(7)
        pend = []
        for j in range(nt):
            for nh, W in enumerate((W6, W7)):
                bk = psum()
                P.op("pe", [lambda e, k=k, W=W, bk=bk: e.matmul(bk[:, :], lhsT=omT[:, k, j * 128:(j + 1) * 128],
                                                                rhs=W[:, k, :], start=(k == 0), stop=(k == 7))
                            for k in range(8)], reads=[omT, W], writes=[bk])
                xs_ = slots[j]
                P.op("dve", lambda e, nh=nh, bk=bk, xs_=xs_: e.tensor_tensor(
                    out=xs_[:, nh * 512:(nh + 1) * 512], in0=bk[:, :], in1=xs_[:, nh * 512:(nh + 1) * 512], op=ALU.add),
                     reads=[bk], writes=[xs_])
                pfree(bk)
            xb = norm_pre(slots[j])
            pend.append((xb, j))
            if len(pend) > 2:
                xb0, j0 = pend.pop(0)
                norm_post(xb0, hnT, j0)
        w_release(u6)
        w_release(u7)
        return pend

    def mlp_steps(B, slots, xb_last):
        TB = B.TB
        nt = TB // 128
        for xb0, j0 in xb_last:
            norm_post(xb0, hnT, j0)
            yield
        for q in range(4):
            for rr in range(2):
                W, u = w_use(8 + 2 * q + rr)
                for qq in range(4):
                    fl = rr * 4 + qq
                    bk = fm_mm(W, qq * 128, hnT, TB)
                    rl = B.rl[fl % 2]
                    P.op("act", lambda e, bk=bk, rl=rl: e.activation(out=rl[:, 0:TB], in_=bk[:, 0:TB], func=AF.Relu),
                         reads=[bk], writes=[rl])
                    pfree(bk)
                    P.op("act", lambda e, fl=fl, rl=rl: e.activation(out=aT[:, fl, 0:TB], in_=rl[:, 0:TB],
                                                                     func=AF.Square), reads=[rl], writes=[aT])
                    yield
                w_release(u)
            for nh in range(2):
                W, u = w_use(16 + nh * 4 + q)
                for j in range(nt):
                    bk = psum()
                    P.op("pe", [lambda e, f=f, W=W, j=j, bk=bk: e.matmul(
                        bk[:, :], lhsT=aT[:, f, j * 128:(j + 1) * 128], rhs=W[:, f, :], start=(f == 0), stop=(f == 7))
                        for f in range(8)], reads=[aT, W], writes=[bk])
                    xs_ = slots[j]
                    P.op("dve", lambda e, nh=nh, bk=bk, xs_=xs_: e.tensor_tensor(
                        out=xs_[:, nh * 512:(nh + 1) * 512], in0=bk[:, :], in1=xs_[:, nh * 512:(nh + 1) * 512],
                        op=ALU.add), reads=[bk], writes=[xs_])
                    pfree(bk)
                    yield
                w_release(u)

    def xres_index(b):
        for i_, bb in enumerate(xres):
            if bb is b:
                return i_
        raise AssertionError("not an xres slot")

    def p8(slots, ydst):
        for j in range(len(slots)):
            rs = norm_stats(slots[j])
            xs_ = slots[j]
            P.op("dve", lambda e, xs_=xs_, rs=rs: e.scalar_tensor_tensor(
                out=xs_[:, :], in0=xs_[:, :], scalar=rs[:, 0:1], in1=C.gfin[:, :], op0=ALU.mult, op1=ALU.mult),
                 reads=[rs, C.gfin], writes=[xs_])
            final_toks.append(P.dma(ydst[j * 128:(j + 1) * 128, :], xs_.ap, reads=[xs_], sem="yx%d" % xres_index(xs_)))

    def drain(gen):
        for _ in gen:
            pass

    def interleave(genA, genB):
        a_ok = b_ok = True
        while a_ok or b_ok:
            if b_ok:
                try:
                    next(genB)
                except StopIteration:
                    b_ok = False
            if a_ok:
                try:
                    next(genA)
                except StopIteration:
                    a_ok = False

    def chain(*gens):
        for g_ in gens:
            for _ in g_:
                yield

    def main_schedule():
        xload = {}

        def load_x(n):
            s = xres[n % NXS]
            P.dma(s.ap, x_d[n * 128:(n + 1) * 128, :], writes=[s], sem="x%d" % (n % NXS))
            xload[n] = s

        def gtiles(g):
            return list(range(g * G, (g + 1) * G))

        Bs = carve(128, True)
        REG["B"] = Bs
        xsmp = xres[7]
        P.dma(xsmp.ap, xs_d, writes=[xsmp], sem="x7")
        w_prefetch()
        p3_sample(Bs)
        drain(head12_steps(Bs, [NT], [xsmp], True))
        chk("p2s")
        drain(p3_sample_body(Bs))
        chk("p3s")
        P.wait_engines(["act", "dve", "pool", "pe", "sp"], ["act", "dve", "pool", "pe"])
        wait_final()
        Bp = carve(512, False)
        REG["B"] = Bp
        for n in gtiles(0):
            load_x(n)

        def sample_tail():
            xb_last = p45(Bs, [xsmp])
            yield
            for _ in mlp_steps(Bs, [xsmp], xb_last):
                yield
            p8([xsmp], ys_d)
            yield

        interleave(sample_tail(),
                   chain(head12_steps(Bp, gtiles(0), [xload[n] for n in gtiles(0)], False), head3_steps(Bp, gtiles(0))))
        chk("h0")
        for g in range(NG):
            slots = [xload[n] for n in gtiles(g)]
            if g + 1 < NG:
                for n in gtiles(g + 1):
                    load_x(n)
            xb_last = p45(Bp, slots)
            genA = mlp_steps(Bp, slots, xb_last)
            if g + 1 < NG:
                nslots = [xload[n] for n in gtiles(g + 1)]
                genB = chain(head12_steps(Bp, gtiles(g + 1), nslots, False), head3_steps(Bp, gtiles(g + 1)))
                interleave(genA, genB)
            else:
                drain(genA)
            p8(slots, y_d[g * G * 128:(g + 1) * G * 128, :])
            chk("g%d" % g)

    try:
        main_schedule()
    except _Stop:
        pass
    REG.update(dict(xnT=xnT, hnT=hnT, omT=omT, kT_all=kT_all, vS_all=vS_all, S_bf=S_bf, xres0=xres[0], xres1=xres[1],
                    lb=C.lb, seqmask=C.seqmask))
    for dn in dumps:
        b = REG[dn] if dn in REG else getattr(REG["B"], dn)
        shp = [int(v) for v in b.ap.shape]
        dd = nc.dram_tensor("dbg_" + dn, shp, b.ap.dtype, kind="ExternalOutput").ap()
        final_toks.append(P.dma(dd, b.ap, reads=[b], sem="dbg"))

    wait_final()
    P.wait_engines(["sp"], ["act", "dve", "pool", "pe"])
    return nc, WORDER


def _chunk_k(a):
    return np.ascontiguousarray(a.reshape(8, 128, 512).transpose(1, 0, 2)).reshape(128, 4096)


def _build_wall(w_in, w_out, w_up, w_down):
    zq, zf, zi, zg = w_in[:, 0:512], w_in[:, 512:1024], w_in[:, 1024:1536], w_in[:, 1536:2048]
    sq, sk, sv = w_in[:, 2048:2560], w_in[:, 2560:2688], w_in[:, 2688:2816]
    sqP = np.concatenate([np.concatenate([sq[:, g * 64:(g + 1) * 64], sq[:, (4 + g) * 64:(5 + g) * 64]], axis=1)
                          for g in range(4)], axis=1)
    c5 = np.concatenate([sk, sv, np.zeros((1024, 256), np.float32)], axis=1)
    chunks = [_chunk_k(a) for a in (zq, zg, zf, sqP, zi, c5)]
    perm = list(range(512)) + [512 + (kv * 4 + g) * 64 + d for g in range(4) for kv in range(2) for d in range(64)]
    wo = w_out[perm]
    chunks += [_chunk_k(wo[:, nh * 512:(nh + 1) * 512]) for nh in range(2)]
    chunks += [_chunk_k(w_up[:, r * 512:(r + 1) * 512]) for r in range(8)]
    for nh in range(2):
        for r4 in range(4):
            chunks.append(_chunk_k(w_down[r4 * 1024:(r4 + 1) * 1024, nh * 512:(nh + 1) * 512]))
    return np.ascontiguousarray(np.stack(chunks, axis=0), dtype=np.float32)


def _make_in_maps(x_prompt, x_sample, state_hgrn, cache_swa_k, cache_swa_v, ln_mix, w_in, lb_logits, hg_norm, sinks,
           w_out, ln_mlp, w_up, w_down, ln_final):
    f = lambda a: np.ascontiguousarray(np.asarray(a), dtype=np.float32)
    x_prompt, x_sample, state_hgrn = f(x_prompt), f(x_sample), f(state_hgrn)
    cache_swa_k, cache_swa_v = f(cache_swa_k), f(cache_swa_v)
    wall = _build_wall(f(w_in)[0], f(w_out)[0], f(w_up)[0], f(w_down)[0])
    lnmix = np.ascontiguousarray(f(ln_mix)[0].reshape(8, 128).T)
    lnmlp = np.ascontiguousarray(f(ln_mlp)[0].reshape(8, 128).T)
    lbl = np.ascontiguousarray(f(lb_logits).reshape(2, 4, 128).transpose(2, 0, 1).reshape(128, 8))
    hgn = np.ascontiguousarray(f(hg_norm)[0].reshape(128, 1))
    sk_ = f(sinks)[0]
    sinkT = np.ascontiguousarray(np.stack([sk_[(p // 64) * 4:(p // 64) * 4 + 4] for p in range(128)], axis=0))
    gfin = np.ascontiguousarray(np.broadcast_to(f(ln_final)[None, :], (128, D)))
    in_maps = []
    for c in range(NCORES):
        in_maps.append({
            "x": x_prompt[c],
            "xs": np.ascontiguousarray(x_sample[16 * c:16 * (c + 1)].reshape(128, D)),
            "st": state_hgrn[0, 16 * c:16 * (c + 1)],
            "ck": np.ascontiguousarray(cache_swa_k[0, 16 * c:16 * (c + 1)].reshape(16, 128, 128)),
            "cv": np.ascontiguousarray(cache_swa_v[0, 16 * c:16 * (c + 1)].reshape(16, 128, 128)),
            "wall": wall, "lnmix": lnmix, "lnmlp": lnmlp, "lbl": lbl, "hgn": hgn, "sinkT": sinkT, "gfin": gfin,
        })
    return in_maps


def kernel(x_prompt, x_sample, state_hgrn, cache_swa_k, cache_swa_v, ln_mix, w_in, lb_logits, hg_norm, sinks,
           w_out, ln_mlp, w_up, w_down, ln_final):
    in_maps = _make_in_maps(x_prompt, x_sample, state_hgrn, cache_swa_k, cache_swa_v, ln_mix, w_in, lb_logits, hg_norm,
                            sinks, w_out, ln_mlp, w_up, w_down, ln_final)
    _, order = build_program()
    nc, _ = build_program(worder=order)
    res = run_bass_kernel_spmd(nc, in_maps, core_ids=list(range(NCORES)))
    R = res.results
    y_prompt = np.stack([R[c]["y"] for c in range(NCORES)], axis=0).reshape(8, 4096, D)
    y_sample = np.concatenate([R[c]["ys"].reshape(16, 8, D) for c in range(NCORES)], axis=0)
    sp = np.stack([R[c]["sp"] for c in range(NCORES)], axis=0)[None]
    kp = np.stack([R[c]["kp"].reshape(128, 2, 64) for c in range(NCORES)], axis=0)[None]
    vp = np.stack([R[c]["vp"].reshape(128, 2, 64) for c in range(NCORES)], axis=0)[None]
    ss = np.concatenate([R[c]["ss"] for c in range(NCORES)], axis=0)[None]
    ks = np.concatenate([R[c]["ks"].reshape(16, 128, 2, 64) for c in range(NCORES)], axis=0)[None]
    vs = np.concatenate([R[c]["vs"].reshape(16, 128, 2, 64) for c in range(NCORES)], axis=0)[None]
    out = (y_prompt, y_sample, sp, kp, vp, ss, ks, vs)
    return tuple(np.ascontiguousarray(o, dtype=np.float32) for o in out)
```
